# Optimizing a Trainium2 kernel written in Bass

```python
import math
import jax, jax.numpy as jnp
from jax import lax
import numpy as np

D_MODEL = 4096
BATCH = 4
SEQ = 2048
DEPTH = 2

MIX_WIDTH = D_MODEL
CONV_WIDTH = D_MODEL // 4
SB_WIDTH = D_MODEL // 2
SSM_WIDTH = D_MODEL // 4
CONV_TAPS = 31
SB_HEAD_DIM = 128
SB_HEADS = SB_WIDTH // SB_HEAD_DIM
SB_BLOCK = 128
SSM_GROUP = 16
SSM_GROUPS = SSM_WIDTH // SSM_GROUP
SSM_STATE = 64
DT_MIN = 1e-3
DT_MAX = 1e-1
MEM_LEN = 256
XA_HEADS = 4
XA_HEAD_DIM = D_MODEL // XA_HEADS
EPS = 1e-6

IN_SIZES = (CONV_WIDTH, CONV_WIDTH, CONV_WIDTH,
            SB_WIDTH, SB_WIDTH, SB_WIDTH, SB_WIDTH,
            SSM_WIDTH, SSM_WIDTH)
IN_WIDTH = 3 * CONV_WIDTH + 4 * SB_WIDTH + 2 * SSM_WIDTH

kernel_name = "hybrid_conv_stickbreak_s5_xattn"


def rmsnorm(x, g):
    xf = x.astype(jnp.float32)
    y = xf * lax.rsqrt(jnp.mean(xf * xf, axis=-1, keepdims=True) + EPS)
    return (y * g.astype(jnp.float32)).astype(x.dtype)


def layernorm(x, g, b):
    xf = x.astype(jnp.float32)
    mu = jnp.mean(xf, axis=-1, keepdims=True)
    xc = xf - mu
    y = xc * lax.rsqrt(jnp.mean(xc * xc, axis=-1, keepdims=True) + EPS)
    return (y * g.astype(jnp.float32) + b.astype(jnp.float32)).astype(x.dtype)


def conformer_conv(a_val, a_glu, conv_w, conv_b, ln_g, ln_b):
    u = a_val * jax.nn.sigmoid(a_glu)
    y = lax.conv_general_dilated(
        u, conv_w[:, None, :].astype(u.dtype), window_strides=(1,),
        padding=[(CONV_TAPS - 1, 0)],
        dimension_numbers=('NWC', 'WIO', 'NWC'),
        feature_group_count=CONV_WIDTH) + conv_b.astype(u.dtype)
    y = layernorm(y, ln_g, ln_b)
    return jax.nn.silu(y)


def stick_breaking_attention(q, k, v):
    bsz, seq, heads, hd = q.shape
    nblk = seq // SB_BLOCK
    scale = 1.0 / math.sqrt(hd)
    qb = q.reshape(bsz, nblk, SB_BLOCK, heads, hd).transpose(1, 0, 3, 2, 4)
    key_pos = jnp.arange(seq)

    def block(args):
        qi, i = args
        z = jnp.einsum('bhqd,bkhd->bhqk', qi, k,
                       preferred_element_type=jnp.float32) * scale
        q_pos = i * SB_BLOCK + jnp.arange(SB_BLOCK)
        before = key_pos[None, :] < q_pos[:, None]
        log_keep = jnp.where(before, jax.nn.log_sigmoid(-z), 0.0)
        later = lax.cumsum(log_keep, axis=3, reverse=True) - log_keep
        w = jnp.where(before, jnp.exp(jax.nn.log_sigmoid(z) + later), 0.0)
        return jnp.einsum('bhqk,bkhd->bqhd', w.astype(v.dtype), v)

    out = lax.map(block, (qb, jnp.arange(nblk)))
    return out.transpose(1, 0, 2, 3, 4).reshape(bsz, seq, heads * hd)


def s5_ssm(u, lam_re, lam_im, log_dt, b_re, b_im, c_re, c_im, d_skip, glu_w, glu_b):
    f32 = jnp.float32
    bsz, seq, _ = u.shape
    ug = u.astype(f32).reshape(bsz, seq, SSM_GROUPS, SSM_GROUP)
    lam = lax.complex(lam_re.astype(f32), lam_im.astype(f32))
    dt = jnp.exp(log_dt.astype(f32))[:, None]
    lam_bar = jnp.exp(lam * dt)
    b = lax.complex(b_re.astype(f32), b_im.astype(f32))
    b_bar = ((lam_bar - 1.0) / lam)[..., None] * b
    bu = jnp.einsum('gpc,bsgc->bsgp', b_bar, ug.astype(jnp.complex64))
    a = jnp.broadcast_to(lam_bar, bu.shape)

    def combine(e1, e2):
        a1, b1 = e1
        a2, b2 = e2
        return a1 * a2, a2 * b1 + b2

    _, h = lax.associative_scan(combine, (a, bu), axis=1)
    c = lax.complex(c_re.astype(f32), c_im.astype(f32))
    y = jnp.einsum('gcp,bsgp->bsgc', c, h).real + d_skip.astype(f32) * ug
    y = jax.nn.gelu(y.reshape(bsz, seq, SSM_WIDTH))
    y = y * jax.nn.sigmoid(y @ glu_w.astype(f32) + glu_b.astype(f32))
    return y.astype(u.dtype)


def memory_cross_attention(h, m, wq, wk, wv, wo):
    bsz, seq, _ = h.shape
    q = (h @ wq).reshape(bsz, seq, XA_HEADS, XA_HEAD_DIM)
    k = (m @ wk).reshape(bsz, m.shape[1], XA_HEADS, XA_HEAD_DIM)
    v = (m @ wv).reshape(bsz, m.shape[1], XA_HEADS, XA_HEAD_DIM)
    s = jnp.einsum('bqhd,bkhd->bhqk', q, k,
                   preferred_element_type=jnp.float32) / math.sqrt(XA_HEAD_DIM)
    p = jax.nn.softmax(s, axis=-1).astype(v.dtype)
    o = jnp.einsum('bhqk,bkhd->bqhd', p, v).reshape(bsz, seq, D_MODEL)
    return o @ wo


def setup_inputs(seed: int = 0) -> dict:
    key = jax.random.key(seed)
    ks = jax.random.split(key, 32)
    L, D = DEPTH, D_MODEL
    nrm = jax.random.normal
    f32 = jnp.float32

    def gain(k, n):
        return 1.0 + 0.02 * nrm(k, (L, n), f32)

    lam_im_base = math.pi * jnp.arange(SSM_STATE, dtype=f32)
    return {
        "x": nrm(ks[0], (BATCH, SEQ, D), f32),
        "mem": nrm(ks[1], (BATCH, MEM_LEN, D), f32),
        "pre_norm_g": gain(ks[2], D),
        "w_in": nrm(ks[3], (L, D, IN_WIDTH), f32) * D ** -0.5,
        "conv_w": nrm(ks[4], (L, CONV_TAPS, CONV_WIDTH), f32) * CONV_TAPS ** -0.5,
        "conv_b": 0.02 * nrm(ks[5], (L, CONV_WIDTH), f32),
        "conv_ln_g": gain(ks[6], CONV_WIDTH),
        "conv_ln_b": 0.02 * nrm(ks[7], (L, CONV_WIDTH), f32),
        "ssm_lambda_re": -0.5 + 0.01 * nrm(ks[8], (L, SSM_GROUPS, SSM_STATE), f32),
        "ssm_lambda_im": lam_im_base + 0.01 * nrm(ks[9], (L, SSM_GROUPS, SSM_STATE), f32),
        "ssm_log_dt": jax.random.uniform(ks[10], (L, SSM_GROUPS), f32,
                                         math.log(DT_MIN), math.log(DT_MAX)),
        "ssm_b_re": nrm(ks[11], (L, SSM_GROUPS, SSM_STATE, SSM_GROUP), f32) * (2 * SSM_GROUP) ** -0.5,
        "ssm_b_im": nrm(ks[12], (L, SSM_GROUPS, SSM_STATE, SSM_GROUP), f32) * (2 * SSM_GROUP) ** -0.5,
        "ssm_c_re": nrm(ks[13], (L, SSM_GROUPS, SSM_GROUP, SSM_STATE), f32) * (2 * SSM_STATE) ** -0.5,
        "ssm_c_im": nrm(ks[14], (L, SSM_GROUPS, SSM_GROUP, SSM_STATE), f32) * (2 * SSM_STATE) ** -0.5,
        "ssm_d": nrm(ks[15], (L, SSM_GROUPS, SSM_GROUP), f32),
        "ssm_glu_w": nrm(ks[16], (L, SSM_WIDTH, SSM_WIDTH), f32) * SSM_WIDTH ** -0.5,
        "ssm_glu_b": 0.02 * nrm(ks[17], (L, SSM_WIDTH), f32),
        "branch_norm_g": gain(ks[18], MIX_WIDTH),
        "w_out": nrm(ks[19], (L, MIX_WIDTH, D), f32) * MIX_WIDTH ** -0.5,
        "post_norm_g": gain(ks[20], D),
        "xa_pre_g": gain(ks[21], D),
        "xa_mem_g": gain(ks[22], D),
        "xa_wq": nrm(ks[23], (L, D, D), f32) * D ** -0.5,
        "xa_wk": nrm(ks[24], (L, D, D), f32) * D ** -0.5,
        "xa_wv": nrm(ks[25], (L, D, D), f32) * D ** -0.5,
        "xa_wo": nrm(ks[26], (L, D, D), f32) * D ** -0.5,
        "xa_post_g": gain(ks[27], D),
    }


def reference(x, mem, pre_norm_g, w_in, conv_w, conv_b, conv_ln_g, conv_ln_b,
              ssm_lambda_re, ssm_lambda_im, ssm_log_dt, ssm_b_re, ssm_b_im, ssm_c_re, ssm_c_im,
              ssm_d, ssm_glu_w, ssm_glu_b, branch_norm_g, w_out, post_norm_g,
              xa_pre_g, xa_mem_g, xa_wq, xa_wk, xa_wv, xa_wo, xa_post_g):
    bsz, seq, _ = x.shape
    offsets = np.cumsum(IN_SIZES)[:-1].tolist()
    gain_offsets = [CONV_WIDTH, CONV_WIDTH + SB_WIDTH]
    for l in range(DEPTH):
        h = rmsnorm(x, pre_norm_g[l])
        proj = h @ w_in[l]
        a_val, a_glu, a_gate, q, k, v, b_gate, c_in, c_gate = jnp.split(proj, offsets, axis=-1)

        y_a = conformer_conv(a_val, a_glu, conv_w[l], conv_b[l], conv_ln_g[l], conv_ln_b[l])
        y_a = y_a * jax.nn.silu(a_gate)

        hs = (bsz, seq, SB_HEADS, SB_HEAD_DIM)
        y_b = stick_breaking_attention(q.reshape(hs), k.reshape(hs), v.reshape(hs))
        y_b = y_b * jax.nn.silu(b_gate)

        y_c = s5_ssm(c_in, ssm_lambda_re[l], ssm_lambda_im[l], ssm_log_dt[l],
                     ssm_b_re[l], ssm_b_im[l], ssm_c_re[l], ssm_c_im[l], ssm_d[l],
                     ssm_glu_w[l], ssm_glu_b[l])
        y_c = y_c * jax.nn.silu(c_gate)

        g_a, g_b, g_c = jnp.split(branch_norm_g[l], gain_offsets)
        y = jnp.concatenate([rmsnorm(y_a, g_a), rmsnorm(y_b, g_b), rmsnorm(y_c, g_c)], axis=-1)
        x = x + rmsnorm(y @ w_out[l], post_norm_g[l])

        h = rmsnorm(x, xa_pre_g[l])
        m = rmsnorm(mem, xa_mem_g[l])
        o = memory_cross_attention(h, m, xa_wq[l], xa_wk[l], xa_wv[l], xa_wo[l])
        x = x + rmsnorm(o, xa_post_g[l])
    return x
```

```python
import numpy as np
from contextlib import ExitStack
import concourse.bass as bass
import concourse.mybir as mybir
from concourse.bass_utils import run_bass_kernel_spmd

F32 = mybir.dt.float32
BF16 = mybir.dt.bfloat16
AF = mybir.ActivationFunctionType
ALU = mybir.AluOpType
AX = mybir.AxisListType

ENGS = ("pe", "act", "dve", "pool", "sp")
EPOCH = 24000
NDSEM = 6


class Buf:
    __slots__ = ("name", "last_w", "readers")

    def __init__(self, name=""):
        self.name = name
        self.last_w = None
        self.readers = []


class Op:
    __slots__ = ("eng", "fn", "deps", "is_dma", "sig", "tok", "n")

    def __init__(self, eng, fn, is_dma):
        self.eng = eng
        self.fn = fn
        self.is_dma = is_dma
        self.deps = set()
        self.sig = False
        self.tok = None
        self.n = -1


class Prog:
    def __init__(self, nc):
        self.nc = nc
        self.ops = []
        self.stack = ExitStack()
        self.last = {e: None for e in ENGS}
        self.barrier_deps = {e: [] for e in ENGS}
        self.dma_slots = {e: [None] * NDSEM for e in ENGS}
        self.dma_cnt = {e: 0 for e in ENGS}
        self.outstanding_dma = []
        self.dma_slots_nb = {}
        self._n = 0

    def sbuf(self, name, shape, dtype, st=None):
        t = (st or self.stack).enter_context(self.nc.sbuf_tensor(name, list(shape), dtype))
        return t

    def psum(self, name, shape, dtype, st=None):
        t = (st or self.stack).enter_context(self.nc.psum_tensor(name, list(shape), dtype))
        return t

    def _add(self, op, reads, writes):
        op.n = self._n
        self._n += 1
        deps = op.deps
        for r in reads:
            if r.last_w is not None:
                deps.add(r.last_w)
        for w in writes:
            if w.last_w is not None:
                deps.add(w.last_w)
            for rd in w.readers:
                deps.add(rd)
        for r in reads:
            r.readers.append(op)
            if len(r.readers) > 64:
                keep = {}
                dm = []
                for o in r.readers:
                    if o.is_dma:
                        dm.append(o)
                    else:
                        keep[o.eng] = o
                r.readers = dm + list(keep.values())
        for w in writes:
            w.last_w = op
            w.readers = []
        for d in self.barrier_deps[op.eng]:
            deps.add(d)
        self.barrier_deps[op.eng] = []
        deps.discard(op)
        self.ops.append(op)
        self.last[op.eng] = op
        return op

    def op(self, eng, fn, reads=(), writes=()):
        return self._add(Op(eng, fn, False), reads, writes)

    def dma(self, fn, reads=(), writes=(), q="sp", nobar=False):
        op = Op(q, fn, True)
        if nobar:
            qk = q + "_nb"
            if qk not in self.dma_cnt:
                self.dma_cnt[qk] = 0
                self.dma_slots_nb[qk] = [None] * NDSEM
            i = self.dma_cnt[qk]
            self.dma_cnt[qk] = i + 1
            slot = i % NDSEM
            prev = self.dma_slots_nb[qk][slot]
            if prev is not None:
                op.deps.add(prev)
            self.dma_slots_nb[qk][slot] = op
            op.tok = ("d", qk, slot, 16 * (i // NDSEM + 1))
            return self._add(op, reads, writes)
        i = self.dma_cnt[q]
        self.dma_cnt[q] = i + 1
        slot = i % NDSEM
        prev = self.dma_slots[q][slot]
        if prev is not None:
            op.deps.add(prev)
        self.dma_slots[q][slot] = op
        op.tok = ("d", q, slot, 16 * (i // NDSEM + 1))
        self.outstanding_dma.append(op)
        if len(self.outstanding_dma) > 4 * NDSEM * 3:
            self.outstanding_dma = self.outstanding_dma[-(NDSEM * 3):]
        return self._add(op, reads, writes)

    def cc(self, fn, reads=(), writes=()):
        op = Op("pool", fn, True)
        self.n_cc = getattr(self, "n_cc", 0) + 1
        op.tok = ("x", "cc", self.n_cc, 1)
        self.cc_ops = getattr(self, "cc_ops", []) + [op]
        return self._add(op, reads, writes)

    def barrier(self):
        deps = [o for o in self.last.values() if o is not None]
        for q in ENGS:
            for o in self.dma_slots[q]:
                if o is not None:
                    deps.append(o)
        for e in ENGS:
            self.barrier_deps[e] = list(deps)

    def emit(self):
        nc = self.nc
        for op in self.ops:
            for d in op.deps:
                if d.is_dma:
                    continue
                if d.eng == "pe" and op.eng == "pe" and not op.is_dma:
                    continue
                d.sig = True
        cnt = {e: 0 for e in ENGS}
        for op in self.ops:
            if op.is_dma:
                continue
            if op.sig:
                cnt[op.eng] += 1
                c = cnt[op.eng]
                op.tok = ("c", op.eng, (c - 1) // EPOCH, (c - 1) % EPOCH + 1)
        nep = {e: (cnt[e] + EPOCH - 1) // EPOCH for e in ENGS}
        st = self.stack
        csem = {}
        for e in ENGS:
            for k in range(max(nep[e], 0)):
                csem[(e, k)] = st.enter_context(nc.semaphore(f"c_{e}_{k}"))
        dsem = {}
        for q in list(self.dma_cnt.keys()):
            if self.dma_cnt[q]:
                for s in range(NDSEM):
                    dsem[(q, s)] = st.enter_context(nc.semaphore(f"d_{q}_{s}"))
        for k in range(1, getattr(self, "n_cc", 0) + 1):
            dsem[("cc", k)] = st.enter_context(nc.semaphore(f"x_cc_{k}"))
        self.n_sems = len(csem) + len(dsem)

        def semof(tok):
            if tok[0] == "c":
                return csem[(tok[1], tok[2])], tok[3]
            return dsem[(tok[1], tok[2])], tok[3]

        per_eng = {e: [] for e in ENGS}
        for op in self.ops:
            per_eng[op.eng].append(op)

        def run(eng_name, eng):
            waited = {}
            for op in per_eng[eng_name]:
                need = {}
                for d in op.deps:
                    if d.tok is None:
                        continue
                    if (not d.is_dma) and d.eng == "pe" and eng_name == "pe" and not op.is_dma:
                        continue
                    key = d.tok[:3]
                    v = d.tok[3]
                    if d.tok[0] == "c":
                        ek = ("c", d.tok[1])
                        cur = need.get(ek)
                        cand = (d.tok[2], v)
                        if cur is None or cand > cur:
                            need[ek] = cand
                    else:
                        cur = need.get(key)
                        if cur is None or v > cur:
                            need[key] = v
                for k, v in need.items():
                    if k[0] == "c":
                        w = waited.get(k)
                        if w is not None and w >= v:
                            continue
                        waited[k] = v
                        eng.wait_ge(csem[(k[1], v[0])], v[1])
                    else:
                        w = waited.get(k)
                        if w is not None and w >= v:
                            continue
                        waited[k] = v
                        eng.wait_ge(dsem[(k[1], k[2])], v)
                ins = op.fn(eng)
                if op.is_dma and op.tok[0] == "x":
                    s, _ = semof(op.tok)
                    ins.then_inc(s)
                elif op.is_dma:
                    s, _ = semof(op.tok)
                    ins.then_inc(s, 16)
                elif op.sig:
                    s, _ = semof(op.tok)
                    ins.then_inc(s, 1)

        with nc.Block() as block:
            @block.tensor
            def _(e):
                run("pe", e)

            @block.scalar
            def _(e):
                run("act", e)

            @block.vector
            def _(e):
                run("dve", e)

            @block.gpsimd
            def _(e):
                run("pool", e)

            @block.sync
            def _(e):
                run("sp", e)

    def scope(self):
        return Scope(self)

    def finish(self, eng="sp"):
        self.barrier()
        extra = list(getattr(self, "cc_ops", []))
        for sl in self.dma_slots_nb.values():
            extra.extend(o for o in sl if o is not None)
        self.barrier_deps[eng] = self.barrier_deps[eng] + extra
        self.op(eng, lambda e: e.nop() if hasattr(e, "nop") else e.engine_nop())


class Tile:
    __slots__ = ("t", "b")

    def __init__(self, t, name=""):
        self.t = t
        self.b = Buf(name)


class Ring:
    def __init__(self, tiles):
        self.tiles = tiles
        self.i = 0

    def next(self):
        t = self.tiles[self.i % len(self.tiles)]
        self.i += 1
        return t


class Scope:
    _uid = 0

    def __init__(self, P):
        self.P = P
        self.st = ExitStack()

    def __enter__(self):
        self.st.__enter__()
        return self

    def __exit__(self, *a):
        self.P.barrier()
        return self.st.__exit__(*a)

    def _nm(self, name):
        Scope._uid += 1
        return f"{name}_{Scope._uid}"

    def sbuf(self, name, shape, dtype):
        return Tile(self.P.sbuf(self._nm(name), shape, dtype, st=self.st), name)

    def psum(self, name, shape, dtype):
        return Tile(self.P.psum(self._nm(name), shape, dtype, st=self.st), name)

    def ring(self, name, n, shape, dtype, psum=False):
        f = self.psum if psum else self.sbuf
        return Ring([f(f"{name}{i}", shape, dtype) for i in range(n)])


import math
import numpy as np
import ml_dtypes

I32 = mybir.dt.int32
D = 4096
T = 1024
S = 2048
NL = 2
INW = 13312
EPS = 1e-6
NB = 512
XA_NCH = 7
XA_ROWS = XA_NCH * 1024 + 256
XB_NCH = 3
XB_ROWS = XB_NCH * 1024
PAIRS = [[0, 1], [2, 3], [4, 5], [6, 7]]


def xrow(s, ro):
    return (ro // 512) * 1024 + s * 512 + (ro % 512)


_ME = {}


def me_idx(e):
    key = id(e)
    if key not in _ME:
        _ME[key] = e.snap(e.partition_id() % 2)
    return _ME[key]


def gsrc(e, Glist, r, ro, n, cols=slice(None)):
    k, w0 = ro // 512, ro % 512
    return Glist[k][r * 512 + w0:r * 512 + w0 + n, cols]


class Exchange:
    def __init__(self, P, DR, src, srcname, dstname, nch, small_rows=0):
        self.P, self.DR, self.src, self.srcname, self.dstname = P, DR, src, srcname, dstname
        self.n = nch + (1 if small_rows else 0)
        self.rows = [1024 if k < nch else small_rows for k in range(self.n)]
        self.g = [DR.get(f"{dstname}g_{k}", [2 * self.rows[k], T], BF16) for k in range(self.n)]
        self.sel = [DR.get(f"{dstname}_{k}", [self.rows[k], T], BF16) for k in range(self.n)]
        self.issued = set()

    def cc(self, k):
        if k in self.issued:
            return
        self.issued.add(k)
        P, DR = self.P, self.DR
        sl = self.src[k * 1024:k * 1024 + self.rows[k], :]
        g = self.g[k]
        P.cc(lambda e: e.collective_compute("AllGather", ALU.bypass, replica_groups=PAIRS, ins=[sl.opt()], outs=[g.opt()]),
             reads=[DR.chunkbuf(self.srcname, k)], writes=[DR.buf(f"{self.dstname}g_{k}")])

    def finish(self):
        P, DR = self.P, self.DR
        for k in range(self.n):
            self.cc(k)
        for k in range(self.n):
            g, sel = self.g[k], self.sel[k]
            P.dma(lambda e, g=g, sel=sel: e.dma_start(out=sel.rearrange("(r s w) c -> r s w c", r=2, s=1),
                                                      in_=g.rearrange("(r s w) c -> r s w c", r=2, s=2)[:, bass.ds(me_idx(e), 1), :, :]),
                  reads=[DR.buf(f"{self.dstname}g_{k}")], writes=[DR.buf(f"{self.dstname}_{k}")], q="pool", nobar=True)
        return self.sel


SB_SCALE = 1.0 / math.sqrt(128.0)
XA_SCALE = 1.0 / math.sqrt(1024.0)
TWO_PI = 2.0 * math.pi


class Dram:
    def __init__(self, nc, io):
        self.nc = nc
        self.io = io
        self.t = {}
        self.b = {}

    def get(self, name, shape=None, dtype=None):
        if name not in self.t:
            kind = {"in": "ExternalInput", "out": "ExternalOutput"}.get(self.io.get(name), "Internal")
            self.t[name] = self.nc.dram_tensor(name, list(shape), dtype, kind=kind).ap()
            self.b[name] = Buf(name)
        return self.t[name]

    def buf(self, name):
        return self.b[name]

    def chunkbuf(self, name, k):
        key = f"{name}#{k}"
        if key not in self.b:
            self.b[key] = Buf(key)
        return self.b[key]


def mm(ps, lhsT, rhs, start, stop):
    return lambda e: e.matmul(ps, lhsT=lhsT, rhs=rhs, start=start, stop=stop)


def load_consts(P, Sg, DR):
    C = {}
    def ld(name, shape, dtype):
        t = Sg.sbuf(name, shape, dtype)
        src = DR.get(name, shape, dtype)
        P.dma(lambda e: e.dma_start(out=t.t[:], in_=src), writes=[t.b])
        C[name] = t
    ld("c_ident", [128, 128], BF16)
    ld("c_ones_f", [128, 128], F32)
    ld("c_ones_b", [128, 128], BF16)
    ld("c_triU", [128, 128], BF16)
    ld("c_triLE", [128, 128], BF16)
    ld("c_triI", [128, 128], BF16)
    ld("c_ntriI", [128, 128], BF16)
    ld("c_zero_b", [128, 128], BF16)
    ld("c_mask", [128, 4, 512], BF16)
    ld("c_iota", [128, 128], F32)
    ld("c_iotap", [128, 1], F32)
    return C


def rms_rstd(P, ss, rstd, n, reads_extra=()):
    P.op("act", lambda e: e.activation(out=rstd.t[:], in_=ss.t[:], func=AF.Ln, scale=1.0 / n, bias=EPS),
         reads=[ss.b], writes=[rstd.b])
    P.op("act", lambda e: e.activation(out=rstd.t[:], in_=rstd.t[:], func=AF.Exp, scale=-0.5),
         reads=[rstd.b], writes=[rstd.b])


EPS_AP = [None]


def norm_tile_to_T(P, Sc, C, x_t, g_rep, hT, tt, ps_ring, h_ring, ss_ring, nT):
    ss = ss_ring.next()
    rstd = ss_ring.next()
    hb = h_ring.next()
    P.op("act", lambda e: e.activation(out=hb.t[:], in_=x_t.t[:], func=AF.Square, accum_out=ss.t[:]),
         reads=[x_t.b], writes=[hb.b, ss.b])
    rms_rstd(P, ss, rstd, D)
    P.op("dve", lambda e: e.scalar_tensor_tensor(out=hb.t[:], in0=x_t.t[:], scalar=rstd.t[:], in1=g_rep.t[:],
                                                 op0=ALU.mult, op1=ALU.mult),
         reads=[x_t.b, rstd.b, g_rep.b], writes=[hb.b])
    for g4 in range(4):
        ps = ps_ring.next()
        for j in range(8):
            kc = g4 * 8 + j
            P.op("pe", lambda e, ps=ps, j=j, kc=kc: e.transpose(ps.t[:, j * 128:(j + 1) * 128], hb.t[:, kc * 128:(kc + 1) * 128], C["c_ident"].t[:]),
                 reads=[hb.b, C["c_ident"].b], writes=[ps.b])
        eng = "act" if g4 % 2 == 0 else "dve"
        dst = hT.t[:, g4 * 8:(g4 + 1) * 8, tt * 128:(tt + 1) * 128]
        src = ps.t[:].rearrange("p (j t) -> p j t", j=8)
        if eng == "act":
            P.op("act", lambda e, dst=dst, src=src: e.copy(out=dst, in_=src), reads=[ps.b], writes=[hT.b])
        else:
            P.op("dve", lambda e, dst=dst, src=src: e.tensor_copy(out=dst, in_=src), reads=[ps.b], writes=[hT.b])


PENDING_CC = []


def w_stream(P, wring, W2d, n0, n1, nk=32):
    wv = W2d.rearrange("(kc p) n -> p kc n", p=128)
    for nb in range(n0, n1, NB):
        for pc in list(PENDING_CC):
            pc[0] -= 1
            if pc[0] <= 0:
                pc[1]()
                PENDING_CC.remove(pc)
        wt = wring.next()
        step = 8
        for j in range(0, nk, step):
            P.dma(lambda e, wt=wt, j=j, nb=nb: e.dma_start(out=wt.t[:, j:j + step, :], in_=wv[:, j:j + step, nb:nb + NB]),
                  writes=[wt.b], q="pool")
        yield nb, wt


def evac(P, i, dst, src, reads, writes):
    if i % 2 == 0:
        P.op("act", lambda e: e.copy(out=dst, in_=src), reads=reads, writes=writes)
    else:
        P.op("dve", lambda e: e.tensor_copy(out=dst, in_=src), reads=reads, writes=writes)


def proj_feat(P, wt, actT, nT, nk, ps_ring, sink, cnt=[0]):
    for j in range(NB // 128):
        for tb in range(nT // 512):
            ps = ps_ring.next()
            for kc in range(nk):
                P.op("pe", mm(ps.t[:], wt.t[:, kc, j * 128:(j + 1) * 128], actT.t[:, kc, tb * 512:(tb + 1) * 512], kc == 0, kc == nk - 1),
                     reads=[wt.b, actT.b], writes=[ps.b])
            sink(j, tb, ps)


def proj_tok(P, wt, actT, nT, nk, ps_ring, sink, ncols=NB):
    for tt in range(nT // 128):
        ps = ps_ring.next()
        for kc in range(nk):
            P.op("pe", mm(ps.t[:, 0:ncols], actT.t[:, kc, tt * 128:(tt + 1) * 128], wt.t[:, kc, 0:ncols], kc == 0, kc == nk - 1),
                 reads=[wt.b, actT.b], writes=[ps.b])
        sink(tt, ps)


def phase_A(P, DR, C, AT, l, xname, xch):
    nc = P.nc
    xin = DR.get(xname, [T, D], F32)
    XA = DR.get(f"XA{l}", [XA_ROWS, T], BF16)
    aT = DR.get(f"aT{l}", [3072, T], BF16)
    bgT = DR.get(f"bgT{l}", [2048, T], BF16)
    cgT = DR.get(f"cgT{l}", [1024, T], BF16)
    w_in = DR.get("w_in", [NL, D, INW], F32)
    g_d = DR.get("pre_norm_g", [NL, D], F32)
    mem = DR.get("mem", [256, D], F32)
    kTm = DR.get("kTm", [D, 256], BF16); kTmb = DR.buf("kTm")
    vm = DR.get("vm", [256, D], BF16); vmb = DR.buf("vm")
    wk = DR.get("xa_wk", [NL, D, D], F32); wv = DR.get("xa_wv", [NL, D, D], F32)
    SA = P.scope()
    SA.__enter__()
    AT = SA.sbuf("AT", [128, 32, T], BF16)
    mT = SA.sbuf("mT", [128, 32, 256], BF16)
    with P.scope() as Sc:
        g_rep = Sc.sbuf("g_rep", [128, D], F32)
        P.dma(lambda e: e.dma_start(out=g_rep.t[:], in_=g_d[l:l + 1, :].partition_broadcast(128)), writes=[g_rep.b])
        gm_rep = grep_load(P, Sc, DR, "xa_mem_g", l)
        xr = Sc.ring("x", 3, [128, D], F32)
        hr = Sc.ring("hb", 3, [128, D], BF16)
        ssr = Sc.ring("ss", 8, [128, 1], F32)
        pst = Sc.ring("pst", 2, [128, 1024], BF16, psum=True)
        for mt in range(2):
            xt = xr.next()
            P.dma(lambda e, xt=xt, mt=mt: e.dma_start(out=xt.t[:], in_=mem[mt * 128:(mt + 1) * 128, :]), writes=[xt.b])
            norm_tile_to_T(P, Sc, C, xt, gm_rep, mT, mt, pst, hr, ssr, 256)
        for tt in range(T // 128):
            xt = xr.next()
            P.dma(lambda e, xt=xt, tt=tt: e.dma_start(out=xt.t[:], in_=xin[tt * 128:(tt + 1) * 128, :]),
                  reads=[DR.buf(xname)], writes=[xt.b])
            norm_tile_to_T(P, Sc, C, xt, g_rep, AT, tt, pst, hr, ssr, T)
    with P.scope() as Sc:
        wring = Sc.ring("w", 3, [128, 32, NB], BF16)
        psr = Sc.ring("ps", 6, [128, 512], F32, psum=True)
        stg = Sc.ring("stg", 4, [128, 512], BF16)
        ev = [0]
        segs = [(0, 3072, aT, None, 0, "feat"),
                (3072, 4096, XA, 0, 0, "feat"), (4096, 5120, XA, 1, 0, "feat"),
                (5120, 6144, XA, 0, 1024, "feat"), (6144, 7168, XA, 1, 1024, "feat"),
                (7168, 8192, XA, 0, 2048, "tok"), (8192, 9216, XA, 1, 2048, "tok"),
                (9216, 11264, bgT, None, 0, "feat"),
                (11264, 11776, XA, 0, 3072, "feat"), (11776, 12288, XA, 1, 3072, "feat"),
                (12288, 13312, cgT, None, 0, "feat")]
        names = {id(aT): f"aT{l}", id(XA): f"XA{l}", id(bgT): f"bgT{l}", id(cgT): f"cgT{l}"}
        def kv_tiles():
            for nb, wt in w_stream(P, wring, wk[l], 0, D):
                for j in range(4):
                    ps = psr.next()
                    for kc in range(32):
                        P.op("pe", mm(ps.t[:, 0:256], wt.t[:, kc, j * 128:(j + 1) * 128], mT.t[:, kc, :], kc == 0, kc == 31), reads=[wt.b, mT.b], writes=[ps.b])
                    st = stg.next(); ev[0] += 1
                    evac(P, ev[0], st.t[:, 0:256], ps.t[:, 0:256], [ps.b], [st.b])
                    row = nb + j * 128
                    P.dma(lambda e, st=st, row=row: e.dma_start(out=kTm[row:row + 128, :], in_=st.t[:, 0:256]), reads=[st.b], writes=[kTmb])
                yield
            for nb, wt in w_stream(P, wring, wv[l], 0, D):
                def sink(tt_, ps, nb=nb):
                    st = stg.next(); ev[0] += 1
                    evac(P, ev[0], st.t[:], ps.t[:], [ps.b], [st.b])
                    P.dma(lambda e: e.dma_start(out=vm[tt_ * 128:(tt_ + 1) * 128, nb:nb + NB], in_=st.t[:]), reads=[st.b], writes=[vmb])
                proj_tok(P, wt, mT, 256, 32, psr, sink)
                yield
        kvg = kv_tiles()
        ntile = [0]
        halo_done = [False]

        def write_halo():
            hs = Sc.sbuf("halo", [128, 16, 32], BF16)
            P.dma(lambda e: e.dma_start(out=hs.t[:], in_=aT[0:2048, T - 32:T].rearrange("(c p) t -> p c t", p=128)),
                  reads=[DR.buf(f"aT{l}")], writes=[hs.b])
            for sh_ in range(2):
                base = XA_NCH * 1024 + sh_ * 128
                P.dma(lambda e, base=base: e.dma_start(out=XA[base:base + 128, 0:512].rearrange("p (c t) -> p c t", t=32), in_=hs.t[:]),
                      reads=[hs.b], writes=[DR.chunkbuf(f"XA{l}", XA_NCH)])
            PENDING_CC.append([5, (lambda: xch.cc(XA_NCH))])

        for si_, (c0, c1, dst, sh, r0, mode) in enumerate(segs):
            if si_ == 1:
                write_halo()
            if sh == 0 and si_ > 1 or (sh is None and si_ > 1):
                pass
            for nb, wt in w_stream(P, wring, w_in[l], c0, c1):
                if mode == "feat":
                    def sink(j, tb, ps, nb=nb, dst=dst, r0=r0, c0=c0, sh=sh):
                        st = stg.next()
                        ev[0] += 1
                        evac(P, ev[0], st.t[:], ps.t[:], [ps.b], [st.b])
                        ro = r0 + (nb - c0) + j * 128
                        if sh is None:
                            row, dbuf = ro, DR.buf(names[id(dst)])
                        else:
                            row, dbuf = xrow(sh, ro), DR.chunkbuf(f"XA{l}", ro // 512)
                        P.dma(lambda e: e.dma_start(out=dst[row:row + 128, tb * 512:(tb + 1) * 512], in_=st.t[:]),
                              reads=[st.b], writes=[dbuf])
                    proj_feat(P, wt, AT, T, 32, psr, sink)
                else:
                    def sink(tt, ps, nb=nb, c0=c0, r0=r0, sh=sh, dst=dst):
                        st = stg.next()
                        ev[0] += 1
                        evac(P, ev[0], st.t[:], ps.t[:], [ps.b], [st.b])
                        col = nb - c0
                        ro = r0 + tt * 128
                        row = xrow(sh, ro)
                        P.dma(lambda e: e.dma_start(out=dst[row:row + 128, col:col + 512], in_=st.t[:]),
                              reads=[st.b], writes=[DR.chunkbuf(f"XA{l}", ro // 512)])
                    proj_tok(P, wt, AT, T, 32, psr, sink)
                ntile[0] += 1
                if ntile[0] % 5 != 0:
                    next(kvg, None)
            if sh == 1:
                for k_ in range(r0 // 512, (r0 + (c1 - c0) + 511) // 512):
                    PENDING_CC.append([5, (lambda k_=k_: xch.cc(k_))])
        for _ in kvg:
            pass
        for pc in list(PENDING_CC):
            pc[1]()
        PENDING_CC.clear()
    SA.__exit__(None, None, None)


def attn_item(P, C, qT, kT, nkT, vt, qb, zl, Ob, Sbufs, wk, h, XB, xbbuf, stg, ev):
    nkb = 4 * qb + 4
    first = True
    mask = C["c_mask"]
    Sx = Sbufs[0]
    P.op("pool", lambda e: e.memset(Sx.t[:], 0.0), writes=[Sx.b])
    P.op("pe", mm(Ob.t[:], C["c_zero_b"].t[:], mask.t[:, 0, :], True, False), reads=[C["c_zero_b"].b, mask.b], writes=[Ob.b])
    for kb in range(nkb - 1, -1, -1):
        diag = kb >= 4 * qb
        mi = kb - 4 * qb
        c0 = mi * 128 if diag else 0
        cs = slice(c0, 512)
        dsl = slice(c0, c0 + 128)
        Z = zl.next()
        qsl = slice(qb * 512 + c0, (qb + 1) * 512)
        P.op("pe", mm(Z.t[:, cs], kT.t[:, kb * 128:(kb + 1) * 128], qT.t[:, qsl], True, True),
             reads=[kT.b, qT.b], writes=[Z.b])
        yield
        Ft, SP, Wt = wk["F"].next(), wk["SP"].next(), wk["W"].next()
        E2 = wk["E2"].next()
        P.op("act", lambda e, E2=E2, Z=Z, cs=cs: e.activation(out=E2.t[:, cs], in_=Z.t[:, cs], func=AF.Exp, scale=SB_SCALE),
             reads=[Z.b], writes=[E2.b])
        yield
        P.op("act", lambda e, Ft=Ft, E2=E2, cs=cs: e.activation(out=Ft.t[:, cs], in_=E2.t[:, cs], func=AF.Ln, bias=1.0),
             reads=[E2.b], writes=[Ft.b])
        yield
        P.op("pool", lambda e, SP=SP, Ft=Ft, cs=cs: e.tensor_copy(out=SP.t[:, cs], in_=Ft.t[:, cs]), reads=[Ft.b], writes=[SP.b])
        if diag:
            P.op("pool", lambda e, SP=SP, mi=mi, dsl=dsl: e.tensor_tensor(out=SP.t[:, dsl], in0=SP.t[:, dsl], in1=mask.t[:, mi, dsl], op=ALU.mult),
                 reads=[SP.b, mask.b], writes=[SP.b])
        yield
        Lt = zl.next()
        P.op("pe", mm(Lt.t[:, cs], nkT.t[:, kb * 128:(kb + 1) * 128], qT.t[:, qsl], True, False),
             reads=[nkT.b, qT.b], writes=[Lt.b])
        P.op("pe", mm(Lt.t[:, cs], C["c_triU"].t[:], SP.t[:, cs], False, first), reads=[SP.b, C["c_triU"].b], writes=[Lt.b])
        if not first:
            P.op("pe", mm(Lt.t[:, cs], C["c_ones_b"].t[:], Sx.t[:, cs], False, True), reads=[Sx.b, C["c_ones_b"].b], writes=[Lt.b])
        yield
        P.op("dve", lambda e, Ft=Ft, Lt=Lt, cs=cs: e.tensor_tensor(out=Ft.t[:, cs], in0=Lt.t[:, cs], in1=Ft.t[:, cs], op=ALU.add),
             reads=[Lt.b, Ft.b], writes=[Ft.b])
        yield
        P.op("act", lambda e, Wt=Wt, Ft=Ft, cs=cs: e.activation(out=Wt.t[:, cs], in_=Ft.t[:, cs], func=AF.Exp, scale=-1.0),
             reads=[Ft.b], writes=[Wt.b])
        if diag:
            P.op("pool", lambda e, Wt=Wt, mi=mi, dsl=dsl: e.tensor_tensor(out=Wt.t[:, dsl], in0=Wt.t[:, dsl], in1=mask.t[:, mi, dsl], op=ALU.mult),
                 reads=[Wt.b, mask.b], writes=[Wt.b])
        yield
        last = kb == 0
        if diag:
            P.op("pe", mm(Ob.t[:, dsl], vt.t[:, kb, :], Wt.t[:, dsl], False, last), reads=[vt.b, Wt.b], writes=[Ob.b])
            if c0 + 128 < 512:
                osl = slice(c0 + 128, 512)
                P.op("pe", mm(Ob.t[:, osl], vt.t[:, kb, :], Wt.t[:, osl], False, last), reads=[vt.b, Wt.b], writes=[Ob.b])
        else:
            P.op("pe", mm(Ob.t[:], vt.t[:, kb, :], Wt.t[:], False, last), reads=[vt.b, Wt.b], writes=[Ob.b])
        if kb > 0:
            P.op("pool", lambda e, SP=SP, cs=cs: e.tensor_tensor(out=Sx.t[:, cs], in0=Sx.t[:, cs], in1=SP.t[:, cs], op=ALU.add),
                 reads=[SP.b, Sx.b], writes=[Sx.b])
        first = False
        yield
    st = stg.next()
    ev[0] += 1
    evac(P, ev[0], st.t[:], Ob.t[:], [Ob.b], [st.b])
    r = qb // 2
    row = xrow(r, h * 128)
    col = (qb % 2) * 512
    P.dma(lambda e: e.dma_start(out=XB[row:row + 128, col:col + 512], in_=st.t[:]), reads=[st.b], writes=[xbbuf[(h * 128) // 512]])
    yield


NSLOT = 3


def attn_gen(P, DR, C, l, Sc, xch=None):
    GA = DR.GA[l]
    XB = DR.get(f"XB{l}", [XB_ROWS, T], BF16)
    qr = Sc.ring("qT", 2, [128, S], BF16)
    kr = Sc.ring("kT", 2, [128, S], BF16)
    nkr = Sc.ring("nkT", 2, [128, S], BF16)
    vr = Sc.ring("v", 2, [128, 16, 128], BF16)
    wk = {"F": Sc.ring("F", 6, [128, 512], F32), "SP": Sc.ring("SP", 6, [128, 512], BF16), "W": Sc.ring("W", 6, [128, 512], BF16),
          "E2": Sc.ring("E2", 4, [128, 512], F32)}
    Sb = [[Sc.sbuf(f"S{i}_{j}", [128, 512], BF16) for j in range(2)] for i in range(NSLOT)]
    stg = Sc.ring("stg", 3, [128, 512], BF16)
    zl = Sc.ring("ZL", 3, [128, 512], F32, psum=True)
    Obs = [Sc.psum(f"O{i}", [128, 512], F32) for i in range(NSLOT)]
    ev = [0]
    xbb = [DR.chunkbuf(f"XB{l}", k) for k in range(3)]

    def items():
        for h in range(8):
            qT, kT, vt, nkT = qr.next(), kr.next(), vr.next(), nkr.next()
            for r in range(2):
                P.dma(lambda e, qT=qT, r=r, h=h: e.dma_start(out=qT.t[:, r * 1024:(r + 1) * 1024], in_=gsrc(e, GA, r, h * 128, 128)),
                      reads=[DR.buf(f"GA{l}_{(h * 128) // 512}")], writes=[qT.b])
                P.dma(lambda e, kT=kT, r=r, h=h: e.dma_start(out=kT.t[:, r * 1024:(r + 1) * 1024], in_=gsrc(e, GA, r, 1024 + h * 128, 128)),
                      reads=[DR.buf(f"GA{l}_{(1024 + h * 128) // 512}")], writes=[kT.b])
                for vb in range(2):
                    P.dma(lambda e, vt=vt, r=r, vb=vb, h=h: e.dma_start(
                        out=vt.t[:, r * 8 + vb * 4:r * 8 + (vb + 1) * 4, :],
                        in_=gsrc(e, GA, r, 2048 + vb * 512, 512, slice(h * 128, (h + 1) * 128)).rearrange("(kt p) d -> p kt d", p=128)),
                        reads=[DR.buf(f"GA{l}_{4 + vb}")], writes=[vt.b])
            P.op("pool", lambda e, nkT=nkT, kT=kT: e.tensor_scalar(out=nkT.t[:], in0=kT.t[:], scalar1=-SB_SCALE, scalar2=None, op0=ALU.mult),
                 reads=[kT.b], writes=[nkT.b])
            for qb in (3, 2, 1, 0):
                yield (lambda sl, qT=qT, kT=kT, nkT=nkT, vt=vt, qb=qb, h=h:
                       attn_item(P, C, qT, kT, nkT, vt, qb, zl, Obs[sl], Sb[sl], wk, h, XB, xbb, stg, ev)), h

    it = items()
    slots = [None] * NSLOT
    slot_head = [None] * NSLOT
    done = False
    left = {h: 4 for h in range(8)}
    while True:
        active = False
        for sl in range(NSLOT):
            if slots[sl] is None and not done:
                nxt = next(it, None)
                if nxt is None:
                    done = True
                else:
                    slots[sl] = nxt[0](sl)
                    slot_head[sl] = nxt[1]
            if slots[sl] is not None:
                active = True
                try:
                    next(slots[sl])
                except StopIteration:
                    slots[sl] = None
                    left[slot_head[sl]] -= 1
                    if xch is not None and all(left[h_] == 0 for h_ in range(4)):
                        xch.cc(0)
        if not active and done:
            break
        yield


def sincos(P, Sc, arg, N, tag):
    outs = []
    for which, shift in (("s", 0.0), ("c", math.pi / 2)):
        ki = Sc.sbuf(f"ki_{tag}{which}", [128, N], I32)
        kf = Sc.sbuf(f"kf_{tag}{which}", [128, N], F32)
        r = Sc.sbuf(f"r_{tag}{which}", [128, N], F32)
        o = Sc.sbuf(f"o_{tag}{which}", [128, N], F32)
        P.op("dve", lambda e, ki=ki, shift=shift: e.tensor_scalar(out=ki.t[:], in0=arg.t[:], scalar1=shift, scalar2=1.0 / TWO_PI,
                                                                  op0=ALU.add, op1=ALU.mult), reads=[arg.b], writes=[ki.b])
        P.op("dve", lambda e, ki=ki, kf=kf: e.tensor_copy(out=kf.t[:], in_=ki.t[:]), reads=[ki.b], writes=[kf.b])
        P.op("dve", lambda e, kf=kf, r=r: e.scalar_tensor_tensor(out=r.t[:], in0=kf.t[:], scalar=-TWO_PI, in1=arg.t[:],
                                                                op0=ALU.mult, op1=ALU.add), reads=[kf.b, arg.b], writes=[r.b])
        P.op("dve", lambda e, r=r, shift=shift: e.tensor_scalar(out=r.t[:], in0=r.t[:], scalar1=3.14159 - shift, scalar2=-3.14159 - shift,
                                                               op0=ALU.min, op1=ALU.max), reads=[r.b], writes=[r.b])
        sh_t = SHIFT_AP[which]
        P.op("act", lambda e, r=r, o=o, sh_t=sh_t: e.activation(out=o.t[:], in_=r.t[:], func=AF.Sin, bias=sh_t.t[:]),
             reads=[r.b, sh_t.b], writes=[o.b])
        outs.append(o)
    return outs


SHIFT_AP = {}
DBG = {}


def tt(P, eng, out, in0, in1, op, reads, writes):
    P.op(eng, lambda e: e.tensor_tensor(out=out, in0=in0, in1=in1, op=op), reads=reads, writes=writes)


def ssm_tables(P, DR, C, l, S0):
    NS = 2048
    d_rows = {n: DR.get(n, [NL, NS], F32) for n in ("s_lre_row", "s_lim_row", "s_ldt_row")}
    d_f = {n: DR.get(n, [NL, 128, 16], F32) for n in ("s_lre_f", "s_lim_f", "s_ldt_f")}
    d_B = {n: DR.get(n, [NL, 128, 4, 512], F32) for n in ("s_Bre_blk", "s_Bim_blk")}
    d_C = {n: DR.get(n, [NL, 128, 16, 128], F32) for n in ("s_Cre_blk", "s_Cim_blk")}
    d_d = DR.get("s_d", [NL, 128, 4], F32)
    if True:
        Tm_re = S0.sbuf("Tm_re", [128, NS], F32); nTm_im = S0.sbuf("nTm_im", [128, NS], F32)
        Tp_re = S0.sbuf("Tp_re", [128, 16, 128], F32); nTp_im = S0.sbuf("nTp_im", [128, 16, 128], F32)
        BDre = S0.sbuf("BDre", [128, 4, 512], BF16); BDim = S0.sbuf("BDim", [128, 4, 512], BF16)
        CBre = S0.sbuf("CBre", [128, 16, 128], BF16); CBim = S0.sbuf("CBim", [128, 16, 128], BF16); nCBim = S0.sbuf("nCBim", [128, 16, 128], BF16)
        L_re = S0.sbuf("L_re", [128, 16], F32); L_im = S0.sbuf("L_im", [128, 16], F32)
        dcol = S0.sbuf("dcol", [128, 4], F32)
        for which, val in (("s", 0.0), ("c", math.pi / 2)):
            t = S0.sbuf(f"shift_{which}", [128, 1], F32)
            P.op("dve", lambda e, t=t, val=val: e.memset(t.t[:], val), writes=[t.b])
            SHIFT_AP[which] = t
        P.dma(lambda e: e.dma_start(out=dcol.t[:], in_=d_d[l]), writes=[dcol.b])
        def row_pre(kq):
          with P.scope() as S1:
              NS_ = 512
              cq = slice(kq * 512, (kq + 1) * 512)
              def ldrow(name):
                  t = S1.sbuf(name, [128, NS_], F32)
                  P.dma(lambda e: e.dma_start(out=t.t[:], in_=d_rows[name][l:l + 1, cq].partition_broadcast(128)), writes=[t.b])
                  return t
              lre, lim, ldt = ldrow("s_lre_row"), ldrow("s_lim_row"), ldrow("s_ldt_row")
              dt = S1.sbuf("dt", [128, NS_], F32)
              P.op("act", lambda e: e.activation(out=dt.t[:], in_=ldt.t[:], func=AF.Exp), reads=[ldt.b], writes=[dt.b])
              a_r = S1.sbuf("a_r", [128, NS_], F32); b_r = S1.sbuf("b_r", [128, NS_], F32)
              tt(P, "dve", a_r.t[:], lre.t[:], dt.t[:], ALU.mult, [lre.b, dt.b], [a_r.b])
              tt(P, "dve", b_r.t[:], lim.t[:], dt.t[:], ALU.mult, [lim.b, dt.b], [b_r.b])
              niota = S1.sbuf("niota", [128, 1], F32)
              P.op("dve", lambda e: e.tensor_scalar(out=niota.t[:], in0=C["c_iotap"].t[:], scalar1=-1.0, scalar2=None, op0=ALU.mult),
                   reads=[C["c_iotap"].b], writes=[niota.b])
              arg = S1.sbuf("arg", [128, NS_], F32)
              P.op("dve", lambda e: e.tensor_scalar(out=arg.t[:], in0=b_r.t[:], scalar1=C["c_iotap"].t[:], scalar2=None, op0=ALU.mult),
                   reads=[b_r.b, C["c_iotap"].b], writes=[arg.b])
              sn, cs = sincos(P, S1, arg, NS_, "m")
              mag = S1.sbuf("mag", [128, NS_], F32)
              P.op("dve", lambda e: e.tensor_scalar(out=mag.t[:], in0=a_r.t[:], scalar1=niota.t[:], scalar2=None, op0=ALU.mult),
                   reads=[a_r.b, niota.b], writes=[mag.b])
              P.op("act", lambda e: e.activation(out=mag.t[:], in_=mag.t[:], func=AF.Exp), reads=[mag.b], writes=[mag.b])
              tt(P, "dve", Tm_re.t[:, cq], mag.t[:], cs.t[:], ALU.mult, [mag.b, cs.b], [Tm_re.b])
              tt(P, "dve", nTm_im.t[:, cq], mag.t[:], sn.t[:], ALU.mult, [mag.b, sn.b], [nTm_im.b])
              sn1, cs1 = sincos(P, S1, b_r, NS_, "one")
              m1 = S1.sbuf("m1", [128, NS_], F32)
              P.op("act", lambda e: e.activation(out=m1.t[:], in_=a_r.t[:], func=AF.Exp), reads=[a_r.b], writes=[m1.b])
              xr_ = S1.sbuf("xr", [128, NS_], F32); yr_ = S1.sbuf("yr", [128, NS_], F32)
              tt(P, "dve", xr_.t[:], m1.t[:], cs1.t[:], ALU.mult, [m1.b, cs1.b], [xr_.b])
              P.op("dve", lambda e: e.tensor_scalar(out=xr_.t[:], in0=xr_.t[:], scalar1=-1.0, scalar2=None, op0=ALU.add), reads=[xr_.b], writes=[xr_.b])
              tt(P, "dve", yr_.t[:], m1.t[:], sn1.t[:], ALU.mult, [m1.b, sn1.b], [yr_.b])
              den = S1.sbuf("den", [128, NS_], F32); t1 = S1.sbuf("t1", [128, NS_], F32); t2 = S1.sbuf("t2", [128, NS_], F32)
              tt(P, "dve", den.t[:], lre.t[:], lre.t[:], ALU.mult, [lre.b], [den.b])
              tt(P, "dve", t1.t[:], lim.t[:], lim.t[:], ALU.mult, [lim.b], [t1.b])
              tt(P, "dve", den.t[:], den.t[:], t1.t[:], ALU.add, [den.b, t1.b], [den.b])
              P.op("dve", lambda e: e.reciprocal(out=den.t[:], in_=den.t[:]), reads=[den.b], writes=[den.b])
              cre = S1.sbuf("cre", [128, NS_], F32); cim = S1.sbuf("cim", [128, NS_], F32)
              tt(P, "dve", t1.t[:], xr_.t[:], lre.t[:], ALU.mult, [xr_.b, lre.b], [t1.b])
              tt(P, "dve", t2.t[:], yr_.t[:], lim.t[:], ALU.mult, [yr_.b, lim.b], [t2.b])
              tt(P, "dve", t1.t[:], t1.t[:], t2.t[:], ALU.add, [t1.b, t2.b], [t1.b])
              tt(P, "dve", cre.t[:], t1.t[:], den.t[:], ALU.mult, [t1.b, den.b], [cre.b])
              tt(P, "dve", t1.t[:], yr_.t[:], lre.t[:], ALU.mult, [yr_.b, lre.b], [t1.b])
              tt(P, "dve", t2.t[:], xr_.t[:], lim.t[:], ALU.mult, [xr_.b, lim.b], [t2.b])
              tt(P, "dve", t1.t[:], t1.t[:], t2.t[:], ALU.subtract, [t1.b, t2.b], [t1.b])
              tt(P, "dve", cim.t[:], t1.t[:], den.t[:], ALU.mult, [t1.b, den.b], [cim.b])
              Bre = S1.sbuf("Bre", [128, NS_], F32); Bim = S1.sbuf("Bim", [128, NS_], F32)
              P.dma(lambda e: e.dma_start(out=Bre.t[:], in_=d_B["s_Bre_blk"][l][:, kq, :]), writes=[Bre.b])
              P.dma(lambda e: e.dma_start(out=Bim.t[:], in_=d_B["s_Bim_blk"][l][:, kq, :]), writes=[Bim.b])
              BDre_f = BDre.t[:, kq, :]; BDim_f = BDim.t[:, kq, :]
              tt(P, "dve", t1.t[:], cre.t[:], Bre.t[:], ALU.mult, [cre.b, Bre.b], [t1.b])
              tt(P, "dve", t2.t[:], cim.t[:], Bim.t[:], ALU.mult, [cim.b, Bim.b], [t2.b])
              tt(P, "dve", BDre_f, t1.t[:], t2.t[:], ALU.subtract, [t1.b, t2.b], [BDre.b])
              tt(P, "dve", t1.t[:], cre.t[:], Bim.t[:], ALU.mult, [cre.b, Bim.b], [t1.b])
              tt(P, "dve", t2.t[:], cim.t[:], Bre.t[:], ALU.mult, [cim.b, Bre.b], [t2.b])
              tt(P, "dve", BDim_f, t1.t[:], t2.t[:], ALU.add, [t1.b, t2.b], [BDim.b])
        for kq_ in range(4):
            row_pre(kq_)
        with P.scope() as S2:
            def ldf(name):
                t = S2.sbuf(name, [128, 16], F32)
                P.dma(lambda e: e.dma_start(out=t.t[:], in_=d_f[name][l]), writes=[t.b])
                return t
            lre, lim, ldt = ldf("s_lre_f"), ldf("s_lim_f"), ldf("s_ldt_f")
            dt = S2.sbuf("dtf", [128, 16], F32)
            P.op("act", lambda e: e.activation(out=dt.t[:], in_=ldt.t[:], func=AF.Exp), reads=[ldt.b], writes=[dt.b])
            a_f = S2.sbuf("a_f", [128, 16], F32); b_f = S2.sbuf("b_f", [128, 16], F32)
            tt(P, "dve", a_f.t[:], lre.t[:], dt.t[:], ALU.mult, [lre.b, dt.b], [a_f.b])
            tt(P, "dve", b_f.t[:], lim.t[:], dt.t[:], ALU.mult, [lim.b, dt.b], [b_f.b])
            argp = S2.sbuf("argp", [128, 16 * 128], F32); ea = S2.sbuf("ea", [128, 16 * 128], F32)
            for i in range(16):
                P.op("dve", lambda e, i=i: e.tensor_scalar(out=argp.t[:, i * 128:(i + 1) * 128], in0=C["c_iota"].t[:], scalar1=b_f.t[:, i:i + 1], scalar2=None, op0=ALU.mult),
                     reads=[b_f.b, C["c_iota"].b], writes=[argp.b])
                P.op("dve", lambda e, i=i: e.tensor_scalar(out=ea.t[:, i * 128:(i + 1) * 128], in0=C["c_iota"].t[:], scalar1=a_f.t[:, i:i + 1], scalar2=None, op0=ALU.mult),
                     reads=[a_f.b, C["c_iota"].b], writes=[ea.b])
            snp, csp = sincos(P, S2, argp, 2048, "p")
            P.op("act", lambda e: e.activation(out=ea.t[:], in_=ea.t[:], func=AF.Exp), reads=[ea.b], writes=[ea.b])
            Tp_re_f = Tp_re.t[:].rearrange("p i t -> p (i t)"); nTp_im_f = nTp_im.t[:].rearrange("p i t -> p (i t)")
            tt(P, "dve", Tp_re_f, ea.t[:], csp.t[:], ALU.mult, [ea.b, csp.b], [Tp_re.b])
            P.op("dve", lambda e: e.scalar_tensor_tensor(out=nTp_im_f, in0=ea.t[:], scalar=-1.0, in1=snp.t[:], op0=ALU.mult, op1=ALU.mult),
                 reads=[ea.b, snp.b], writes=[nTp_im.b])
            a128 = S2.sbuf("a128", [128, 16], F32); b128 = S2.sbuf("b128", [128, 16], F32)
            P.op("dve", lambda e: e.tensor_scalar(out=b128.t[:], in0=b_f.t[:], scalar1=128.0, scalar2=None, op0=ALU.mult), reads=[b_f.b], writes=[b128.b])
            sL, cL = sincos(P, S2, b128, 16, "L")
            P.op("act", lambda e: e.activation(out=a128.t[:], in_=a_f.t[:], func=AF.Exp, scale=128.0), reads=[a_f.b], writes=[a128.b])
            tt(P, "dve", L_re.t[:], a128.t[:], cL.t[:], ALU.mult, [a128.b, cL.b], [L_re.b])
            tt(P, "dve", L_im.t[:], a128.t[:], sL.t[:], ALU.mult, [a128.b, sL.b], [L_im.b])
            for nm, dst in (("s_Cre_blk", CBre), ("s_Cim_blk", CBim)):
                tmp = S2.sbuf("ctmp" + nm, [128, 16, 128], F32)
                P.dma(lambda e, tmp=tmp, nm=nm: e.dma_start(out=tmp.t[:], in_=d_C[nm][l]), writes=[tmp.b])
                P.op("dve", lambda e, tmp=tmp, dst=dst: e.tensor_copy(out=dst.t[:], in_=tmp.t[:]), reads=[tmp.b], writes=[dst.b])
                if nm == "s_Cim_blk":
                    P.op("dve", lambda e, tmp=tmp: e.tensor_scalar(out=nCBim.t[:], in0=tmp.t[:], scalar1=-1.0, scalar2=None, op0=ALU.mult), reads=[tmp.b], writes=[nCBim.b])
    return dict(Tm_re=Tm_re, nTm_im=nTm_im, Tp_re=Tp_re, nTp_im=nTp_im, BDre=BDre, BDim=BDim, CBre=CBre, CBim=CBim, nCBim=nCBim,
                L_re=L_re, L_im=L_im, dcol=dcol)


def ssm_gen(P, DR, C, l, S3, tb, Ba, Bb):
    GA = DR.GA[l]
    XB = DR.get(f"XB{l}", [XB_ROWS, T], BF16)
    xbb = DR.chunkbuf(f"XB{l}", 2)
    Tm_re, nTm_im, Tp_re, nTp_im = tb["Tm_re"], tb["nTm_im"], tb["Tp_re"], tb["nTp_im"]
    BDre, BDim, CBre, CBim, nCBim, L_re, L_im, dcol = tb["BDre"], tb["BDim"], tb["CBre"], tb["CBim"], tb["nCBim"], tb["L_re"], tb["L_im"], tb["dcol"]
    uT = S3.sbuf("uT", [128, 4, S], BF16)
    for r in range(2):
        P.dma(lambda e, r=r: e.dma_start(out=uT.t[:, :, r * 1024:(r + 1) * 1024],
                                         in_=gsrc(e, GA, r, 3072, 512).rearrange("(kc p) t -> p kc t", p=128)),
              reads=[DR.buf(f"GA{l}_6")], writes=[uT.b])
    ysr = [S3.ring(f"ys{k}", 2, [128, 512], F32) for k in range(4)]
    qring = S3.ring("Q", 8, [128, 512], BF16)
    pring = S3.ring("Pp", 8, [128, 4, 128], BF16)
    cr = [[S3.sbuf(f"cr{k}_{i}", [128, 4], F32) for i in range(2)] for k in range(4)]
    ci = [[S3.sbuf(f"ci{k}_{i}", [128, 4], F32) for i in range(2)] for k in range(4)]
    tmp4 = S3.ring("tmp4", 12, [128, 4], F32)
    g1 = S3.ring("g1", 2, [128, 512], F32)
    gob = S3.ring("gob", 2, [128, 512], BF16)
    for k in range(4):
        P.op("dve", lambda e, k=k: e.memset(cr[k][0].t[:], 0.0), writes=[cr[k][0].b])
        P.op("dve", lambda e, k=k: e.memset(ci[k][0].t[:], 0.0), writes=[ci[k][0].b])
    triI, ntriI = C["c_triI"], C["c_ntriI"]
    Ga = Ba.t[:].rearrange("p (i t) -> p i t", i=4)
    Gb = Bb.t[:].rearrange("p (i t) -> p i t", i=4)
    ycur = [None] * 4
    for j in range(16):
        for kc in range(4):
            cur, nxt = j % 2, (j + 1) % 2
            cre_t, cim_t = cr[kc][cur], ci[kc][cur]
            P.op("pe", mm(Ba.t[:], uT.t[:, kc, j * 128:(j + 1) * 128], BDre.t[:, kc, :], True, True), reads=[uT.b, BDre.b], writes=[Ba.b])
            P.op("pe", mm(Bb.t[:], uT.t[:, kc, j * 128:(j + 1) * 128], BDim.t[:, kc, :], True, True), reads=[uT.b, BDim.b], writes=[Bb.b])
            yield
            Q1, Q2, Q3, Q4 = qring.next(), qring.next(), qring.next(), qring.next()
            sl = slice(kc * 512, (kc + 1) * 512)
            tt(P, "dve", Q1.t[:], Ba.t[:], Tm_re.t[:, sl], ALU.mult, [Ba.b, Tm_re.b], [Q1.b])
            tt(P, "dve", Q2.t[:], Bb.t[:], nTm_im.t[:, sl], ALU.mult, [Bb.b, nTm_im.b], [Q2.b])
            tt(P, "dve", Q3.t[:], Bb.t[:], Tm_re.t[:, sl], ALU.mult, [Bb.b, Tm_re.b], [Q3.b])
            tt(P, "dve", Q4.t[:], Ba.t[:], nTm_im.t[:, sl], ALU.mult, [Ba.b, nTm_im.b], [Q4.b])
            yield
            for i in range(4):
                cs_ = slice(i * 128, (i + 1) * 128)
                P.op("pe", mm(Ga[:, i, :], Q1.t[:, cs_], triI.t[:], True, False), reads=[Q1.b, triI.b], writes=[Ba.b])
                P.op("pe", mm(Ga[:, i, :], Q2.t[:, cs_], triI.t[:], False, True), reads=[Q2.b, triI.b], writes=[Ba.b])
                P.op("pe", mm(Gb[:, i, :], Q3.t[:, cs_], triI.t[:], True, False), reads=[Q3.b, triI.b], writes=[Bb.b])
                P.op("pe", mm(Gb[:, i, :], Q4.t[:, cs_], ntriI.t[:], False, True), reads=[Q4.b, ntriI.b], writes=[Bb.b])
            yield
            P1, P2, P3, P4 = pring.next(), pring.next(), pring.next(), pring.next()
            for i in range(4):
                ti = kc * 4 + i
                for (Pt, G, Gbuf, cc, tab) in ((P1, Ga, Ba, cre_t, Tp_re), (P2, Gb, Bb, cim_t, nTp_im), (P3, Ga, Ba, cre_t, nTp_im), (P4, Gb, Bb, cim_t, Tp_re)):
                    P.op("dve", lambda e, Pt=Pt, G=G, cc=cc, tab=tab, i=i, ti=ti: e.scalar_tensor_tensor(
                        out=Pt.t[:, i, :], in0=G[:, i, :], scalar=cc.t[:, i:i + 1], in1=tab.t[:, ti, :], op0=ALU.add, op1=ALU.mult),
                        reads=[Gbuf.b, cc.b, tab.b], writes=[Pt.b])
            if j < 15:
                gcr, gci, u1, u2 = tmp4.next(), tmp4.next(), tmp4.next(), tmp4.next()
                Lr = L_re.t[:, kc * 4:(kc + 1) * 4]; Li = L_im.t[:, kc * 4:(kc + 1) * 4]
                tt(P, "dve", gcr.t[:], Ga[:, :, 127], cre_t.t[:], ALU.add, [Ba.b, cre_t.b], [gcr.b])
                tt(P, "dve", gci.t[:], Gb[:, :, 127], cim_t.t[:], ALU.add, [Bb.b, cim_t.b], [gci.b])
                ncr, nci = cr[kc][nxt], ci[kc][nxt]
                tt(P, "dve", u1.t[:], gcr.t[:], Lr, ALU.mult, [gcr.b, L_re.b], [u1.b])
                tt(P, "dve", u2.t[:], gci.t[:], Li, ALU.mult, [gci.b, L_im.b], [u2.b])
                tt(P, "dve", ncr.t[:], u1.t[:], u2.t[:], ALU.subtract, [u1.b, u2.b], [ncr.b])
                u3, u4 = tmp4.next(), tmp4.next()
                tt(P, "dve", u3.t[:], gci.t[:], Lr, ALU.mult, [gci.b, L_re.b], [u3.b])
                tt(P, "dve", u4.t[:], gcr.t[:], Li, ALU.mult, [gcr.b, L_im.b], [u4.b])
                tt(P, "dve", nci.t[:], u3.t[:], u4.t[:], ALU.add, [u3.b, u4.b], [nci.b])
            yield
            Y = Ba.t[:, 0:128]
            n = 0
            for i in range(4):
                ti = kc * 4 + i
                for (Cm, Pt) in ((CBre, P1), (CBre, P2), (CBim, P3), (nCBim, P4)):
                    P.op("pe", mm(Y, Cm.t[:, ti, :], Pt.t[:, i, :], n == 0, n == 15), reads=[Cm.b, Pt.b], writes=[Ba.b])
                    n += 1
            yield
            if j % 4 == 0:
                ycur[kc] = ysr[kc].next()
            yt = ycur[kc]
            jj = j % 4
            P.op("dve", lambda e, yt=yt, kc=kc, j=j, jj=jj: e.scalar_tensor_tensor(
                out=yt.t[:, jj * 128:(jj + 1) * 128], in0=uT.t[:, kc, j * 128:(j + 1) * 128], scalar=dcol.t[:, kc:kc + 1],
                in1=Ba.t[:, 0:128], op0=ALU.mult, op1=ALU.add), reads=[uT.b, dcol.b, Ba.b], writes=[yt.b])
            if jj == 3:
                a_ = g1.next(); o = gob.next()
                y = yt.t[:]
                tt(P, "dve", a_.t[:], y, y, ALU.mult, [yt.b], [a_.b])
                P.op("dve", lambda e, a_=a_: e.tensor_scalar(out=a_.t[:], in0=a_.t[:], scalar1=0.044715, scalar2=1.0, op0=ALU.mult, op1=ALU.add), reads=[a_.b], writes=[a_.b])
                tt(P, "dve", a_.t[:], a_.t[:], y, ALU.mult, [a_.b, yt.b], [a_.b])
                P.op("act", lambda e, a_=a_: e.activation(out=a_.t[:], in_=a_.t[:], func=AF.Sigmoid, scale=1.5957691216057308), reads=[a_.b], writes=[a_.b])
                tt(P, "dve", o.t[:], a_.t[:], y, ALU.mult, [a_.b, yt.b], [o.b])
                t0 = (j - 3) * 128
                r = t0 // 1024
                row = xrow(r, 1024 + kc * 128)
                col = t0 % 1024
                P.dma(lambda e, o=o, row=row, col=col: e.dma_start(out=XB[row:row + 128, col:col + 512], in_=o.t[:]), reads=[o.b], writes=[xbb])
            yield


def phase_B(P, DR, C, l, xch):
    with P.scope() as S0:
        tb = ssm_tables(P, DR, C, l, S0)
        with P.scope() as Sc:
            Ba = Sc.psum("ssmA", [128, 512], F32)
            Bb = Sc.psum("ssmB", [128, 512], F32)
            ga = attn_gen(P, DR, C, l, Sc, xch)
            gs = ssm_gen(P, DR, C, l, Sc, tb, Ba, Bb)
            a_live = s_live = True
            rnd = 0
            while a_live or s_live:
                rnd += 1
                if a_live:
                    try:
                        next(ga)
                    except StopIteration:
                        a_live = False
                if s_live and (rnd % 2 == 0 or not a_live):
                    try:
                        next(gs)
                    except StopIteration:
                        s_live = False
                        xch.cc(2)


def grep_load(P, Sc, DR, name, l):
    g_d = DR.get(name, [NL, D], F32)
    t = Sc.sbuf("g_" + name, [128, D], F32)
    P.dma(lambda e: e.dma_start(out=t.t[:], in_=g_d[l:l + 1, :].partition_broadcast(128)), writes=[t.b])
    return t


def row_rstd(P, ps_pair, n, out_t):
    for hb in range(2):
        P.op("act", lambda e, hb=hb: e.activation(out=out_t.t[:, hb * 512:(hb + 1) * 512], in_=ps_pair[hb].t[:], func=AF.Ln, scale=1.0 / n, bias=EPS),
             reads=[ps_pair[hb].b], writes=[out_t.b])
    P.op("act", lambda e: e.activation(out=out_t.t[:], in_=out_t.t[:], func=AF.Exp, scale=-0.5), reads=[out_t.b], writes=[out_t.b])


def sumsq_acc(P, C, ps_pair, src_bf, sqr, first, last, extra_reads):
    sq = sqr.next()
    P.op("act", lambda e: e.activation(out=sq.t[:], in_=src_bf, func=AF.Square), reads=extra_reads, writes=[sq.b])
    for hb in range(2):
        P.op("pe", mm(ps_pair[hb].t[:], C["c_ones_b"].t[:], sq.t[:, hb * 512:(hb + 1) * 512], first, last),
             reads=[sq.b, C["c_ones_b"].b], writes=[ps_pair[hb].b])


def phase_C(P, DR, C, AT_unused, l, xname, xoname):
    GA = DR.GA[l]
    GB = DR.GB[l]
    aT = DR.get(f"aT{l}", [3072, T], BF16); aTb = DR.buf(f"aT{l}")
    bgT = DR.get(f"bgT{l}", [2048, T], BF16); bgb = DR.buf(f"bgT{l}")
    cgT = DR.get(f"cgT{l}", [1024, T], BF16); cgb = DR.buf(f"cgT{l}")
    xin = DR.get(xname, [T, D], F32); xinb = DR.buf(xname)
    xo = DR.get(xoname, [T, D], F32); xob = DR.buf(xoname)
    mem = DR.get("mem", [256, D], F32)
    r1 = DR.get("r1", [T, D], F32); r1b = DR.buf("r1")
    x1 = DR.get("x1", [T, D], F32); x1b = DR.buf("x1")
    qTd = DR.get("qTd", [D, T], BF16); qTb = DR.buf("qTd")
    kTm = DR.get("kTm", [D, 256], BF16); kTmb = DR.buf("kTm")
    vm = DR.get("vm", [256, D], BF16); vmb = DR.buf("vm")
    d_convw = DR.get("conv_wT", [NL, 128, 8, 31], F32)
    d_cb = DR.get("conv_b_l", [NL, 128, 8], F32); d_lg = DR.get("conv_ln_g_l", [NL, 128, 8], F32); d_lb = DR.get("conv_ln_b_l", [NL, 128, 8], F32)
    d_bg = DR.get("branch_g_l", [NL, 128, 32], F32); d_glub = DR.get("glu_b_l", [NL, 128, 8], F32)
    d_hs = DR.get("halo_scale", [128, 1], F32)
    w_out = DR.get("w_out", [NL, D, D], F32); wq = DR.get("xa_wq", [NL, D, D], F32); wk = DR.get("xa_wk", [NL, D, D], F32)
    wv = DR.get("xa_wv", [NL, D, D], F32); wo = DR.get("xa_wo", [NL, D, D], F32); gluw = DR.get("ssm_glu_w", [NL, 1024, 1024], F32)

    SA = P.scope()
    SA.__enter__()
    AT = SA.sbuf("AT", [128, 32, T], BF16)
    with P.scope() as Sc:
        cw = Sc.sbuf("cw", [128, 8, 31], F32); cb = Sc.sbuf("cb", [128, 8], F32); lg = Sc.sbuf("lg", [128, 8], F32); lb = Sc.sbuf("lb", [128, 8], F32)
        bgn = Sc.sbuf("bgn", [128, 32], F32); glub = Sc.sbuf("glub", [128, 8], F32); hsc = Sc.sbuf("hsc", [128, 1], F32)
        for t_, d_ in ((cw, d_convw), (cb, d_cb), (lg, d_lg), (lb, d_lb), (bgn, d_bg), (glub, d_glub)):
            P.dma(lambda e, t_=t_, d_=d_: e.dma_start(out=t_.t[:], in_=d_[l]), writes=[t_.b])
        P.dma(lambda e: e.dma_start(out=hsc.t[:], in_=d_hs), writes=[hsc.b])
        sqr = Sc.ring("sq", 3, [128, T], BF16)
        S1p = [Sc.psum(f"S1p{i}", [128, 512], F32) for i in range(2)]
        S2p = [Sc.psum(f"S2p{i}", [128, 512], F32) for i in range(2)]
        SSp = [Sc.psum(f"SSp{i}", [128, 512], F32) for i in range(2)]
        rstd_br = [Sc.sbuf(f"rstd_br{i}", [128, T], F32) for i in range(3)]
        def halo_v(e, slot):
            return GA[XA_NCH][0:128, slot * 32:(slot + 1) * 32]
        gab = DR.buf(f"GA{l}_{XA_NCH}")

        def conv_part():
          with P.scope() as S1:
            yconv = S1.sbuf("yconv", [128, 8, T], F32)
            vr = S1.ring("val", 3, [128, T + 32], BF16); gr_ = S1.ring("glu", 3, [128, T + 32], BF16)
            sgr = S1.ring("sig", 2, [128, T + 32], F32); ur = S1.ring("u", 3, [128, T + 32], BF16)
            ybr = S1.ring("yb", 2, [128, T], BF16)
            dgr = S1.ring("dg", 2, [128, 31, 128], BF16)
            cps = S1.ring("cps", 2, [128, 512], F32, psum=True)
            def prep(cc):
                val, glu, sig, u = vr.next(), gr_.next(), sgr.next(), ur.next()
                P.dma(lambda e: e.dma_start(out=val.t[:, 32:], in_=aT[cc * 128:(cc + 1) * 128, :]), reads=[aTb], writes=[val.b])
                P.dma(lambda e: e.dma_start(out=glu.t[:, 32:], in_=aT[1024 + cc * 128:1024 + (cc + 1) * 128, :]), reads=[aTb], writes=[glu.b])
                P.dma(lambda e: e.dma_start(out=val.t[:, 0:32], in_=halo_v(e, cc)), reads=[gab], writes=[val.b])
                P.dma(lambda e: e.dma_start(out=glu.t[:, 0:32], in_=halo_v(e, 8 + cc)), reads=[gab], writes=[glu.b])
                dg = dgr.next()
                for k in range(31):
                    if k % 2 == 0:
                        P.op("act", lambda e, k=k: e.activation(out=dg.t[:, k, :], in_=C["c_ident"].t[:], func=AF.Copy, scale=cw.t[:, cc, k:k + 1]),
                             reads=[C["c_ident"].b, cw.b], writes=[dg.b])
                    else:
                        P.op("dve", lambda e, k=k: e.tensor_scalar(out=dg.t[:, k, :], in0=C["c_ident"].t[:], scalar1=cw.t[:, cc, k:k + 1], scalar2=None, op0=ALU.mult),
                             reads=[C["c_ident"].b, cw.b], writes=[dg.b])
                P.op("act", lambda e: e.activation(out=sig.t[:], in_=glu.t[:], func=AF.Sigmoid), reads=[glu.b], writes=[sig.b])
                tt(P, "dve", u.t[:], val.t[:], sig.t[:], ALU.mult, [val.b, sig.b], [u.b])
                P.op("dve", lambda e: e.tensor_scalar(out=u.t[:, 0:32], in0=u.t[:, 0:32], scalar1=hsc.t[:], scalar2=None, op0=ALU.mult),
                     reads=[u.b, hsc.b], writes=[u.b])
                return dg, u

            def main(cc, dg, u):
                acc = yconv.t[:, cc, :]
                for tb in range(2):
                    ps = cps.next()
                    for k in range(31):
                        P.op("pe", mm(ps.t[:], dg.t[:, k, :], u.t[:, 2 + k + tb * 512:2 + k + tb * 512 + 512], k == 0, k == 30), reads=[dg.b, u.b], writes=[ps.b])
                    P.op("act", lambda e, ps=ps, tb=tb: e.activation(out=yconv.t[:, cc, tb * 512:(tb + 1) * 512], in_=ps.t[:], func=AF.Identity, bias=cb.t[:, cc:cc + 1]),
                         reads=[ps.b, cb.b], writes=[yconv.b])
                yb = ybr.next()
                P.op("act", lambda e: e.copy(out=yb.t[:], in_=acc), reads=[yconv.b], writes=[yb.b])
                for hb in range(2):
                    P.op("pe", mm(S1p[hb].t[:], C["c_ones_b"].t[:], yb.t[:, hb * 512:(hb + 1) * 512], cc == 0, cc == 7), reads=[yb.b, C["c_ones_b"].b], writes=[S1p[hb].b])
                sumsq_acc(P, C, S2p, yb.t[:], sqr, cc == 0, cc == 7, [yb.b])

            nxt_ = prep(0)
            for cc in range(8):
                cur_ = nxt_
                if cc + 1 < 8:
                    nxt_ = prep(cc + 1)
                main(cc, *cur_)
            mu = S1.sbuf("mu", [128, T], F32); rs = S1.sbuf("rs", [128, T], F32); m2 = S1.sbuf("m2", [128, T], F32)
            for hb in range(2):
                hs_ = slice(hb * 512, (hb + 1) * 512)
                P.op("act", lambda e, hb=hb, hs_=hs_: e.mul(out=mu.t[:, hs_], in_=S1p[hb].t[:], mul=1.0 / 1024), reads=[S1p[hb].b], writes=[mu.b])
                P.op("act", lambda e, hb=hb, hs_=hs_: e.mul(out=rs.t[:, hs_], in_=S2p[hb].t[:], mul=1.0 / 1024), reads=[S2p[hb].b], writes=[rs.b])
            tt(P, "dve", m2.t[:], mu.t[:], mu.t[:], ALU.mult, [mu.b], [m2.b])
            tt(P, "dve", rs.t[:], rs.t[:], m2.t[:], ALU.subtract, [rs.b, m2.b], [rs.b])
            P.op("act", lambda e: e.activation(out=rs.t[:], in_=rs.t[:], func=AF.Ln, bias=EPS), reads=[rs.b], writes=[rs.b])
            P.op("act", lambda e: e.activation(out=rs.t[:], in_=rs.t[:], func=AF.Exp, scale=-0.5), reads=[rs.b], writes=[rs.b])
            gtr = S1.ring("gt", 3, [128, T], BF16); gsr = S1.ring("gs", 2, [128, T], F32); zr = S1.ring("z", 2, [128, T], F32)
            for cc in range(8):
                acc = yconv.t[:, cc, :]
                z = zr.next(); gt = gtr.next(); gs = gsr.next()
                tt(P, "dve", z.t[:], acc, mu.t[:], ALU.subtract, [yconv.b, mu.b], [z.b])
                tt(P, "dve", z.t[:], z.t[:], rs.t[:], ALU.mult, [z.b, rs.b], [z.b])
                P.op("dve", lambda e, z=z, cc=cc: e.tensor_scalar(out=z.t[:], in0=z.t[:], scalar1=lg.t[:, cc:cc + 1], scalar2=lb.t[:, cc:cc + 1], op0=ALU.mult, op1=ALU.add),
                     reads=[z.b, lg.b, lb.b], writes=[z.b])
                P.op("act", lambda e, z=z: e.activation(out=z.t[:], in_=z.t[:], func=AF.Silu), reads=[z.b], writes=[z.b])
                P.dma(lambda e, gt=gt, cc=cc: e.dma_start(out=gt.t[:], in_=aT[2048 + cc * 128:2048 + (cc + 1) * 128, :]), reads=[aTb], writes=[gt.b])
                P.op("act", lambda e, gs=gs, gt=gt: e.activation(out=gs.t[:], in_=gt.t[:], func=AF.Silu), reads=[gt.b], writes=[gs.b])
                tt(P, "dve", AT.t[:, cc, :], z.t[:], gs.t[:], ALU.mult, [z.b, gs.b], [AT.b])
                sumsq_acc(P, C, SSp, AT.t[:, cc, :], sqr, cc == 0, cc == 7, [AT.b])
            row_rstd(P, SSp, 1024, rstd_br[0])
        conv_part()

        def attn_part():
          with P.scope() as S1:
            orr = S1.ring("o", 4, [128, T], BF16); gtr = S1.ring("gt", 4, [128, T], BF16); gsr = S1.ring("gs", 4, [128, T], F32)
            for hc in range(16):
                o = orr.next(); gt = gtr.next(); gs = gsr.next()
                r, h = hc // 8, hc % 8
                P.dma(lambda e, o=o, r=r, h=h: e.dma_start(out=o.t[:], in_=gsrc(e, GB, r, h * 128, 128)), reads=[DR.buf(f"GB{l}_{(h * 128) // 512}")], writes=[o.b])
                P.dma(lambda e, gt=gt, hc=hc: e.dma_start(out=gt.t[:], in_=bgT[hc * 128:(hc + 1) * 128, :]), reads=[bgb], writes=[gt.b])
                P.op("act", lambda e, gs=gs, gt=gt: e.activation(out=gs.t[:], in_=gt.t[:], func=AF.Silu), reads=[gt.b], writes=[gs.b])
                tt(P, "dve", AT.t[:, 8 + hc, :], o.t[:], gs.t[:], ALU.mult, [o.b, gs.b], [AT.b])
                sumsq_acc(P, C, SSp, AT.t[:, 8 + hc, :], sqr, hc == 0, hc == 15, [AT.b])
            row_rstd(P, SSp, 2048, rstd_br[1])
        attn_part()

        def ssm_part():
          with P.scope() as S1:
            zps = S1.ring("zps", 2, [128, 512], F32, psum=True)
            ygT = S1.sbuf("ygT", [128, 8, T], BF16)
            gw = S1.sbuf("gw", [128, 8, 1024], BF16)
            gwv = gluw[l].rearrange("(kc p) n -> p kc n", p=128)
            for j in range(0, 8, 4):
                P.dma(lambda e, j=j: e.dma_start(out=gw.t[:, j:j + 4, :], in_=gwv[:, j:j + 4, :]), writes=[gw.b], q="pool")
            for c in range(8):
                r, k4 = c // 4, c % 4
                P.dma(lambda e, c=c, r=r, k4=k4: e.dma_start(out=ygT.t[:, c, :], in_=gsrc(e, GB, r, 1024 + k4 * 128, 128)), reads=[DR.buf(f"GB{l}_2")], writes=[ygT.b])
            sgr = S1.ring("sg", 3, [128, T], F32); gtr = S1.ring("gt", 3, [128, T], BF16); gsr = S1.ring("gs", 3, [128, T], F32)
            for n in range(8):
                sg = sgr.next(); gt = gtr.next(); gs = gsr.next()
                for tb in range(2):
                    ps = zps.next()
                    for kc in range(8):
                        P.op("pe", mm(ps.t[:], gw.t[:, kc, n * 128:(n + 1) * 128], ygT.t[:, kc, tb * 512:(tb + 1) * 512], kc == 0, kc == 7), reads=[gw.b, ygT.b], writes=[ps.b])
                    P.op("act", lambda e, sg=sg, ps=ps, tb=tb, n=n: e.activation(out=sg.t[:, tb * 512:(tb + 1) * 512], in_=ps.t[:], func=AF.Sigmoid, bias=glub.t[:, n:n + 1]),
                         reads=[ps.b, glub.b], writes=[sg.b])
                P.dma(lambda e, gt=gt, n=n: e.dma_start(out=gt.t[:], in_=cgT[n * 128:(n + 1) * 128, :]), reads=[cgb], writes=[gt.b])
                P.op("act", lambda e, gs=gs, gt=gt: e.activation(out=gs.t[:], in_=gt.t[:], func=AF.Silu), reads=[gt.b], writes=[gs.b])
                tt(P, "dve", sg.t[:], sg.t[:], ygT.t[:, n, :], ALU.mult, [sg.b, ygT.b], [sg.b])
                tt(P, "dve", AT.t[:, 24 + n, :], sg.t[:], gs.t[:], ALU.mult, [sg.b, gs.b], [AT.b])
                sumsq_acc(P, C, SSp, AT.t[:, 24 + n, :], sqr, n == 0, n == 7, [AT.b])
            row_rstd(P, SSp, 1024, rstd_br[2])
        ssm_part()
        for c in range(32):
            br = 0 if c < 8 else (1 if c < 24 else 2)
            P.op("dve", lambda e, c=c, br=br: e.scalar_tensor_tensor(out=AT.t[:, c, :], in0=AT.t[:, c, :], scalar=bgn.t[:, c:c + 1], in1=rstd_br[br].t[:],
                                                                  op0=ALU.mult, op1=ALU.mult), reads=[AT.b, bgn.b, rstd_br[br].b], writes=[AT.b])

    def proj_to_r1(W2d):
        with P.scope() as Sc:
            wring = Sc.ring("w", 3, [128, 32, NB], BF16)
            psr = Sc.ring("ps", 6, [128, 512], F32, psum=True)
            stg = Sc.ring("stg", 4, [128, 512], F32)
            ev = [0]
            for nb, wt in w_stream(P, wring, W2d, 0, D):
                def sink(tt_, ps, nb=nb):
                    st = stg.next(); ev[0] += 1
                    evac(P, ev[0], st.t[:], ps.t[:], [ps.b], [st.b])
                    P.dma(lambda e: e.dma_start(out=r1[tt_ * 128:(tt_ + 1) * 128, nb:nb + NB], in_=st.t[:]), reads=[st.b], writes=[r1b])
                proj_tok(P, wt, AT, T, 32, psr, sink)

    def resid_norm(gname, xsrc, xsrcb, xdst, xdstb, to_AT_gname=None):
        with P.scope() as Sc:
            g_rep = grep_load(P, Sc, DR, gname, l)
            g2 = grep_load(P, Sc, DR, to_AT_gname, l) if to_AT_gname else None
            rr = Sc.ring("r", 2 if to_AT_gname else 3, [128, D], F32); xr = Sc.ring("x", 3, [128, D], F32)
            ssr = Sc.ring("ss", 12, [128, 1], F32)
            hr = Sc.ring("hb", 2, [128, D], BF16)
            if to_AT_gname:
                pst = Sc.ring("pst", 2, [128, 1024], BF16, psum=True)
            for tt_ in range(T // 128):
                r_t = rr.next(); x_t = xr.next(); ss = ssr.next(); rstd = ssr.next()
                P.dma(lambda e, r_t=r_t, tt_=tt_: e.dma_start(out=r_t.t[:], in_=r1[tt_ * 128:(tt_ + 1) * 128, :]), reads=[r1b], writes=[r_t.b])
                P.dma(lambda e, x_t=x_t, tt_=tt_: e.dma_start(out=x_t.t[:], in_=xsrc[tt_ * 128:(tt_ + 1) * 128, :]), reads=[xsrcb], writes=[x_t.b])
                junk = hr.next()
                P.op("act", lambda e, junk=junk, r_t=r_t, ss=ss: e.activation(out=junk.t[:], in_=r_t.t[:], func=AF.Square, accum_out=ss.t[:]), reads=[r_t.b], writes=[junk.b, ss.b])
                rms_rstd(P, ss, rstd, D)
                P.op("dve", lambda e, r_t=r_t, rstd=rstd: e.scalar_tensor_tensor(out=r_t.t[:], in0=r_t.t[:], scalar=rstd.t[:], in1=g_rep.t[:], op0=ALU.mult, op1=ALU.mult),
                     reads=[r_t.b, rstd.b, g_rep.b], writes=[r_t.b])
                tt(P, "dve", x_t.t[:], x_t.t[:], r_t.t[:], ALU.add, [x_t.b, r_t.b], [x_t.b])
                P.dma(lambda e, x_t=x_t, tt_=tt_: e.dma_start(out=xdst[tt_ * 128:(tt_ + 1) * 128, :], in_=x_t.t[:]), reads=[x_t.b], writes=[xdstb], q="pool")
                if to_AT_gname:
                    norm_tile_to_T(P, Sc, C, x_t, g2, AT, tt_, pst, hr, ssr, T)

    proj_to_r1(w_out[l])
    resid_norm("post_norm_g", xin, xinb, x1, x1b, to_AT_gname="xa_pre_g")

    with P.scope() as Sc:
        wring = Sc.ring("w", 3, [128, 32, NB], BF16)
        psr = Sc.ring("ps", 6, [128, 512], F32, psum=True)
        stg = Sc.ring("stg", 4, [128, 512], BF16)
        ev = [0]
        for nb, wt in w_stream(P, wring, wq[l], 0, D):
            def sink(j, tb, ps, nb=nb):
                st = stg.next(); ev[0] += 1
                evac(P, ev[0], st.t[:], ps.t[:], [ps.b], [st.b])
                row = nb + j * 128
                P.dma(lambda e: e.dma_start(out=qTd[row:row + 128, tb * 512:(tb + 1) * 512], in_=st.t[:]), reads=[st.b], writes=[qTb])
            proj_feat(P, wt, AT, T, 32, psr, sink)

    with P.scope() as Sc:
        qr = Sc.ring("q", 2, [128, 8, T], BF16); kr = Sc.ring("k", 2, [128, 8, 256], BF16); vr = Sc.ring("v", 2, [128, 2, 1024], BF16)
        er = Sc.ring("e", 8, [128, 512], BF16); rsr = Sc.ring("rs", 4, [128, 512], F32)
        sps = Sc.ring("sps", 4, [128, 512], F32, psum=True); sump = Sc.ring("sump", 2, [128, 512], F32, psum=True); ops_ = Sc.ring("ops", 2, [128, 512], F32, psum=True)
        for hd in range(4):
            q_t, k_t, v_t = qr.next(), kr.next(), vr.next()
            P.dma(lambda e, q_t=q_t, hd=hd: e.dma_start(out=q_t.t[:], in_=qTd[hd * 1024:(hd + 1) * 1024, :].rearrange("(dc p) t -> p dc t", p=128)), reads=[qTb], writes=[q_t.b])
            P.dma(lambda e, k_t=k_t, hd=hd: e.dma_start(out=k_t.t[:], in_=kTm[hd * 1024:(hd + 1) * 1024, :].rearrange("(dc p) m -> p dc m", p=128)), reads=[kTmb], writes=[k_t.b])
            P.dma(lambda e, v_t=v_t, hd=hd: e.dma_start(out=v_t.t[:], in_=vm[:, hd * 1024:(hd + 1) * 1024].rearrange("(mt p) d -> p mt d", p=128)), reads=[vmb], writes=[v_t.b])
            es = {}
            sms = {}
            rss = {}
            for tb in range(2):
                for mt in range(2):
                    sp_ = sps.next()
                    for dc in range(8):
                        P.op("pe", mm(sp_.t[:], k_t.t[:, dc, mt * 128:(mt + 1) * 128], q_t.t[:, dc, tb * 512:(tb + 1) * 512], dc == 0, dc == 7), reads=[k_t.b, q_t.b], writes=[sp_.b])
                    es[(tb, mt)] = (er.next(), sp_)
            for tb in range(2):
                for mt in range(2):
                    e_t, sp_ = es[(tb, mt)]
                    P.op("act", lambda e, e_t=e_t, sp_=sp_: e.activation(out=e_t.t[:], in_=sp_.t[:], func=AF.Exp, scale=XA_SCALE), reads=[sp_.b], writes=[e_t.b])
            for tb in range(2):
                sm = sump.next()
                for mt in range(2):
                    P.op("pe", mm(sm.t[:], C["c_ones_b"].t[:], es[(tb, mt)][0].t[:], mt == 0, mt == 1), reads=[es[(tb, mt)][0].b, C["c_ones_b"].b], writes=[sm.b])
                sms[tb] = sm
            for tb in range(2):
                rs = rsr.next()
                P.op("dve", lambda e, rs=rs, sm=sms[tb]: e.reciprocal(out=rs.t[:], in_=sm.t[:]), reads=[sms[tb].b], writes=[rs.b])
                rss[tb] = rs
            for tb in range(2):
                for dc in range(8):
                    op_ = ops_.next()
                    for mt in range(2):
                        P.op("pe", mm(op_.t[:], v_t.t[:, mt, dc * 128:(dc + 1) * 128], es[(tb, mt)][0].t[:], mt == 0, mt == 1), reads=[v_t.b, es[(tb, mt)][0].b], writes=[op_.b])
                    tt(P, "dve", AT.t[:, hd * 8 + dc, tb * 512:(tb + 1) * 512], op_.t[:], rss[tb].t[:], ALU.mult, [op_.b, rss[tb].b], [AT.b])

    proj_to_r1(wo[l])
    resid_norm("xa_post_g", x1, x1b, xo, xob, to_AT_gname=None)
    SA.__exit__(None, None, None)


def make_consts():
    bf = ml_dtypes.bfloat16
    i = np.arange(128)
    c = {}
    c["c_ident"] = np.eye(128, dtype=np.float32).astype(bf)
    c["c_ones_f"] = np.ones((128, 128), np.float32)
    c["c_ones_b"] = np.ones((128, 128), np.float32).astype(bf)
    c["c_triU"] = (i[:, None] > i[None, :]).astype(np.float32).astype(bf)
    c["c_triLE"] = (i[:, None] <= i[None, :]).astype(np.float32).astype(bf)
    c["c_triI"] = (i[:, None] <= i[None, :]).astype(np.float32).astype(bf)
    c["c_ntriI"] = (-(i[:, None] <= i[None, :]).astype(np.float32)).astype(bf)
    c["c_zero_b"] = np.zeros((128, 128), np.float32).astype(bf)
    m = np.zeros((128, 4, 512), np.float32)
    for blk in range(4):
        s_pos = blk * 128 + i
        m[:, blk, :] = (s_pos[:, None] < np.arange(512)[None, :])
    c["c_mask"] = m.astype(bf)
    c["c_iota"] = np.tile(np.arange(128, dtype=np.float32)[None, :], (128, 1))
    c["c_iotap"] = np.arange(128, dtype=np.float32)[:, None].copy()
    return c


def build_fused(nlayers=NL):
    _ME.clear()
    nc = bass.Bass("TRN2", target_bir_lowering=False)
    P = Prog(nc)
    io = {n: "in" for n in ALL_INPUT_NAMES}
    io["x_out"] = "out"
    DR = Dram(nc, io)
    DR.GA, DR.GB = {}, {}
    with P.scope() as Sg:
        C = load_consts(P, Sg, DR)
        xname = "x_in"
        for l in range(nlayers):
            XA = DR.get(f"XA{l}", [XA_ROWS, T], BF16)
            xa = Exchange(P, DR, XA, f"XA{l}", f"GA{l}", XA_NCH, small_rows=256)
            phase_A(P, DR, C, None, l, xname, xa)
            DR.GA[l] = xa.finish()
            XB = DR.get(f"XB{l}", [XB_ROWS, T], BF16)
            xb = Exchange(P, DR, XB, f"XB{l}", f"GB{l}", XB_NCH)
            phase_B(P, DR, C, l, xb)
            DR.GB[l] = xb.finish()
            xo = "x_out" if l == nlayers - 1 else f"x2_{l}"
            phase_C(P, DR, C, None, l, xname, xo)
            xname = xo
        P.finish()
    P.emit()
    return nc, P, DR


def ssm_host_layout(inp, hf):
    L = NL
    out = {}
    gl = np.arange(32)
    G = hf * 32 + gl
    lre = inp["ssm_lambda_re"][:, G, :]
    lim = inp["ssm_lambda_im"][:, G, :]
    ldt = np.repeat(inp["ssm_log_dt"][:, G][:, :, None], 64, axis=2)
    out["s_lre_row"] = np.ascontiguousarray(lre.reshape(L, 2048))
    out["s_lim_row"] = np.ascontiguousarray(lim.reshape(L, 2048))
    out["s_ldt_row"] = np.ascontiguousarray(ldt.reshape(L, 2048))
    for nm, a in (("s_lre_f", lre), ("s_lim_f", lim), ("s_ldt_f", ldt)):
        out[nm] = np.ascontiguousarray(a.reshape(L, 16, 128).transpose(0, 2, 1))
    Bre = np.zeros((L, 128, 4, 512), np.float32); Bim = np.zeros((L, 128, 4, 512), np.float32)
    Cre = np.zeros((L, 128, 16, 128), np.float32); Cim = np.zeros((L, 128, 16, 128), np.float32)
    for g in range(32):
        kc, g8 = g // 8, g % 8
        Bre[:, g8 * 16:(g8 + 1) * 16, kc, g8 * 64:(g8 + 1) * 64] = inp["ssm_b_re"][:, G[g]].transpose(0, 2, 1)
        Bim[:, g8 * 16:(g8 + 1) * 16, kc, g8 * 64:(g8 + 1) * 64] = inp["ssm_b_im"][:, G[g]].transpose(0, 2, 1)
        i, a = g // 2, g % 2
        Cre[:, a * 64:(a + 1) * 64, i, g8 * 16:(g8 + 1) * 16] = inp["ssm_c_re"][:, G[g]].transpose(0, 2, 1)
        Cim[:, a * 64:(a + 1) * 64, i, g8 * 16:(g8 + 1) * 16] = inp["ssm_c_im"][:, G[g]].transpose(0, 2, 1)
    out["s_Bre_blk"], out["s_Bim_blk"], out["s_Cre_blk"], out["s_Cim_blk"] = Bre, Bim, Cre, Cim
    d = inp["ssm_d"].reshape(L, 1024)[:, hf * 512:(hf + 1) * 512]
    out["s_d"] = np.ascontiguousarray(d.reshape(L, 4, 128).transpose(0, 2, 1))
    return out


CONST_NAMES = ["c_ident", "c_ones_f", "c_ones_b", "c_triU", "c_triLE", "c_triI", "c_ntriI", "c_zero_b", "c_mask", "c_iota", "c_iotap"]
ALL_INPUT_NAMES = CONST_NAMES + ["x_in", "w_in", "pre_norm_g",
                                 "s_lre_row", "s_lim_row", "s_ldt_row", "s_lre_f", "s_lim_f", "s_ldt_f", "s_Bre_blk", "s_Bim_blk", "s_Cre_blk", "s_Cim_blk", "s_d",
                                 "mem", "conv_wT", "conv_b_l", "conv_ln_g_l", "conv_ln_b_l", "branch_g_l", "glu_b_l", "halo_scale", "w_out", "xa_wq", "xa_wk", "xa_wv",
                                 "xa_wo", "ssm_glu_w", "xa_mem_g", "post_norm_g", "xa_pre_g", "xa_post_g"]


def host_params(inp, b, hf):
    p = {}
    for n in ("w_in", "pre_norm_g", "w_out", "xa_wq", "xa_wk", "xa_wv", "xa_wo", "ssm_glu_w", "xa_mem_g", "post_norm_g", "xa_pre_g", "xa_post_g"):
        p[n] = inp[n]
    p["mem"] = np.ascontiguousarray(inp["mem"][b])
    p["conv_wT"] = np.ascontiguousarray(inp["conv_w"].reshape(NL, 31, 8, 128).transpose(0, 3, 2, 1))
    for src, dst in (("conv_b", "conv_b_l"), ("conv_ln_g", "conv_ln_g_l"), ("conv_ln_b", "conv_ln_b_l"), ("ssm_glu_b", "glu_b_l")):
        p[dst] = np.ascontiguousarray(inp[src].reshape(NL, 8, 128).transpose(0, 2, 1))
    p["branch_g_l"] = np.ascontiguousarray(inp["branch_norm_g"].reshape(NL, 32, 128).transpose(0, 2, 1))
    p["halo_scale"] = np.full((128, 1), float(hf), np.float32)
    p.update(ssm_host_layout(inp, hf))
    return p


_prog = [None]


def run_fused(inp, trace=False):
    if _prog[0] is None:
        _prog[0] = build_fused()
    nc, P, DR = _prog[0]
    consts = make_consts()
    maps = []
    for c in range(8):
        b, hf = c // 2, c % 2
        d = dict(consts)
        d.update(host_params(inp, b, hf))
        d["x_in"] = np.ascontiguousarray(inp["x"][b, hf * T:(hf + 1) * T])
        maps.append({n: d[n] for n in ALL_INPUT_NAMES})
    res = run_bass_kernel_spmd(nc, maps, core_ids=list(range(8)), trace=trace)
    out = np.zeros((4, S, D), np.float32)
    for c in range(8):
        out[c // 2, (c % 2) * T:(c % 2 + 1) * T] = res.results[c]["x_out"]
    return out, res


def kernel(**inputs):
    inp = {k: np.asarray(v) for k, v in inputs.items()}
    out, _ = run_fused(inp)
    return out.astype(np.float32)
```

```python
import numpy as np
from contextlib import ExitStack
import concourse.bass as bass
import concourse.mybir as mybir
from concourse.bass_utils import run_bass_kernel_spmd

F32 = mybir.dt.float32
BF16 = mybir.dt.bfloat16
AF = mybir.ActivationFunctionType
ALU = mybir.AluOpType
AX = mybir.AxisListType

ENGS = ("pe", "act", "dve", "pool", "sp")
EPOCH = 24000
NDSEM = 6


class Buf:
    __slots__ = ("name", "last_w", "readers")

    def __init__(self, name=""):
        self.name = name
        self.last_w = None
        self.readers = []


class Op:
    __slots__ = ("eng", "fn", "deps", "is_dma", "sig", "tok", "n")

    def __init__(self, eng, fn, is_dma):
        self.eng = eng
        self.fn = fn
        self.is_dma = is_dma
        self.deps = set()
        self.sig = False
        self.tok = None
        self.n = -1


class Prog:
    def __init__(self, nc):
        self.nc = nc
        self.ops = []
        self.stack = ExitStack()
        self.last = {e: None for e in ENGS}
        self.barrier_deps = {e: [] for e in ENGS}
        self.dma_slots = {e: [None] * NDSEM for e in ENGS}
        self.dma_cnt = {e: 0 for e in ENGS}
        self.outstanding_dma = []
        self.dma_slots_nb = {}
        self._n = 0

    def sbuf(self, name, shape, dtype, st=None):
        t = (st or self.stack).enter_context(self.nc.sbuf_tensor(name, list(shape), dtype))
        return t

    def psum(self, name, shape, dtype, st=None):
        t = (st or self.stack).enter_context(self.nc.psum_tensor(name, list(shape), dtype))
        return t

    def _add(self, op, reads, writes):
        op.n = self._n
        self._n += 1
        deps = op.deps
        for r in reads:
            if r.last_w is not None:
                deps.add(r.last_w)
        for w in writes:
            if w.last_w is not None:
                deps.add(w.last_w)
            for rd in w.readers:
                deps.add(rd)
        for r in reads:
            r.readers.append(op)
            if len(r.readers) > 64:
                keep = {}
                dm = []
                for o in r.readers:
                    if o.is_dma:
                        dm.append(o)
                    else:
                        keep[o.eng] = o
                r.readers = dm + list(keep.values())
        for w in writes:
            w.last_w = op
            w.readers = []
        for d in self.barrier_deps[op.eng]:
            deps.add(d)
        self.barrier_deps[op.eng] = []
        deps.discard(op)
        self.ops.append(op)
        self.last[op.eng] = op
        return op

    def op(self, eng, fn, reads=(), writes=()):
        return self._add(Op(eng, fn, False), reads, writes)

    def dma(self, fn, reads=(), writes=(), q="sp", nobar=False):
        op = Op(q, fn, True)
        if nobar:
            qk = q + "_nb"
            if qk not in self.dma_cnt:
                self.dma_cnt[qk] = 0
                self.dma_slots_nb[qk] = [None] * NDSEM
            i = self.dma_cnt[qk]
            self.dma_cnt[qk] = i + 1
            slot = i % NDSEM
            prev = self.dma_slots_nb[qk][slot]
            if prev is not None:
                op.deps.add(prev)
            self.dma_slots_nb[qk][slot] = op
            op.tok = ("d", qk, slot, 16 * (i // NDSEM + 1))
            return self._add(op, reads, writes)
        i = self.dma_cnt[q]
        self.dma_cnt[q] = i + 1
        slot = i % NDSEM
        prev = self.dma_slots[q][slot]
        if prev is not None:
            op.deps.add(prev)
        self.dma_slots[q][slot] = op
        op.tok = ("d", q, slot, 16 * (i // NDSEM + 1))
        self.outstanding_dma.append(op)
        if len(self.outstanding_dma) > 4 * NDSEM * 3:
            self.outstanding_dma = self.outstanding_dma[-(NDSEM * 3):]
        return self._add(op, reads, writes)

    def cc(self, fn, reads=(), writes=()):
        op = Op("pool", fn, True)
        self.n_cc = getattr(self, "n_cc", 0) + 1
        op.tok = ("x", "cc", self.n_cc, 1)
        self.cc_ops = getattr(self, "cc_ops", []) + [op]
        return self._add(op, reads, writes)

    def barrier(self):
        deps = [o for o in self.last.values() if o is not None]
        for q in ENGS:
            for o in self.dma_slots[q]:
                if o is not None:
                    deps.append(o)
        for e in ENGS:
            self.barrier_deps[e] = list(deps)

    def emit(self):
        nc = self.nc
        for op in self.ops:
            for d in op.deps:
                if d.is_dma:
                    continue
                if d.eng == "pe" and op.eng == "pe" and not op.is_dma:
                    continue
                d.sig = True
        cnt = {e: 0 for e in ENGS}
        for op in self.ops:
            if op.is_dma:
                continue
            if op.sig:
                cnt[op.eng] += 1
                c = cnt[op.eng]
                op.tok = ("c", op.eng, (c - 1) // EPOCH, (c - 1) % EPOCH + 1)
        nep = {e: (cnt[e] + EPOCH - 1) // EPOCH for e in ENGS}
        st = self.stack
        csem = {}
        for e in ENGS:
            for k in range(max(nep[e], 0)):
                csem[(e, k)] = st.enter_context(nc.semaphore(f"c_{e}_{k}"))
        dsem = {}
        for q in list(self.dma_cnt.keys()):
            if self.dma_cnt[q]:
                for s in range(NDSEM):
                    dsem[(q, s)] = st.enter_context(nc.semaphore(f"d_{q}_{s}"))
        for k in range(1, getattr(self, "n_cc", 0) + 1):
            dsem[("cc", k)] = st.enter_context(nc.semaphore(f"x_cc_{k}"))
        self.n_sems = len(csem) + len(dsem)

        def semof(tok):
            if tok[0] == "c":
                return csem[(tok[1], tok[2])], tok[3]
            return dsem[(tok[1], tok[2])], tok[3]

        per_eng = {e: [] for e in ENGS}
        for op in self.ops:
            per_eng[op.eng].append(op)

        def run(eng_name, eng):
            waited = {}
            for op in per_eng[eng_name]:
                need = {}
                for d in op.deps:
                    if d.tok is None:
                        continue
                    if (not d.is_dma) and d.eng == "pe" and eng_name == "pe" and not op.is_dma:
                        continue
                    key = d.tok[:3]
                    v = d.tok[3]
                    if d.tok[0] == "c":
                        ek = ("c", d.tok[1])
                        cur = need.get(ek)
                        cand = (d.tok[2], v)
                        if cur is None or cand > cur:
                            need[ek] = cand
                    else:
                        cur = need.get(key)
                        if cur is None or v > cur:
                            need[key] = v
                for k, v in need.items():
                    if k[0] == "c":
                        w = waited.get(k)
                        if w is not None and w >= v:
                            continue
                        waited[k] = v
                        eng.wait_ge(csem[(k[1], v[0])], v[1])
                    else:
                        w = waited.get(k)
                        if w is not None and w >= v:
                            continue
                        waited[k] = v
                        eng.wait_ge(dsem[(k[1], k[2])], v)
                ins = op.fn(eng)
                if op.is_dma and op.tok[0] == "x":
                    s, _ = semof(op.tok)
                    ins.then_inc(s)
                elif op.is_dma:
                    s, _ = semof(op.tok)
                    ins.then_inc(s, 16)
                elif op.sig:
                    s, _ = semof(op.tok)
                    ins.then_inc(s, 1)

        with nc.Block() as block:
            @block.tensor
            def _(e):
                run("pe", e)

            @block.scalar
            def _(e):
                run("act", e)

            @block.vector
            def _(e):
                run("dve", e)

            @block.gpsimd
            def _(e):
                run("pool", e)

            @block.sync
            def _(e):
                run("sp", e)

    def scope(self):
        return Scope(self)

    def finish(self, eng="sp"):
        self.barrier()
        extra = list(getattr(self, "cc_ops", []))
        for sl in self.dma_slots_nb.values():
            extra.extend(o for o in sl if o is not None)
        self.barrier_deps[eng] = self.barrier_deps[eng] + extra
        self.op(eng, lambda e: e.nop() if hasattr(e, "nop") else e.engine_nop())


class Tile:
    __slots__ = ("t", "b")

    def __init__(self, t, name=""):
        self.t = t
        self.b = Buf(name)


class Ring:
    def __init__(self, tiles):
        self.tiles = tiles
        self.i = 0

    def next(self):
        t = self.tiles[self.i % len(self.tiles)]
        self.i += 1
        return t


class Scope:
    _uid = 0

    def __init__(self, P):
        self.P = P
        self.st = ExitStack()

    def __enter__(self):
        self.st.__enter__()
        return self

    def __exit__(self, *a):
        self.P.barrier()
        return self.st.__exit__(*a)

    def _nm(self, name):
        Scope._uid += 1
        return f"{name}_{Scope._uid}"

    def sbuf(self, name, shape, dtype):
        return Tile(self.P.sbuf(self._nm(name), shape, dtype, st=self.st), name)

    def psum(self, name, shape, dtype):
        return Tile(self.P.psum(self._nm(name), shape, dtype, st=self.st), name)

    def ring(self, name, n, shape, dtype, psum=False):
        f = self.psum if psum else self.sbuf
        return Ring([f(f"{name}{i}", shape, dtype) for i in range(n)])


import math
import numpy as np
import ml_dtypes

I32 = mybir.dt.int32
D = 4096
T = 1024
S = 2048
NL = 2
INW = 13312
EPS = 1e-6
NB = 512
XA_NCH = 7
XA_ROWS = XA_NCH * 1024 + 256
XB_NCH = 3
XB_ROWS = XB_NCH * 1024
PAIRS = [[0, 1], [2, 3], [4, 5], [6, 7]]


def xrow(s, ro):
    return (ro // 512) * 1024 + s * 512 + (ro % 512)


_ME = {}


def me_idx(e):
    key = id(e)
    if key not in _ME:
        _ME[key] = e.snap(e.partition_id() % 2)
    return _ME[key]


def gsrc(e, Glist, r, ro, n, cols=slice(None)):
    k, w0 = ro // 512, ro % 512
    return Glist[k][r * 512 + w0:r * 512 + w0 + n, cols]


class Exchange:
    def __init__(self, P, DR, src, srcname, dstname, nch, small_rows=0):
        self.P, self.DR, self.src, self.srcname, self.dstname = P, DR, src, srcname, dstname
        self.n = nch + (1 if small_rows else 0)
        self.rows = [1024 if k < nch else small_rows for k in range(self.n)]
        self.g = [DR.get(f"{dstname}g_{k}", [2 * self.rows[k], T], BF16) for k in range(self.n)]
        self.sel = [DR.get(f"{dstname}_{k}", [self.rows[k], T], BF16) for k in range(self.n)]
        self.issued = set()

    def cc(self, k):
        if k in self.issued:
            return
        self.issued.add(k)
        P, DR = self.P, self.DR
        sl = self.src[k * 1024:k * 1024 + self.rows[k], :]
        g = self.g[k]
        P.cc(lambda e: e.collective_compute("AllGather", ALU.bypass, replica_groups=PAIRS, ins=[sl.opt()], outs=[g.opt()]),
             reads=[DR.chunkbuf(self.srcname, k)], writes=[DR.buf(f"{self.dstname}g_{k}")])

    def finish(self):
        P, DR = self.P, self.DR
        for k in range(self.n):
            self.cc(k)
        for k in range(self.n):
            g, sel = self.g[k], self.sel[k]
            P.dma(lambda e, g=g, sel=sel: e.dma_start(out=sel.rearrange("(r s w) c -> r s w c", r=2, s=1),
                                                      in_=g.rearrange("(r s w) c -> r s w c", r=2, s=2)[:, bass.ds(me_idx(e), 1), :, :]),
                  reads=[DR.buf(f"{self.dstname}g_{k}")], writes=[DR.buf(f"{self.dstname}_{k}")], q="pool", nobar=True)
        return self.sel


SB_SCALE = 1.0 / math.sqrt(128.0)
XA_SCALE = 1.0 / math.sqrt(1024.0)
TWO_PI = 2.0 * math.pi


class Dram:
    def __init__(self, nc, io):
        self.nc = nc
        self.io = io
        self.t = {}
        self.b = {}

    def get(self, name, shape=None, dtype=None):
        if name not in self.t:
            kind = {"in": "ExternalInput", "out": "ExternalOutput"}.get(self.io.get(name), "Internal")
            self.t[name] = self.nc.dram_tensor(name, list(shape), dtype, kind=kind).ap()
            self.b[name] = Buf(name)
        return self.t[name]

    def buf(self, name):
        return self.b[name]

    def chunkbuf(self, name, k):
        key = f"{name}#{k}"
        if key not in self.b:
            self.b[key] = Buf(key)
        return self.b[key]


def mm(ps, lhsT, rhs, start, stop):
    return lambda e: e.matmul(ps, lhsT=lhsT, rhs=rhs, start=start, stop=stop)


def load_consts(P, Sg, DR):
    C = {}
    def ld(name, shape, dtype):
        t = Sg.sbuf(name, shape, dtype)
        src = DR.get(name, shape, dtype)
        P.dma(lambda e: e.dma_start(out=t.t[:], in_=src), writes=[t.b])
        C[name] = t
    ld("c_ident", [128, 128], BF16)
    ld("c_ones_f", [128, 128], F32)
    ld("c_ones_b", [128, 128], BF16)
    ld("c_triU", [128, 128], BF16)
    ld("c_triLE", [128, 128], BF16)
    ld("c_triI", [128, 128], BF16)
    ld("c_ntriI", [128, 128], BF16)
    ld("c_zero_b", [128, 128], BF16)
    ld("c_mask", [128, 4, 512], BF16)
    ld("c_iota", [128, 128], F32)
    ld("c_iotap", [128, 1], F32)
    return C


def rms_rstd(P, ss, rstd, n, reads_extra=()):
    P.op("act", lambda e: e.activation(out=rstd.t[:], in_=ss.t[:], func=AF.Ln, scale=1.0 / n, bias=EPS),
         reads=[ss.b], writes=[rstd.b])
    P.op("act", lambda e: e.activation(out=rstd.t[:], in_=rstd.t[:], func=AF.Exp, scale=-0.5),
         reads=[rstd.b], writes=[rstd.b])


EPS_AP = [None]


def norm_tile_to_T(P, Sc, C, x_t, g_rep, hT, tt, ps_ring, h_ring, ss_ring, nT):
    ss = ss_ring.next()
    rstd = ss_ring.next()
    hb = h_ring.next()
    P.op("act", lambda e: e.activation(out=hb.t[:], in_=x_t.t[:], func=AF.Square, accum_out=ss.t[:]),
         reads=[x_t.b], writes=[hb.b, ss.b])
    rms_rstd(P, ss, rstd, D)
    P.op("dve", lambda e: e.scalar_tensor_tensor(out=hb.t[:], in0=x_t.t[:], scalar=rstd.t[:], in1=g_rep.t[:],
                                                 op0=ALU.mult, op1=ALU.mult),
         reads=[x_t.b, rstd.b, g_rep.b], writes=[hb.b])
    for g4 in range(4):
        ps = ps_ring.next()
        for j in range(8):
            kc = g4 * 8 + j
            P.op("pe", lambda e, ps=ps, j=j, kc=kc: e.transpose(ps.t[:, j * 128:(j + 1) * 128], hb.t[:, kc * 128:(kc + 1) * 128], C["c_ident"].t[:]),
                 reads=[hb.b, C["c_ident"].b], writes=[ps.b])
        eng = "act" if g4 % 2 == 0 else "dve"
        dst = hT.t[:, g4 * 8:(g4 + 1) * 8, tt * 128:(tt + 1) * 128]
        src = ps.t[:].rearrange("p (j t) -> p j t", j=8)
        if eng == "act":
            P.op("act", lambda e, dst=dst, src=src: e.copy(out=dst, in_=src), reads=[ps.b], writes=[hT.b])
        else:
            P.op("dve", lambda e, dst=dst, src=src: e.tensor_copy(out=dst, in_=src), reads=[ps.b], writes=[hT.b])


PENDING_CC = []


def w_stream(P, wring, W2d, n0, n1, nk=32):
    wv = W2d.rearrange("(kc p) n -> p kc n", p=128)
    for nb in range(n0, n1, NB):
        for pc in list(PENDING_CC):
            pc[0] -= 1
            if pc[0] <= 0:
                pc[1]()
                PENDING_CC.remove(pc)
        wt = wring.next()
        step = 8
        for j in range(0, nk, step):
            P.dma(lambda e, wt=wt, j=j, nb=nb: e.dma_start(out=wt.t[:, j:j + step, :], in_=wv[:, j:j + step, nb:nb + NB]),
                  writes=[wt.b], q="pool")
        yield nb, wt


def evac(P, i, dst, src, reads, writes):
    if i % 2 == 0:
        P.op("act", lambda e: e.copy(out=dst, in_=src), reads=reads, writes=writes)
    else:
        P.op("dve", lambda e: e.tensor_copy(out=dst, in_=src), reads=reads, writes=writes)


def proj_feat(P, wt, actT, nT, nk, ps_ring, sink, cnt=[0]):
    for j in range(NB // 128):
        for tb in range(nT // 512):
            ps = ps_ring.next()
            for kc in range(nk):
                P.op("pe", mm(ps.t[:], wt.t[:, kc, j * 128:(j + 1) * 128], actT.t[:, kc, tb * 512:(tb + 1) * 512], kc == 0, kc == nk - 1),
                     reads=[wt.b, actT.b], writes=[ps.b])
            sink(j, tb, ps)


def proj_tok(P, wt, actT, nT, nk, ps_ring, sink, ncols=NB):
    for tt in range(nT // 128):
        ps = ps_ring.next()
        for kc in range(nk):
            P.op("pe", mm(ps.t[:, 0:ncols], actT.t[:, kc, tt * 128:(tt + 1) * 128], wt.t[:, kc, 0:ncols], kc == 0, kc == nk - 1),
                 reads=[wt.b, actT.b], writes=[ps.b])
        sink(tt, ps)


def phase_A(P, DR, C, AT, l, xname, xch):
    nc = P.nc
    xin = DR.get(xname, [T, D], F32)
    XA = DR.get(f"XA{l}", [XA_ROWS, T], BF16)
    aT = DR.get(f"aT{l}", [3072, T], BF16)
    bgT = DR.get(f"bgT{l}", [2048, T], BF16)
    cgT = DR.get(f"cgT{l}", [1024, T], BF16)
    w_in = DR.get("w_in", [NL, D, INW], F32)
    g_d = DR.get("pre_norm_g", [NL, D], F32)
    mem = DR.get("mem", [256, D], F32)
    kTm = DR.get("kTm", [D, 256], BF16); kTmb = DR.buf("kTm")
    vm = DR.get("vm", [256, D], BF16); vmb = DR.buf("vm")
    wk = DR.get("xa_wk", [NL, D, D], F32); wv = DR.get("xa_wv", [NL, D, D], F32)
    SA = P.scope()
    SA.__enter__()
    AT = SA.sbuf("AT", [128, 32, T], BF16)
    mT = SA.sbuf("mT", [128, 32, 256], BF16)
    with P.scope() as Sc:
        g_rep = Sc.sbuf("g_rep", [128, D], F32)
        P.dma(lambda e: e.dma_start(out=g_rep.t[:], in_=g_d[l:l + 1, :].partition_broadcast(128)), writes=[g_rep.b])
        gm_rep = grep_load(P, Sc, DR, "xa_mem_g", l)
        xr = Sc.ring("x", 3, [128, D], F32)
        hr = Sc.ring("hb", 3, [128, D], BF16)
        ssr = Sc.ring("ss", 8, [128, 1], F32)
        pst = Sc.ring("pst", 2, [128, 1024], BF16, psum=True)
        for mt in range(2):
            xt = xr.next()
            P.dma(lambda e, xt=xt, mt=mt: e.dma_start(out=xt.t[:], in_=mem[mt * 128:(mt + 1) * 128, :]), writes=[xt.b])
            norm_tile_to_T(P, Sc, C, xt, gm_rep, mT, mt, pst, hr, ssr, 256)
        for tt in range(T // 128):
            xt = xr.next()
            P.dma(lambda e, xt=xt, tt=tt: e.dma_start(out=xt.t[:], in_=xin[tt * 128:(tt + 1) * 128, :]),
                  reads=[DR.buf(xname)], writes=[xt.b])
            norm_tile_to_T(P, Sc, C, xt, g_rep, AT, tt, pst, hr, ssr, T)
    with P.scope() as Sc:
        wring = Sc.ring("w", 3, [128, 32, NB], BF16)
        psr = Sc.ring("ps", 6, [128, 512], F32, psum=True)
        stg = Sc.ring("stg", 4, [128, 512], BF16)
        ev = [0]
        segs = [(0, 3072, aT, None, 0, "feat"),
                (3072, 4096, XA, 0, 0, "feat"), (4096, 5120, XA, 1, 0, "feat"),
                (5120, 6144, XA, 0, 1024, "feat"), (6144, 7168, XA, 1, 1024, "feat"),
                (7168, 8192, XA, 0, 2048, "tok"), (8192, 9216, XA, 1, 2048, "tok"),
                (9216, 11264, bgT, None, 0, "feat"),
                (11264, 11776, XA, 0, 3072, "feat"), (11776, 12288, XA, 1, 3072, "feat"),
                (12288, 13312, cgT, None, 0, "feat")]
        names = {id(aT): f"aT{l}", id(XA): f"XA{l}", id(bgT): f"bgT{l}", id(cgT): f"cgT{l}"}
        def kv_tiles():
            for nb, wt in w_stream(P, wring, wk[l], 0, D):
                for j in range(4):
                    ps = psr.next()
                    for kc in range(32):
                        P.op("pe", mm(ps.t[:, 0:256], wt.t[:, kc, j * 128:(j + 1) * 128], mT.t[:, kc, :], kc == 0, kc == 31), reads=[wt.b, mT.b], writes=[ps.b])
                    st = stg.next(); ev[0] += 1
                    evac(P, ev[0], st.t[:, 0:256], ps.t[:, 0:256], [ps.b], [st.b])
                    row = nb + j * 128
                    P.dma(lambda e, st=st, row=row: e.dma_start(out=kTm[row:row + 128, :], in_=st.t[:, 0:256]), reads=[st.b], writes=[kTmb])
                yield
            for nb, wt in w_stream(P, wring, wv[l], 0, D):
                def sink(tt_, ps, nb=nb):
                    st = stg.next(); ev[0] += 1
                    evac(P, ev[0], st.t[:], ps.t[:], [ps.b], [st.b])
                    P.dma(lambda e: e.dma_start(out=vm[tt_ * 128:(tt_ + 1) * 128, nb:nb + NB], in_=st.t[:]), reads=[st.b], writes=[vmb])
                proj_tok(P, wt, mT, 256, 32, psr, sink)
                yield
        kvg = kv_tiles()
        ntile = [0]
        halo_done = [False]

        def write_halo():
            hs = Sc.sbuf("halo", [128, 16, 32], BF16)
            P.dma(lambda e: e.dma_start(out=hs.t[:], in_=aT[0:2048, T - 32:T].rearrange("(c p) t -> p c t", p=128)),
                  reads=[DR.buf(f"aT{l}")], writes=[hs.b])
            for sh_ in range(2):
                base = XA_NCH * 1024 + sh_ * 128
                P.dma(lambda e, base=base: e.dma_start(out=XA[base:base + 128, 0:512].rearrange("p (c t) -> p c t", t=32), in_=hs.t[:]),
                      reads=[hs.b], writes=[DR.chunkbuf(f"XA{l}", XA_NCH)])
            PENDING_CC.append([5, (lambda: xch.cc(XA_NCH))])

        for si_, (c0, c1, dst, sh, r0, mode) in enumerate(segs):
            if si_ == 1:
                write_halo()
            if sh == 0 and si_ > 1 or (sh is None and si_ > 1):
                pass
            for nb, wt in w_stream(P, wring, w_in[l], c0, c1):
                if mode == "feat":
                    def sink(j, tb, ps, nb=nb, dst=dst, r0=r0, c0=c0, sh=sh):
                        st = stg.next()
                        ev[0] += 1
                        evac(P, ev[0], st.t[:], ps.t[:], [ps.b], [st.b])
                        ro = r0 + (nb - c0) + j * 128
                        if sh is None:
                            row, dbuf = ro, DR.buf(names[id(dst)])
                        else:
                            row, dbuf = xrow(sh, ro), DR.chunkbuf(f"XA{l}", ro // 512)
                        P.dma(lambda e: e.dma_start(out=dst[row:row + 128, tb * 512:(tb + 1) * 512], in_=st.t[:]),
                              reads=[st.b], writes=[dbuf])
                    proj_feat(P, wt, AT, T, 32, psr, sink)
                else:
                    def sink(tt, ps, nb=nb, c0=c0, r0=r0, sh=sh, dst=dst):
                        st = stg.next()
                        ev[0] += 1
                        evac(P, ev[0], st.t[:], ps.t[:], [ps.b], [st.b])
                        col = nb - c0
                        ro = r0 + tt * 128
                        row = xrow(sh, ro)
                        P.dma(lambda e: e.dma_start(out=dst[row:row + 128, col:col + 512], in_=st.t[:]),
                              reads=[st.b], writes=[DR.chunkbuf(f"XA{l}", ro // 512)])
                    proj_tok(P, wt, AT, T, 32, psr, sink)
                ntile[0] += 1
                if ntile[0] % 5 != 0:
                    next(kvg, None)
            if sh == 1:
                for k_ in range(r0 // 512, (r0 + (c1 - c0) + 511) // 512):
                    PENDING_CC.append([5, (lambda k_=k_: xch.cc(k_))])
        for _ in kvg:
            pass
        for pc in list(PENDING_CC):
            pc[1]()
        PENDING_CC.clear()
    SA.__exit__(None, None, None)


def attn_item(P, C, qT, kT, nkT, vt, qb, zl, Ob, Sbufs, wk, h, XB, xbbuf, stg, ev):
    nkb = 4 * qb + 4
    first = True
    mask = C["c_mask"]
    Sx = Sbufs[0]
    P.op("pool", lambda e: e.memset(Sx.t[:], 0.0), writes=[Sx.b])
    P.op("pe", mm(Ob.t[:], C["c_zero_b"].t[:], mask.t[:, 0, :], True, False), reads=[C["c_zero_b"].b, mask.b], writes=[Ob.b])
    for kb in range(nkb - 1, -1, -1):
        diag = kb >= 4 * qb
        mi = kb - 4 * qb
        c0 = mi * 128 if diag else 0
        cs = slice(c0, 512)
        dsl = slice(c0, c0 + 128)
        Z = zl.next()
        qsl = slice(qb * 512 + c0, (qb + 1) * 512)
        P.op("pe", mm(Z.t[:, cs], kT.t[:, kb * 128:(kb + 1) * 128], qT.t[:, qsl], True, True),
             reads=[kT.b, qT.b], writes=[Z.b])
        yield
        Ft, SP, Wt = wk["F"].next(), wk["SP"].next(), wk["W"].next()
        E2 = wk["E2"].next()
        P.op("act", lambda e, E2=E2, Z=Z, cs=cs: e.activation(out=E2.t[:, cs], in_=Z.t[:, cs], func=AF.Exp, scale=SB_SCALE),
             reads=[Z.b], writes=[E2.b])
        yield
        P.op("act", lambda e, Ft=Ft, E2=E2, cs=cs: e.activation(out=Ft.t[:, cs], in_=E2.t[:, cs], func=AF.Ln, bias=1.0),
             reads=[E2.b], writes=[Ft.b])
        yield
        P.op("act", lambda e, SP=SP, Ft=Ft, cs=cs: e.copy(out=SP.t[:, cs], in_=Ft.t[:, cs]), reads=[Ft.b], writes=[SP.b])
        if diag:
            P.op("pool", lambda e, SP=SP, mi=mi, dsl=dsl: e.tensor_tensor(out=SP.t[:, dsl], in0=SP.t[:, dsl], in1=mask.t[:, mi, dsl], op=ALU.mult),
                 reads=[SP.b, mask.b], writes=[SP.b])
        yield
        Lt = zl.next()
        P.op("pe", mm(Lt.t[:, cs], nkT.t[:, kb * 128:(kb + 1) * 128], qT.t[:, qsl], True, False),
             reads=[nkT.b, qT.b], writes=[Lt.b])
        P.op("pe", mm(Lt.t[:, cs], C["c_triU"].t[:], SP.t[:, cs], False, first), reads=[SP.b, C["c_triU"].b], writes=[Lt.b])
        if not first:
            P.op("pe", mm(Lt.t[:, cs], C["c_ones_b"].t[:], Sx.t[:, cs], False, True), reads=[Sx.b, C["c_ones_b"].b], writes=[Lt.b])
        yield
        P.op("dve", lambda e, Ft=Ft, Lt=Lt, cs=cs: e.tensor_tensor(out=Ft.t[:, cs], in0=Lt.t[:, cs], in1=Ft.t[:, cs], op=ALU.add),
             reads=[Lt.b, Ft.b], writes=[Ft.b])
        yield
        P.op("act", lambda e, Wt=Wt, Ft=Ft, cs=cs: e.activation(out=Wt.t[:, cs], in_=Ft.t[:, cs], func=AF.Exp, scale=-1.0),
             reads=[Ft.b], writes=[Wt.b])
        if diag:
            P.op("pool", lambda e, Wt=Wt, mi=mi, dsl=dsl: e.tensor_tensor(out=Wt.t[:, dsl], in0=Wt.t[:, dsl], in1=mask.t[:, mi, dsl], op=ALU.mult),
                 reads=[Wt.b, mask.b], writes=[Wt.b])
        yield
        last = kb == 0
        if diag:
            P.op("pe", mm(Ob.t[:, dsl], vt.t[:, kb, :], Wt.t[:, dsl], False, last), reads=[vt.b, Wt.b], writes=[Ob.b])
            if c0 + 128 < 512:
                osl = slice(c0 + 128, 512)
                P.op("pe", mm(Ob.t[:, osl], vt.t[:, kb, :], Wt.t[:, osl], False, last), reads=[vt.b, Wt.b], writes=[Ob.b])
        else:
            P.op("pe", mm(Ob.t[:], vt.t[:, kb, :], Wt.t[:], False, last), reads=[vt.b, Wt.b], writes=[Ob.b])
        if kb > 0:
            P.op("pool", lambda e, SP=SP, cs=cs: e.tensor_tensor(out=Sx.t[:, cs], in0=Sx.t[:, cs], in1=SP.t[:, cs], op=ALU.add),
                 reads=[SP.b, Sx.b], writes=[Sx.b])
        first = False
        yield
    st = stg.next()
    ev[0] += 1
    evac(P, ev[0], st.t[:], Ob.t[:], [Ob.b], [st.b])
    r = qb // 2
    row = xrow(r, h * 128)
    col = (qb % 2) * 512
    P.dma(lambda e: e.dma_start(out=XB[row:row + 128, col:col + 512], in_=st.t[:]), reads=[st.b], writes=[xbbuf[(h * 128) // 512]])
    yield


NSLOT = 3


def attn_gen(P, DR, C, l, Sc, xch=None):
    GA = DR.GA[l]
    XB = DR.get(f"XB{l}", [XB_ROWS, T], BF16)
    qr = Sc.ring("qT", 2, [128, S], BF16)
    kr = Sc.ring("kT", 2, [128, S], BF16)
    nkr = Sc.ring("nkT", 2, [128, S], BF16)
    vr = Sc.ring("v", 2, [128, 16, 128], BF16)
    wk = {"F": Sc.ring("F", 6, [128, 512], F32), "SP": Sc.ring("SP", 6, [128, 512], BF16), "W": Sc.ring("W", 6, [128, 512], BF16),
          "E2": Sc.ring("E2", 4, [128, 512], F32)}
    Sb = [[Sc.sbuf(f"S{i}_{j}", [128, 512], BF16) for j in range(2)] for i in range(NSLOT)]
    stg = Sc.ring("stg", 3, [128, 512], BF16)
    zl = Sc.ring("ZL", 3, [128, 512], F32, psum=True)
    Obs = [Sc.psum(f"O{i}", [128, 512], F32) for i in range(NSLOT)]
    ev = [0]
    xbb = [DR.chunkbuf(f"XB{l}", k) for k in range(3)]

    def items():
        for h in range(8):
            qT, kT, vt, nkT = qr.next(), kr.next(), vr.next(), nkr.next()
            for r in range(2):
                P.dma(lambda e, qT=qT, r=r, h=h: e.dma_start(out=qT.t[:, r * 1024:(r + 1) * 1024], in_=gsrc(e, GA, r, h * 128, 128)),
                      reads=[DR.buf(f"GA{l}_{(h * 128) // 512}")], writes=[qT.b])
                P.dma(lambda e, kT=kT, r=r, h=h: e.dma_start(out=kT.t[:, r * 1024:(r + 1) * 1024], in_=gsrc(e, GA, r, 1024 + h * 128, 128)),
                      reads=[DR.buf(f"GA{l}_{(1024 + h * 128) // 512}")], writes=[kT.b])
                for vb in range(2):
                    P.dma(lambda e, vt=vt, r=r, vb=vb, h=h: e.dma_start(
                        out=vt.t[:, r * 8 + vb * 4:r * 8 + (vb + 1) * 4, :],
                        in_=gsrc(e, GA, r, 2048 + vb * 512, 512, slice(h * 128, (h + 1) * 128)).rearrange("(kt p) d -> p kt d", p=128)),
                        reads=[DR.buf(f"GA{l}_{4 + vb}")], writes=[vt.b])
            P.op("pool", lambda e, nkT=nkT, kT=kT: e.tensor_scalar(out=nkT.t[:], in0=kT.t[:], scalar1=-SB_SCALE, scalar2=None, op0=ALU.mult),
                 reads=[kT.b], writes=[nkT.b])
            for qb in (3, 2, 1, 0):
                yield (lambda sl, qT=qT, kT=kT, nkT=nkT, vt=vt, qb=qb, h=h:
                       attn_item(P, C, qT, kT, nkT, vt, qb, zl, Obs[sl], Sb[sl], wk, h, XB, xbb, stg, ev)), h

    it = items()
    slots = [None] * NSLOT
    slot_head = [None] * NSLOT
    done = False
    left = {h: 4 for h in range(8)}
    while True:
        active = False
        for sl in range(NSLOT):
            if slots[sl] is None and not done:
                nxt = next(it, None)
                if nxt is None:
                    done = True
                else:
                    slots[sl] = nxt[0](sl)
                    slot_head[sl] = nxt[1]
            if slots[sl] is not None:
                active = True
                try:
                    next(slots[sl])
                except StopIteration:
                    slots[sl] = None
                    left[slot_head[sl]] -= 1
                    if xch is not None and all(left[h_] == 0 for h_ in range(4)):
                        xch.cc(0)
        if not active and done:
            break
        yield


def sincos(P, Sc, arg, N, tag):
    outs = []
    for which, shift in (("s", 0.0), ("c", math.pi / 2)):
        ki = Sc.sbuf(f"ki_{tag}{which}", [128, N], I32)
        kf = Sc.sbuf(f"kf_{tag}{which}", [128, N], F32)
        r = Sc.sbuf(f"r_{tag}{which}", [128, N], F32)
        o = Sc.sbuf(f"o_{tag}{which}", [128, N], F32)
        P.op("dve", lambda e, ki=ki, shift=shift: e.tensor_scalar(out=ki.t[:], in0=arg.t[:], scalar1=shift, scalar2=1.0 / TWO_PI,
                                                                  op0=ALU.add, op1=ALU.mult), reads=[arg.b], writes=[ki.b])
        P.op("dve", lambda e, ki=ki, kf=kf: e.tensor_copy(out=kf.t[:], in_=ki.t[:]), reads=[ki.b], writes=[kf.b])
        P.op("dve", lambda e, kf=kf, r=r: e.scalar_tensor_tensor(out=r.t[:], in0=kf.t[:], scalar=-TWO_PI, in1=arg.t[:],
                                                                op0=ALU.mult, op1=ALU.add), reads=[kf.b, arg.b], writes=[r.b])
        P.op("dve", lambda e, r=r, shift=shift: e.tensor_scalar(out=r.t[:], in0=r.t[:], scalar1=3.14159 - shift, scalar2=-3.14159 - shift,
                                                               op0=ALU.min, op1=ALU.max), reads=[r.b], writes=[r.b])
        sh_t = SHIFT_AP[which]
        P.op("act", lambda e, r=r, o=o, sh_t=sh_t: e.activation(out=o.t[:], in_=r.t[:], func=AF.Sin, bias=sh_t.t[:]),
             reads=[r.b, sh_t.b], writes=[o.b])
        outs.append(o)
    return outs


SHIFT_AP = {}
DBG = {}


def tt(P, eng, out, in0, in1, op, reads, writes):
    P.op(eng, lambda e: e.tensor_tensor(out=out, in0=in0, in1=in1, op=op), reads=reads, writes=writes)


def ssm_tables(P, DR, C, l, S0):
    NS = 2048
    d_rows = {n: DR.get(n, [NL, NS], F32) for n in ("s_lre_row", "s_lim_row", "s_ldt_row")}
    d_f = {n: DR.get(n, [NL, 128, 16], F32) for n in ("s_lre_f", "s_lim_f", "s_ldt_f")}
    d_B = {n: DR.get(n, [NL, 128, 4, 512], F32) for n in ("s_Bre_blk", "s_Bim_blk")}
    d_C = {n: DR.get(n, [NL, 128, 16, 128], F32) for n in ("s_Cre_blk", "s_Cim_blk")}
    d_d = DR.get("s_d", [NL, 128, 4], F32)
    if True:
        Tm_re = S0.sbuf("Tm_re", [128, NS], F32); nTm_im = S0.sbuf("nTm_im", [128, NS], F32)
        Tp_re = S0.sbuf("Tp_re", [128, 16, 128], F32); nTp_im = S0.sbuf("nTp_im", [128, 16, 128], F32)
        BDre = S0.sbuf("BDre", [128, 4, 512], BF16); BDim = S0.sbuf("BDim", [128, 4, 512], BF16)
        CBre = S0.sbuf("CBre", [128, 16, 128], BF16); CBim = S0.sbuf("CBim", [128, 16, 128], BF16); nCBim = S0.sbuf("nCBim", [128, 16, 128], BF16)
        L_re = S0.sbuf("L_re", [128, 16], F32); L_im = S0.sbuf("L_im", [128, 16], F32)
        dcol = S0.sbuf("dcol", [128, 4], F32)
        for which, val in (("s", 0.0), ("c", math.pi / 2)):
            t = S0.sbuf(f"shift_{which}", [128, 1], F32)
            P.op("dve", lambda e, t=t, val=val: e.memset(t.t[:], val), writes=[t.b])
            SHIFT_AP[which] = t
        P.dma(lambda e: e.dma_start(out=dcol.t[:], in_=d_d[l]), writes=[dcol.b])
        def row_pre(kq):
          with P.scope() as S1:
              NS_ = 512
              cq = slice(kq * 512, (kq + 1) * 512)
              def ldrow(name):
                  t = S1.sbuf(name, [128, NS_], F32)
                  P.dma(lambda e: e.dma_start(out=t.t[:], in_=d_rows[name][l:l + 1, cq].partition_broadcast(128)), writes=[t.b])
                  return t
              lre, lim, ldt = ldrow("s_lre_row"), ldrow("s_lim_row"), ldrow("s_ldt_row")
              dt = S1.sbuf("dt", [128, NS_], F32)
              P.op("act", lambda e: e.activation(out=dt.t[:], in_=ldt.t[:], func=AF.Exp), reads=[ldt.b], writes=[dt.b])
              a_r = S1.sbuf("a_r", [128, NS_], F32); b_r = S1.sbuf("b_r", [128, NS_], F32)
              tt(P, "dve", a_r.t[:], lre.t[:], dt.t[:], ALU.mult, [lre.b, dt.b], [a_r.b])
              tt(P, "dve", b_r.t[:], lim.t[:], dt.t[:], ALU.mult, [lim.b, dt.b], [b_r.b])
              niota = S1.sbuf("niota", [128, 1], F32)
              P.op("dve", lambda e: e.tensor_scalar(out=niota.t[:], in0=C["c_iotap"].t[:], scalar1=-1.0, scalar2=None, op0=ALU.mult),
                   reads=[C["c_iotap"].b], writes=[niota.b])
              arg = S1.sbuf("arg", [128, NS_], F32)
              P.op("dve", lambda e: e.tensor_scalar(out=arg.t[:], in0=b_r.t[:], scalar1=C["c_iotap"].t[:], scalar2=None, op0=ALU.mult),
                   reads=[b_r.b, C["c_iotap"].b], writes=[arg.b])
              sn, cs = sincos(P, S1, arg, NS_, "m")
              mag = S1.sbuf("mag", [128, NS_], F32)
              P.op("dve", lambda e: e.tensor_scalar(out=mag.t[:], in0=a_r.t[:], scalar1=niota.t[:], scalar2=None, op0=ALU.mult),
                   reads=[a_r.b, niota.b], writes=[mag.b])
              P.op("act", lambda e: e.activation(out=mag.t[:], in_=mag.t[:], func=AF.Exp), reads=[mag.b], writes=[mag.b])
              tt(P, "dve", Tm_re.t[:, cq], mag.t[:], cs.t[:], ALU.mult, [mag.b, cs.b], [Tm_re.b])
              tt(P, "dve", nTm_im.t[:, cq], mag.t[:], sn.t[:], ALU.mult, [mag.b, sn.b], [nTm_im.b])
              sn1, cs1 = sincos(P, S1, b_r, NS_, "one")
              m1 = S1.sbuf("m1", [128, NS_], F32)
              P.op("act", lambda e: e.activation(out=m1.t[:], in_=a_r.t[:], func=AF.Exp), reads=[a_r.b], writes=[m1.b])
              xr_ = S1.sbuf("xr", [128, NS_], F32); yr_ = S1.sbuf("yr", [128, NS_], F32)
              tt(P, "dve", xr_.t[:], m1.t[:], cs1.t[:], ALU.mult, [m1.b, cs1.b], [xr_.b])
              P.op("dve", lambda e: e.tensor_scalar(out=xr_.t[:], in0=xr_.t[:], scalar1=-1.0, scalar2=None, op0=ALU.add), reads=[xr_.b], writes=[xr_.b])
              tt(P, "dve", yr_.t[:], m1.t[:], sn1.t[:], ALU.mult, [m1.b, sn1.b], [yr_.b])
              den = S1.sbuf("den", [128, NS_], F32); t1 = S1.sbuf("t1", [128, NS_], F32); t2 = S1.sbuf("t2", [128, NS_], F32)
              tt(P, "dve", den.t[:], lre.t[:], lre.t[:], ALU.mult, [lre.b], [den.b])
              tt(P, "dve", t1.t[:], lim.t[:], lim.t[:], ALU.mult, [lim.b], [t1.b])
              tt(P, "dve", den.t[:], den.t[:], t1.t[:], ALU.add, [den.b, t1.b], [den.b])
              P.op("dve", lambda e: e.reciprocal(out=den.t[:], in_=den.t[:]), reads=[den.b], writes=[den.b])
              cre = S1.sbuf("cre", [128, NS_], F32); cim = S1.sbuf("cim", [128, NS_], F32)
              tt(P, "dve", t1.t[:], xr_.t[:], lre.t[:], ALU.mult, [xr_.b, lre.b], [t1.b])
              tt(P, "dve", t2.t[:], yr_.t[:], lim.t[:], ALU.mult, [yr_.b, lim.b], [t2.b])
              tt(P, "dve", t1.t[:], t1.t[:], t2.t[:], ALU.add, [t1.b, t2.b], [t1.b])
              tt(P, "dve", cre.t[:], t1.t[:], den.t[:], ALU.mult, [t1.b, den.b], [cre.b])
              tt(P, "dve", t1.t[:], yr_.t[:], lre.t[:], ALU.mult, [yr_.b, lre.b], [t1.b])
              tt(P, "dve", t2.t[:], xr_.t[:], lim.t[:], ALU.mult, [xr_.b, lim.b], [t2.b])
              tt(P, "dve", t1.t[:], t1.t[:], t2.t[:], ALU.subtract, [t1.b, t2.b], [t1.b])
              tt(P, "dve", cim.t[:], t1.t[:], den.t[:], ALU.mult, [t1.b, den.b], [cim.b])
              Bre = S1.sbuf("Bre", [128, NS_], F32); Bim = S1.sbuf("Bim", [128, NS_], F32)
              P.dma(lambda e: e.dma_start(out=Bre.t[:], in_=d_B["s_Bre_blk"][l][:, kq, :]), writes=[Bre.b])
              P.dma(lambda e: e.dma_start(out=Bim.t[:], in_=d_B["s_Bim_blk"][l][:, kq, :]), writes=[Bim.b])
              BDre_f = BDre.t[:, kq, :]; BDim_f = BDim.t[:, kq, :]
              tt(P, "dve", t1.t[:], cre.t[:], Bre.t[:], ALU.mult, [cre.b, Bre.b], [t1.b])
              tt(P, "dve", t2.t[:], cim.t[:], Bim.t[:], ALU.mult, [cim.b, Bim.b], [t2.b])
              tt(P, "dve", BDre_f, t1.t[:], t2.t[:], ALU.subtract, [t1.b, t2.b], [BDre.b])
              tt(P, "dve", t1.t[:], cre.t[:], Bim.t[:], ALU.mult, [cre.b, Bim.b], [t1.b])
              tt(P, "dve", t2.t[:], cim.t[:], Bre.t[:], ALU.mult, [cim.b, Bre.b], [t2.b])
              tt(P, "dve", BDim_f, t1.t[:], t2.t[:], ALU.add, [t1.b, t2.b], [BDim.b])
        for kq_ in range(4):
            row_pre(kq_)
        with P.scope() as S2:
            def ldf(name):
                t = S2.sbuf(name, [128, 16], F32)
                P.dma(lambda e: e.dma_start(out=t.t[:], in_=d_f[name][l]), writes=[t.b])
                return t
            lre, lim, ldt = ldf("s_lre_f"), ldf("s_lim_f"), ldf("s_ldt_f")
            dt = S2.sbuf("dtf", [128, 16], F32)
            P.op("act", lambda e: e.activation(out=dt.t[:], in_=ldt.t[:], func=AF.Exp), reads=[ldt.b], writes=[dt.b])
            a_f = S2.sbuf("a_f", [128, 16], F32); b_f = S2.sbuf("b_f", [128, 16], F32)
            tt(P, "dve", a_f.t[:], lre.t[:], dt.t[:], ALU.mult, [lre.b, dt.b], [a_f.b])
            tt(P, "dve", b_f.t[:], lim.t[:], dt.t[:], ALU.mult, [lim.b, dt.b], [b_f.b])
            argp = S2.sbuf("argp", [128, 16 * 128], F32); ea = S2.sbuf("ea", [128, 16 * 128], F32)
            for i in range(16):
                P.op("dve", lambda e, i=i: e.tensor_scalar(out=argp.t[:, i * 128:(i + 1) * 128], in0=C["c_iota"].t[:], scalar1=b_f.t[:, i:i + 1], scalar2=None, op0=ALU.mult),
                     reads=[b_f.b, C["c_iota"].b], writes=[argp.b])
                P.op("dve", lambda e, i=i: e.tensor_scalar(out=ea.t[:, i * 128:(i + 1) * 128], in0=C["c_iota"].t[:], scalar1=a_f.t[:, i:i + 1], scalar2=None, op0=ALU.mult),
                     reads=[a_f.b, C["c_iota"].b], writes=[ea.b])
            snp, csp = sincos(P, S2, argp, 2048, "p")
            P.op("act", lambda e: e.activation(out=ea.t[:], in_=ea.t[:], func=AF.Exp), reads=[ea.b], writes=[ea.b])
            Tp_re_f = Tp_re.t[:].rearrange("p i t -> p (i t)"); nTp_im_f = nTp_im.t[:].rearrange("p i t -> p (i t)")
            tt(P, "dve", Tp_re_f, ea.t[:], csp.t[:], ALU.mult, [ea.b, csp.b], [Tp_re.b])
            P.op("dve", lambda e: e.scalar_tensor_tensor(out=nTp_im_f, in0=ea.t[:], scalar=-1.0, in1=snp.t[:], op0=ALU.mult, op1=ALU.mult),
                 reads=[ea.b, snp.b], writes=[nTp_im.b])
            a128 = S2.sbuf("a128", [128, 16], F32); b128 = S2.sbuf("b128", [128, 16], F32)
            P.op("dve", lambda e: e.tensor_scalar(out=b128.t[:], in0=b_f.t[:], scalar1=128.0, scalar2=None, op0=ALU.mult), reads=[b_f.b], writes=[b128.b])
            sL, cL = sincos(P, S2, b128, 16, "L")
            P.op("act", lambda e: e.activation(out=a128.t[:], in_=a_f.t[:], func=AF.Exp, scale=128.0), reads=[a_f.b], writes=[a128.b])
            tt(P, "dve", L_re.t[:], a128.t[:], cL.t[:], ALU.mult, [a128.b, cL.b], [L_re.b])
            tt(P, "dve", L_im.t[:], a128.t[:], sL.t[:], ALU.mult, [a128.b, sL.b], [L_im.b])
            for nm, dst in (("s_Cre_blk", CBre), ("s_Cim_blk", CBim)):
                tmp = S2.sbuf("ctmp" + nm, [128, 16, 128], F32)
                P.dma(lambda e, tmp=tmp, nm=nm: e.dma_start(out=tmp.t[:], in_=d_C[nm][l]), writes=[tmp.b])
                P.op("dve", lambda e, tmp=tmp, dst=dst: e.tensor_copy(out=dst.t[:], in_=tmp.t[:]), reads=[tmp.b], writes=[dst.b])
                if nm == "s_Cim_blk":
                    P.op("dve", lambda e, tmp=tmp: e.tensor_scalar(out=nCBim.t[:], in0=tmp.t[:], scalar1=-1.0, scalar2=None, op0=ALU.mult), reads=[tmp.b], writes=[nCBim.b])
    return dict(Tm_re=Tm_re, nTm_im=nTm_im, Tp_re=Tp_re, nTp_im=nTp_im, BDre=BDre, BDim=BDim, CBre=CBre, CBim=CBim, nCBim=nCBim,
                L_re=L_re, L_im=L_im, dcol=dcol)


def ssm_gen(P, DR, C, l, S3, tb, Ba, Bb):
    GA = DR.GA[l]
    XB = DR.get(f"XB{l}", [XB_ROWS, T], BF16)
    xbb = DR.chunkbuf(f"XB{l}", 2)
    Tm_re, nTm_im, Tp_re, nTp_im = tb["Tm_re"], tb["nTm_im"], tb["Tp_re"], tb["nTp_im"]
    BDre, BDim, CBre, CBim, nCBim, L_re, L_im, dcol = tb["BDre"], tb["BDim"], tb["CBre"], tb["CBim"], tb["nCBim"], tb["L_re"], tb["L_im"], tb["dcol"]
    uT = S3.sbuf("uT", [128, 4, S], BF16)
    for r in range(2):
        P.dma(lambda e, r=r: e.dma_start(out=uT.t[:, :, r * 1024:(r + 1) * 1024],
                                         in_=gsrc(e, GA, r, 3072, 512).rearrange("(kc p) t -> p kc t", p=128)),
              reads=[DR.buf(f"GA{l}_6")], writes=[uT.b])
    ysr = [S3.ring(f"ys{k}", 2, [128, 512], F32) for k in range(4)]
    qring = S3.ring("Q", 8, [128, 512], BF16)
    pring = S3.ring("Pp", 8, [128, 4, 128], BF16)
    cr = [[S3.sbuf(f"cr{k}_{i}", [128, 4], F32) for i in range(2)] for k in range(4)]
    ci = [[S3.sbuf(f"ci{k}_{i}", [128, 4], F32) for i in range(2)] for k in range(4)]
    tmp4 = S3.ring("tmp4", 12, [128, 4], F32)
    g1 = S3.ring("g1", 2, [128, 512], F32)
    gob = S3.ring("gob", 2, [128, 512], BF16)
    for k in range(4):
        P.op("dve", lambda e, k=k: e.memset(cr[k][0].t[:], 0.0), writes=[cr[k][0].b])
        P.op("dve", lambda e, k=k: e.memset(ci[k][0].t[:], 0.0), writes=[ci[k][0].b])
    triI, ntriI = C["c_triI"], C["c_ntriI"]
    Ga = Ba.t[:].rearrange("p (i t) -> p i t", i=4)
    Gb = Bb.t[:].rearrange("p (i t) -> p i t", i=4)
    ycur = [None] * 4
    for j in range(16):
        for kc in range(4):
            cur, nxt = j % 2, (j + 1) % 2
            cre_t, cim_t = cr[kc][cur], ci[kc][cur]
            P.op("pe", mm(Ba.t[:], uT.t[:, kc, j * 128:(j + 1) * 128], BDre.t[:, kc, :], True, True), reads=[uT.b, BDre.b], writes=[Ba.b])
            P.op("pe", mm(Bb.t[:], uT.t[:, kc, j * 128:(j + 1) * 128], BDim.t[:, kc, :], True, True), reads=[uT.b, BDim.b], writes=[Bb.b])
            yield
            Q1, Q2, Q3, Q4 = qring.next(), qring.next(), qring.next(), qring.next()
            sl = slice(kc * 512, (kc + 1) * 512)
            tt(P, "dve", Q1.t[:], Ba.t[:], Tm_re.t[:, sl], ALU.mult, [Ba.b, Tm_re.b], [Q1.b])
            tt(P, "dve", Q2.t[:], Bb.t[:], nTm_im.t[:, sl], ALU.mult, [Bb.b, nTm_im.b], [Q2.b])
            tt(P, "dve", Q3.t[:], Bb.t[:], Tm_re.t[:, sl], ALU.mult, [Bb.b, Tm_re.b], [Q3.b])
            tt(P, "dve", Q4.t[:], Ba.t[:], nTm_im.t[:, sl], ALU.mult, [Ba.b, nTm_im.b], [Q4.b])
            yield
            for i in range(4):
                cs_ = slice(i * 128, (i + 1) * 128)
                P.op("pe", mm(Ga[:, i, :], Q1.t[:, cs_], triI.t[:], True, False), reads=[Q1.b, triI.b], writes=[Ba.b])
                P.op("pe", mm(Ga[:, i, :], Q2.t[:, cs_], triI.t[:], False, True), reads=[Q2.b, triI.b], writes=[Ba.b])
                P.op("pe", mm(Gb[:, i, :], Q3.t[:, cs_], triI.t[:], True, False), reads=[Q3.b, triI.b], writes=[Bb.b])
                P.op("pe", mm(Gb[:, i, :], Q4.t[:, cs_], ntriI.t[:], False, True), reads=[Q4.b, ntriI.b], writes=[Bb.b])
            yield
            P1, P2, P3, P4 = pring.next(), pring.next(), pring.next(), pring.next()
            for i in range(4):
                ti = kc * 4 + i
                for (Pt, G, Gbuf, cc, tab) in ((P1, Ga, Ba, cre_t, Tp_re), (P2, Gb, Bb, cim_t, nTp_im), (P3, Ga, Ba, cre_t, nTp_im), (P4, Gb, Bb, cim_t, Tp_re)):
                    P.op("dve", lambda e, Pt=Pt, G=G, cc=cc, tab=tab, i=i, ti=ti: e.scalar_tensor_tensor(
                        out=Pt.t[:, i, :], in0=G[:, i, :], scalar=cc.t[:, i:i + 1], in1=tab.t[:, ti, :], op0=ALU.add, op1=ALU.mult),
                        reads=[Gbuf.b, cc.b, tab.b], writes=[Pt.b])
            if j < 15:
                gcr, gci, u1, u2 = tmp4.next(), tmp4.next(), tmp4.next(), tmp4.next()
                Lr = L_re.t[:, kc * 4:(kc + 1) * 4]; Li = L_im.t[:, kc * 4:(kc + 1) * 4]
                tt(P, "dve", gcr.t[:], Ga[:, :, 127], cre_t.t[:], ALU.add, [Ba.b, cre_t.b], [gcr.b])
                tt(P, "dve", gci.t[:], Gb[:, :, 127], cim_t.t[:], ALU.add, [Bb.b, cim_t.b], [gci.b])
                ncr, nci = cr[kc][nxt], ci[kc][nxt]
                tt(P, "dve", u1.t[:], gcr.t[:], Lr, ALU.mult, [gcr.b, L_re.b], [u1.b])
                tt(P, "dve", u2.t[:], gci.t[:], Li, ALU.mult, [gci.b, L_im.b], [u2.b])
                tt(P, "dve", ncr.t[:], u1.t[:], u2.t[:], ALU.subtract, [u1.b, u2.b], [ncr.b])
                u3, u4 = tmp4.next(), tmp4.next()
                tt(P, "dve", u3.t[:], gci.t[:], Lr, ALU.mult, [gci.b, L_re.b], [u3.b])
                tt(P, "dve", u4.t[:], gcr.t[:], Li, ALU.mult, [gcr.b, L_im.b], [u4.b])
                tt(P, "dve", nci.t[:], u3.t[:], u4.t[:], ALU.add, [u3.b, u4.b], [nci.b])
            yield
            Y = Ba.t[:, 0:128]
            n = 0
            for i in range(4):
                ti = kc * 4 + i
                for (Cm, Pt) in ((CBre, P1), (CBre, P2), (CBim, P3), (nCBim, P4)):
                    P.op("pe", mm(Y, Cm.t[:, ti, :], Pt.t[:, i, :], n == 0, n == 15), reads=[Cm.b, Pt.b], writes=[Ba.b])
                    n += 1
            yield
            if j % 4 == 0:
                ycur[kc] = ysr[kc].next()
            yt = ycur[kc]
            jj = j % 4
            P.op("dve", lambda e, yt=yt, kc=kc, j=j, jj=jj: e.scalar_tensor_tensor(
                out=yt.t[:, jj * 128:(jj + 1) * 128], in0=uT.t[:, kc, j * 128:(j + 1) * 128], scalar=dcol.t[:, kc:kc + 1],
                in1=Ba.t[:, 0:128], op0=ALU.mult, op1=ALU.add), reads=[uT.b, dcol.b, Ba.b], writes=[yt.b])
            if jj == 3:
                a_ = g1.next(); o = gob.next()
                y = yt.t[:]
                tt(P, "dve", a_.t[:], y, y, ALU.mult, [yt.b], [a_.b])
                P.op("dve", lambda e, a_=a_: e.tensor_scalar(out=a_.t[:], in0=a_.t[:], scalar1=0.044715, scalar2=1.0, op0=ALU.mult, op1=ALU.add), reads=[a_.b], writes=[a_.b])
                tt(P, "dve", a_.t[:], a_.t[:], y, ALU.mult, [a_.b, yt.b], [a_.b])
                P.op("act", lambda e, a_=a_: e.activation(out=a_.t[:], in_=a_.t[:], func=AF.Sigmoid, scale=1.5957691216057308), reads=[a_.b], writes=[a_.b])
                tt(P, "dve", o.t[:], a_.t[:], y, ALU.mult, [a_.b, yt.b], [o.b])
                t0 = (j - 3) * 128
                r = t0 // 1024
                row = xrow(r, 1024 + kc * 128)
                col = t0 % 1024
                P.dma(lambda e, o=o, row=row, col=col: e.dma_start(out=XB[row:row + 128, col:col + 512], in_=o.t[:]), reads=[o.b], writes=[xbb])
            yield


def phase_B(P, DR, C, l, xch):
    with P.scope() as S0:
        tb = ssm_tables(P, DR, C, l, S0)
        with P.scope() as Sc:
            Ba = Sc.psum("ssmA", [128, 512], F32)
            Bb = Sc.psum("ssmB", [128, 512], F32)
            ga = attn_gen(P, DR, C, l, Sc, xch)
            gs = ssm_gen(P, DR, C, l, Sc, tb, Ba, Bb)
            a_live = s_live = True
            rnd = 0
            while a_live or s_live:
                rnd += 1
                if a_live:
                    try:
                        next(ga)
                    except StopIteration:
                        a_live = False
                if s_live and (rnd % 2 == 0 or not a_live):
                    try:
                        next(gs)
                    except StopIteration:
                        s_live = False
                        xch.cc(2)


def grep_load(P, Sc, DR, name, l):
    g_d = DR.get(name, [NL, D], F32)
    t = Sc.sbuf("g_" + name, [128, D], F32)
    P.dma(lambda e: e.dma_start(out=t.t[:], in_=g_d[l:l + 1, :].partition_broadcast(128)), writes=[t.b])
    return t


def row_rstd(P, ps_pair, n, out_t):
    for hb in range(2):
        P.op("act", lambda e, hb=hb: e.activation(out=out_t.t[:, hb * 512:(hb + 1) * 512], in_=ps_pair[hb].t[:], func=AF.Ln, scale=1.0 / n, bias=EPS),
             reads=[ps_pair[hb].b], writes=[out_t.b])
    P.op("act", lambda e: e.activation(out=out_t.t[:], in_=out_t.t[:], func=AF.Exp, scale=-0.5), reads=[out_t.b], writes=[out_t.b])


def sumsq_acc(P, C, ps_pair, src_bf, sqr, first, last, extra_reads):
    sq = sqr.next()
    P.op("act", lambda e: e.activation(out=sq.t[:], in_=src_bf, func=AF.Square), reads=extra_reads, writes=[sq.b])
    for hb in range(2):
        P.op("pe", mm(ps_pair[hb].t[:], C["c_ones_b"].t[:], sq.t[:, hb * 512:(hb + 1) * 512], first, last),
             reads=[sq.b, C["c_ones_b"].b], writes=[ps_pair[hb].b])


def phase_C(P, DR, C, AT_unused, l, xname, xoname):
    GA = DR.GA[l]
    GB = DR.GB[l]
    aT = DR.get(f"aT{l}", [3072, T], BF16); aTb = DR.buf(f"aT{l}")
    bgT = DR.get(f"bgT{l}", [2048, T], BF16); bgb = DR.buf(f"bgT{l}")
    cgT = DR.get(f"cgT{l}", [1024, T], BF16); cgb = DR.buf(f"cgT{l}")
    xin = DR.get(xname, [T, D], F32); xinb = DR.buf(xname)
    xo = DR.get(xoname, [T, D], F32); xob = DR.buf(xoname)
    mem = DR.get("mem", [256, D], F32)
    r1 = DR.get("r1", [T, D], F32); r1b = DR.buf("r1")
    x1 = DR.get("x1", [T, D], F32); x1b = DR.buf("x1")
    qTd = DR.get("qTd", [D, T], BF16); qTb = DR.buf("qTd")
    kTm = DR.get("kTm", [D, 256], BF16); kTmb = DR.buf("kTm")
    vm = DR.get("vm", [256, D], BF16); vmb = DR.buf("vm")
    d_convw = DR.get("conv_wT", [NL, 128, 8, 31], F32)
    d_cb = DR.get("conv_b_l", [NL, 128, 8], F32); d_lg = DR.get("conv_ln_g_l", [NL, 128, 8], F32); d_lb = DR.get("conv_ln_b_l", [NL, 128, 8], F32)
    d_bg = DR.get("branch_g_l", [NL, 128, 32], F32); d_glub = DR.get("glu_b_l", [NL, 128, 8], F32)
    d_hs = DR.get("halo_scale", [128, 1], F32)
    w_out = DR.get("w_out", [NL, D, D], F32); wq = DR.get("xa_wq", [NL, D, D], F32); wk = DR.get("xa_wk", [NL, D, D], F32)
    wv = DR.get("xa_wv", [NL, D, D], F32); wo = DR.get("xa_wo", [NL, D, D], F32); gluw = DR.get("ssm_glu_w", [NL, 1024, 1024], F32)

    SA = P.scope()
    SA.__enter__()
    AT = SA.sbuf("AT", [128, 32, T], BF16)
    with P.scope() as Sc:
        cw = Sc.sbuf("cw", [128, 8, 31], F32); cb = Sc.sbuf("cb", [128, 8], F32); lg = Sc.sbuf("lg", [128, 8], F32); lb = Sc.sbuf("lb", [128, 8], F32)
        bgn = Sc.sbuf("bgn", [128, 32], F32); glub = Sc.sbuf("glub", [128, 8], F32); hsc = Sc.sbuf("hsc", [128, 1], F32)
        for t_, d_ in ((cw, d_convw), (cb, d_cb), (lg, d_lg), (lb, d_lb), (bgn, d_bg), (glub, d_glub)):
            P.dma(lambda e, t_=t_, d_=d_: e.dma_start(out=t_.t[:], in_=d_[l]), writes=[t_.b])
        P.dma(lambda e: e.dma_start(out=hsc.t[:], in_=d_hs), writes=[hsc.b])
        sqr = Sc.ring("sq", 3, [128, T], BF16)
        S1p = [Sc.psum(f"S1p{i}", [128, 512], F32) for i in range(2)]
        S2p = [Sc.psum(f"S2p{i}", [128, 512], F32) for i in range(2)]
        SSp = [Sc.psum(f"SSp{i}", [128, 512], F32) for i in range(2)]
        rstd_br = [Sc.sbuf(f"rstd_br{i}", [128, T], F32) for i in range(3)]
        def halo_v(e, slot):
            return GA[XA_NCH][0:128, slot * 32:(slot + 1) * 32]
        gab = DR.buf(f"GA{l}_{XA_NCH}")

        def conv_part():
          with P.scope() as S1:
            yconv = S1.sbuf("yconv", [128, 8, T], F32)
            vr = S1.ring("val", 3, [128, T + 32], BF16); gr_ = S1.ring("glu", 3, [128, T + 32], BF16)
            sgr = S1.ring("sig", 2, [128, T + 32], F32); ur = S1.ring("u", 3, [128, T + 32], BF16)
            ybr = S1.ring("yb", 2, [128, T], BF16)
            dgr = S1.ring("dg", 2, [128, 31, 128], BF16)
            cps = S1.ring("cps", 2, [128, 512], F32, psum=True)
            def prep(cc):
                val, glu, sig, u = vr.next(), gr_.next(), sgr.next(), ur.next()
                P.dma(lambda e: e.dma_start(out=val.t[:, 32:], in_=aT[cc * 128:(cc + 1) * 128, :]), reads=[aTb], writes=[val.b])
                P.dma(lambda e: e.dma_start(out=glu.t[:, 32:], in_=aT[1024 + cc * 128:1024 + (cc + 1) * 128, :]), reads=[aTb], writes=[glu.b])
                P.dma(lambda e: e.dma_start(out=val.t[:, 0:32], in_=halo_v(e, cc)), reads=[gab], writes=[val.b])
                P.dma(lambda e: e.dma_start(out=glu.t[:, 0:32], in_=halo_v(e, 8 + cc)), reads=[gab], writes=[glu.b])
                dg = dgr.next()
                for k in range(31):
                    if k % 2 == 0:
                        P.op("act", lambda e, k=k: e.activation(out=dg.t[:, k, :], in_=C["c_ident"].t[:], func=AF.Copy, scale=cw.t[:, cc, k:k + 1]),
                             reads=[C["c_ident"].b, cw.b], writes=[dg.b])
                    else:
                        P.op("dve", lambda e, k=k: e.tensor_scalar(out=dg.t[:, k, :], in0=C["c_ident"].t[:], scalar1=cw.t[:, cc, k:k + 1], scalar2=None, op0=ALU.mult),
                             reads=[C["c_ident"].b, cw.b], writes=[dg.b])
                P.op("act", lambda e: e.activation(out=sig.t[:], in_=glu.t[:], func=AF.Sigmoid), reads=[glu.b], writes=[sig.b])
                tt(P, "dve", u.t[:], val.t[:], sig.t[:], ALU.mult, [val.b, sig.b], [u.b])
                P.op("dve", lambda e: e.tensor_scalar(out=u.t[:, 0:32], in0=u.t[:, 0:32], scalar1=hsc.t[:], scalar2=None, op0=ALU.mult),
                     reads=[u.b, hsc.b], writes=[u.b])
                return dg, u

            def main(cc, dg, u):
                acc = yconv.t[:, cc, :]
                for tb in range(2):
                    ps = cps.next()
                    for k in range(31):
                        P.op("pe", mm(ps.t[:], dg.t[:, k, :], u.t[:, 2 + k + tb * 512:2 + k + tb * 512 + 512], k == 0, k == 30), reads=[dg.b, u.b], writes=[ps.b])
                    P.op("act", lambda e, ps=ps, tb=tb: e.activation(out=yconv.t[:, cc, tb * 512:(tb + 1) * 512], in_=ps.t[:], func=AF.Identity, bias=cb.t[:, cc:cc + 1]),
                         reads=[ps.b, cb.b], writes=[yconv.b])
                yb = ybr.next()
                P.op("act", lambda e: e.copy(out=yb.t[:], in_=acc), reads=[yconv.b], writes=[yb.b])
                for hb in range(2):
                    P.op("pe", mm(S1p[hb].t[:], C["c_ones_b"].t[:], yb.t[:, hb * 512:(hb + 1) * 512], cc == 0, cc == 7), reads=[yb.b, C["c_ones_b"].b], writes=[S1p[hb].b])
                sumsq_acc(P, C, S2p, yb.t[:], sqr, cc == 0, cc == 7, [yb.b])

            nxt_ = prep(0)
            for cc in range(8):
                cur_ = nxt_
                if cc + 1 < 8:
                    nxt_ = prep(cc + 1)
                main(cc, *cur_)
            mu = S1.sbuf("mu", [128, T], F32); rs = S1.sbuf("rs", [128, T], F32); m2 = S1.sbuf("m2", [128, T], F32)
            for hb in range(2):
                hs_ = slice(hb * 512, (hb + 1) * 512)
                P.op("act", lambda e, hb=hb, hs_=hs_: e.mul(out=mu.t[:, hs_], in_=S1p[hb].t[:], mul=1.0 / 1024), reads=[S1p[hb].b], writes=[mu.b])
                P.op("act", lambda e, hb=hb, hs_=hs_: e.mul(out=rs.t[:, hs_], in_=S2p[hb].t[:], mul=1.0 / 1024), reads=[S2p[hb].b], writes=[rs.b])
            tt(P, "dve", m2.t[:], mu.t[:], mu.t[:], ALU.mult, [mu.b], [m2.b])
            tt(P, "dve", rs.t[:], rs.t[:], m2.t[:], ALU.subtract, [rs.b, m2.b], [rs.b])
            P.op("act", lambda e: e.activation(out=rs.t[:], in_=rs.t[:], func=AF.Ln, bias=EPS), reads=[rs.b], writes=[rs.b])
            P.op("act", lambda e: e.activation(out=rs.t[:], in_=rs.t[:], func=AF.Exp, scale=-0.5), reads=[rs.b], writes=[rs.b])
            gtr = S1.ring("gt", 3, [128, T], BF16); gsr = S1.ring("gs", 2, [128, T], F32); zr = S1.ring("z", 2, [128, T], F32)
            for cc in range(8):
                acc = yconv.t[:, cc, :]
                z = zr.next(); gt = gtr.next(); gs = gsr.next()
                tt(P, "dve", z.t[:], acc, mu.t[:], ALU.subtract, [yconv.b, mu.b], [z.b])
                tt(P, "dve", z.t[:], z.t[:], rs.t[:], ALU.mult, [z.b, rs.b], [z.b])
                P.op("dve", lambda e, z=z, cc=cc: e.tensor_scalar(out=z.t[:], in0=z.t[:], scalar1=lg.t[:, cc:cc + 1], scalar2=lb.t[:, cc:cc + 1], op0=ALU.mult, op1=ALU.add),
                     reads=[z.b, lg.b, lb.b], writes=[z.b])
                P.op("act", lambda e, z=z: e.activation(out=z.t[:], in_=z.t[:], func=AF.Silu), reads=[z.b], writes=[z.b])
                P.dma(lambda e, gt=gt, cc=cc: e.dma_start(out=gt.t[:], in_=aT[2048 + cc * 128:2048 + (cc + 1) * 128, :]), reads=[aTb], writes=[gt.b])
                P.op("act", lambda e, gs=gs, gt=gt: e.activation(out=gs.t[:], in_=gt.t[:], func=AF.Silu), reads=[gt.b], writes=[gs.b])
                tt(P, "dve", AT.t[:, cc, :], z.t[:], gs.t[:], ALU.mult, [z.b, gs.b], [AT.b])
                sumsq_acc(P, C, SSp, AT.t[:, cc, :], sqr, cc == 0, cc == 7, [AT.b])
            row_rstd(P, SSp, 1024, rstd_br[0])
        conv_part()

        def attn_part():
          with P.scope() as S1:
            orr = S1.ring("o", 4, [128, T], BF16); gtr = S1.ring("gt", 4, [128, T], BF16); gsr = S1.ring("gs", 4, [128, T], F32)
            for hc in range(16):
                o = orr.next(); gt = gtr.next(); gs = gsr.next()
                r, h = hc // 8, hc % 8
                P.dma(lambda e, o=o, r=r, h=h: e.dma_start(out=o.t[:], in_=gsrc(e, GB, r, h * 128, 128)), reads=[DR.buf(f"GB{l}_{(h * 128) // 512}")], writes=[o.b])
                P.dma(lambda e, gt=gt, hc=hc: e.dma_start(out=gt.t[:], in_=bgT[hc * 128:(hc + 1) * 128, :]), reads=[bgb], writes=[gt.b])
                P.op("act", lambda e, gs=gs, gt=gt: e.activation(out=gs.t[:], in_=gt.t[:], func=AF.Silu), reads=[gt.b], writes=[gs.b])
                tt(P, "dve", AT.t[:, 8 + hc, :], o.t[:], gs.t[:], ALU.mult, [o.b, gs.b], [AT.b])
                sumsq_acc(P, C, SSp, AT.t[:, 8 + hc, :], sqr, hc == 0, hc == 15, [AT.b])
            row_rstd(P, SSp, 2048, rstd_br[1])
        attn_part()

        def ssm_part():
          with P.scope() as S1:
            zps = S1.ring("zps", 2, [128, 512], F32, psum=True)
            ygT = S1.sbuf("ygT", [128, 8, T], BF16)
            gw = S1.sbuf("gw", [128, 8, 1024], BF16)
            gwv = gluw[l].rearrange("(kc p) n -> p kc n", p=128)
            for j in range(0, 8, 4):
                P.dma(lambda e, j=j: e.dma_start(out=gw.t[:, j:j + 4, :], in_=gwv[:, j:j + 4, :]), writes=[gw.b], q="pool")
            for c in range(8):
                r, k4 = c // 4, c % 4
                P.dma(lambda e, c=c, r=r, k4=k4: e.dma_start(out=ygT.t[:, c, :], in_=gsrc(e, GB, r, 1024 + k4 * 128, 128)), reads=[DR.buf(f"GB{l}_2")], writes=[ygT.b])
            sgr = S1.ring("sg", 3, [128, T], F32); gtr = S1.ring("gt", 3, [128, T], BF16); gsr = S1.ring("gs", 3, [128, T], F32)
            for n in range(8):
                sg = sgr.next(); gt = gtr.next(); gs = gsr.next()
                for tb in range(2):
                    ps = zps.next()
                    for kc in range(8):
                        P.op("pe", mm(ps.t[:], gw.t[:, kc, n * 128:(n + 1) * 128], ygT.t[:, kc, tb * 512:(tb + 1) * 512], kc == 0, kc == 7), reads=[gw.b, ygT.b], writes=[ps.b])
                    P.op("act", lambda e, sg=sg, ps=ps, tb=tb, n=n: e.activation(out=sg.t[:, tb * 512:(tb + 1) * 512], in_=ps.t[:], func=AF.Sigmoid, bias=glub.t[:, n:n + 1]),
                         reads=[ps.b, glub.b], writes=[sg.b])
                P.dma(lambda e, gt=gt, n=n: e.dma_start(out=gt.t[:], in_=cgT[n * 128:(n + 1) * 128, :]), reads=[cgb], writes=[gt.b])
                P.op("act", lambda e, gs=gs, gt=gt: e.activation(out=gs.t[:], in_=gt.t[:], func=AF.Silu), reads=[gt.b], writes=[gs.b])
                tt(P, "dve", sg.t[:], sg.t[:], ygT.t[:, n, :], ALU.mult, [sg.b, ygT.b], [sg.b])
                tt(P, "dve", AT.t[:, 24 + n, :], sg.t[:], gs.t[:], ALU.mult, [sg.b, gs.b], [AT.b])
                sumsq_acc(P, C, SSp, AT.t[:, 24 + n, :], sqr, n == 0, n == 7, [AT.b])
            row_rstd(P, SSp, 1024, rstd_br[2])
        ssm_part()
        for c in range(32):
            br = 0 if c < 8 else (1 if c < 24 else 2)
            P.op("dve", lambda e, c=c, br=br: e.scalar_tensor_tensor(out=AT.t[:, c, :], in0=AT.t[:, c, :], scalar=bgn.t[:, c:c + 1], in1=rstd_br[br].t[:],
                                                                  op0=ALU.mult, op1=ALU.mult), reads=[AT.b, bgn.b, rstd_br[br].b], writes=[AT.b])

    def proj_to_r1(W2d):
        with P.scope() as Sc:
            wring = Sc.ring("w", 3, [128, 32, NB], BF16)
            psr = Sc.ring("ps", 6, [128, 512], F32, psum=True)
            stg = Sc.ring("stg", 4, [128, 512], F32)
            ev = [0]
            for nb, wt in w_stream(P, wring, W2d, 0, D):
                def sink(tt_, ps, nb=nb):
                    st = stg.next(); ev[0] += 1
                    evac(P, ev[0], st.t[:], ps.t[:], [ps.b], [st.b])
                    P.dma(lambda e: e.dma_start(out=r1[tt_ * 128:(tt_ + 1) * 128, nb:nb + NB], in_=st.t[:]), reads=[st.b], writes=[r1b])
                proj_tok(P, wt, AT, T, 32, psr, sink)

    def resid_norm(gname, xsrc, xsrcb, xdst, xdstb, to_AT_gname=None):
        with P.scope() as Sc:
            g_rep = grep_load(P, Sc, DR, gname, l)
            g2 = grep_load(P, Sc, DR, to_AT_gname, l) if to_AT_gname else None
            rr = Sc.ring("r", 2 if to_AT_gname else 3, [128, D], F32); xr = Sc.ring("x", 3, [128, D], F32)
            ssr = Sc.ring("ss", 12, [128, 1], F32)
            hr = Sc.ring("hb", 2, [128, D], BF16)
            if to_AT_gname:
                pst = Sc.ring("pst", 2, [128, 1024], BF16, psum=True)
            for tt_ in range(T // 128):
                r_t = rr.next(); x_t = xr.next(); ss = ssr.next(); rstd = ssr.next()
                P.dma(lambda e, r_t=r_t, tt_=tt_: e.dma_start(out=r_t.t[:], in_=r1[tt_ * 128:(tt_ + 1) * 128, :]), reads=[r1b], writes=[r_t.b])
                P.dma(lambda e, x_t=x_t, tt_=tt_: e.dma_start(out=x_t.t[:], in_=xsrc[tt_ * 128:(tt_ + 1) * 128, :]), reads=[xsrcb], writes=[x_t.b])
                junk = hr.next()
                P.op("act", lambda e, junk=junk, r_t=r_t, ss=ss: e.activation(out=junk.t[:], in_=r_t.t[:], func=AF.Square, accum_out=ss.t[:]), reads=[r_t.b], writes=[junk.b, ss.b])
                rms_rstd(P, ss, rstd, D)
                P.op("dve", lambda e, r_t=r_t, rstd=rstd: e.scalar_tensor_tensor(out=r_t.t[:], in0=r_t.t[:], scalar=rstd.t[:], in1=g_rep.t[:], op0=ALU.mult, op1=ALU.mult),
                     reads=[r_t.b, rstd.b, g_rep.b], writes=[r_t.b])
                tt(P, "dve", x_t.t[:], x_t.t[:], r_t.t[:], ALU.add, [x_t.b, r_t.b], [x_t.b])
                P.dma(lambda e, x_t=x_t, tt_=tt_: e.dma_start(out=xdst[tt_ * 128:(tt_ + 1) * 128, :], in_=x_t.t[:]), reads=[x_t.b], writes=[xdstb], q="pool")
                if to_AT_gname:
                    norm_tile_to_T(P, Sc, C, x_t, g2, AT, tt_, pst, hr, ssr, T)

    proj_to_r1(w_out[l])
    resid_norm("post_norm_g", xin, xinb, x1, x1b, to_AT_gname="xa_pre_g")

    with P.scope() as Sc:
        wring = Sc.ring("w", 3, [128, 32, NB], BF16)
        psr = Sc.ring("ps", 6, [128, 512], F32, psum=True)
        stg = Sc.ring("stg", 4, [128, 512], BF16)
        ev = [0]
        for nb, wt in w_stream(P, wring, wq[l], 0, D):
            def sink(j, tb, ps, nb=nb):
                st = stg.next(); ev[0] += 1
                evac(P, ev[0], st.t[:], ps.t[:], [ps.b], [st.b])
                row = nb + j * 128
                P.dma(lambda e: e.dma_start(out=qTd[row:row + 128, tb * 512:(tb + 1) * 512], in_=st.t[:]), reads=[st.b], writes=[qTb])
            proj_feat(P, wt, AT, T, 32, psr, sink)

    with P.scope() as Sc:
        qr = Sc.ring("q", 2, [128, 8, T], BF16); kr = Sc.ring("k", 2, [128, 8, 256], BF16); vr = Sc.ring("v", 2, [128, 2, 1024], BF16)
        er = Sc.ring("e", 8, [128, 512], BF16); rsr = Sc.ring("rs", 4, [128, 512], F32)
        sps = Sc.ring("sps", 4, [128, 512], F32, psum=True); sump = Sc.ring("sump", 2, [128, 512], F32, psum=True); ops_ = Sc.ring("ops", 2, [128, 512], F32, psum=True)
        for hd in range(4):
            q_t, k_t, v_t = qr.next(), kr.next(), vr.next()
            P.dma(lambda e, q_t=q_t, hd=hd: e.dma_start(out=q_t.t[:], in_=qTd[hd * 1024:(hd + 1) * 1024, :].rearrange("(dc p) t -> p dc t", p=128)), reads=[qTb], writes=[q_t.b])
            P.dma(lambda e, k_t=k_t, hd=hd: e.dma_start(out=k_t.t[:], in_=kTm[hd * 1024:(hd + 1) * 1024, :].rearrange("(dc p) m -> p dc m", p=128)), reads=[kTmb], writes=[k_t.b])
            P.dma(lambda e, v_t=v_t, hd=hd: e.dma_start(out=v_t.t[:], in_=vm[:, hd * 1024:(hd + 1) * 1024].rearrange("(mt p) d -> p mt d", p=128)), reads=[vmb], writes=[v_t.b])
            es = {}
            sms = {}
            rss = {}
            for tb in range(2):
                for mt in range(2):
                    sp_ = sps.next()
                    for dc in range(8):
                        P.op("pe", mm(sp_.t[:], k_t.t[:, dc, mt * 128:(mt + 1) * 128], q_t.t[:, dc, tb * 512:(tb + 1) * 512], dc == 0, dc == 7), reads=[k_t.b, q_t.b], writes=[sp_.b])
                    es[(tb, mt)] = (er.next(), sp_)
            for tb in range(2):
                for mt in range(2):
                    e_t, sp_ = es[(tb, mt)]
                    P.op("act", lambda e, e_t=e_t, sp_=sp_: e.activation(out=e_t.t[:], in_=sp_.t[:], func=AF.Exp, scale=XA_SCALE), reads=[sp_.b], writes=[e_t.b])
            for tb in range(2):
                sm = sump.next()
                for mt in range(2):
                    P.op("pe", mm(sm.t[:], C["c_ones_b"].t[:], es[(tb, mt)][0].t[:], mt == 0, mt == 1), reads=[es[(tb, mt)][0].b, C["c_ones_b"].b], writes=[sm.b])
                sms[tb] = sm
            for tb in range(2):
                rs = rsr.next()
                P.op("dve", lambda e, rs=rs, sm=sms[tb]: e.reciprocal(out=rs.t[:], in_=sm.t[:]), reads=[sms[tb].b], writes=[rs.b])
                rss[tb] = rs
            for tb in range(2):
                for dc in range(8):
                    op_ = ops_.next()
                    for mt in range(2):
                        P.op("pe", mm(op_.t[:], v_t.t[:, mt, dc * 128:(dc + 1) * 128], es[(tb, mt)][0].t[:], mt == 0, mt == 1), reads=[v_t.b, es[(tb, mt)][0].b], writes=[op_.b])
                    tt(P, "dve", AT.t[:, hd * 8 + dc, tb * 512:(tb + 1) * 512], op_.t[:], rss[tb].t[:], ALU.mult, [op_.b, rss[tb].b], [AT.b])

    proj_to_r1(wo[l])
    resid_norm("xa_post_g", x1, x1b, xo, xob, to_AT_gname=None)
    SA.__exit__(None, None, None)


def make_consts():
    bf = ml_dtypes.bfloat16
    i = np.arange(128)
    c = {}
    c["c_ident"] = np.eye(128, dtype=np.float32).astype(bf)
    c["c_ones_f"] = np.ones((128, 128), np.float32)
    c["c_ones_b"] = np.ones((128, 128), np.float32).astype(bf)
    c["c_triU"] = (i[:, None] > i[None, :]).astype(np.float32).astype(bf)
    c["c_triLE"] = (i[:, None] <= i[None, :]).astype(np.float32).astype(bf)
    c["c_triI"] = (i[:, None] <= i[None, :]).astype(np.float32).astype(bf)
    c["c_ntriI"] = (-(i[:, None] <= i[None, :]).astype(np.float32)).astype(bf)
    c["c_zero_b"] = np.zeros((128, 128), np.float32).astype(bf)
    m = np.zeros((128, 4, 512), np.float32)
    for blk in range(4):
        s_pos = blk * 128 + i
        m[:, blk, :] = (s_pos[:, None] < np.arange(512)[None, :])
    c["c_mask"] = m.astype(bf)
    c["c_iota"] = np.tile(np.arange(128, dtype=np.float32)[None, :], (128, 1))
    c["c_iotap"] = np.arange(128, dtype=np.float32)[:, None].copy()
    return c


def build_fused(nlayers=NL):
    _ME.clear()
    nc = bass.Bass("TRN2", target_bir_lowering=False)
    P = Prog(nc)
    io = {n: "in" for n in ALL_INPUT_NAMES}
    io["x_out"] = "out"
    DR = Dram(nc, io)
    DR.GA, DR.GB = {}, {}
    with P.scope() as Sg:
        C = load_consts(P, Sg, DR)
        xname = "x_in"
        for l in range(nlayers):
            XA = DR.get(f"XA{l}", [XA_ROWS, T], BF16)
            xa = Exchange(P, DR, XA, f"XA{l}", f"GA{l}", XA_NCH, small_rows=256)
            phase_A(P, DR, C, None, l, xname, xa)
            DR.GA[l] = xa.finish()
            XB = DR.get(f"XB{l}", [XB_ROWS, T], BF16)
            xb = Exchange(P, DR, XB, f"XB{l}", f"GB{l}", XB_NCH)
            phase_B(P, DR, C, l, xb)
            DR.GB[l] = xb.finish()
            xo = "x_out" if l == nlayers - 1 else f"x2_{l}"
            phase_C(P, DR, C, None, l, xname, xo)
            xname = xo
        P.finish()
    P.emit()
    return nc, P, DR


def ssm_host_layout(inp, hf):
    L = NL
    out = {}
    gl = np.arange(32)
    G = hf * 32 + gl
    lre = inp["ssm_lambda_re"][:, G, :]
    lim = inp["ssm_lambda_im"][:, G, :]
    ldt = np.repeat(inp["ssm_log_dt"][:, G][:, :, None], 64, axis=2)
    out["s_lre_row"] = np.ascontiguousarray(lre.reshape(L, 2048))
    out["s_lim_row"] = np.ascontiguousarray(lim.reshape(L, 2048))
    out["s_ldt_row"] = np.ascontiguousarray(ldt.reshape(L, 2048))
    for nm, a in (("s_lre_f", lre), ("s_lim_f", lim), ("s_ldt_f", ldt)):
        out[nm] = np.ascontiguousarray(a.reshape(L, 16, 128).transpose(0, 2, 1))
    Bre = np.zeros((L, 128, 4, 512), np.float32); Bim = np.zeros((L, 128, 4, 512), np.float32)
    Cre = np.zeros((L, 128, 16, 128), np.float32); Cim = np.zeros((L, 128, 16, 128), np.float32)
    for g in range(32):
        kc, g8 = g // 8, g % 8
        Bre[:, g8 * 16:(g8 + 1) * 16, kc, g8 * 64:(g8 + 1) * 64] = inp["ssm_b_re"][:, G[g]].transpose(0, 2, 1)
        Bim[:, g8 * 16:(g8 + 1) * 16, kc, g8 * 64:(g8 + 1) * 64] = inp["ssm_b_im"][:, G[g]].transpose(0, 2, 1)
        i, a = g // 2, g % 2
        Cre[:, a * 64:(a + 1) * 64, i, g8 * 16:(g8 + 1) * 16] = inp["ssm_c_re"][:, G[g]].transpose(0, 2, 1)
        Cim[:, a * 64:(a + 1) * 64, i, g8 * 16:(g8 + 1) * 16] = inp["ssm_c_im"][:, G[g]].transpose(0, 2, 1)
    out["s_Bre_blk"], out["s_Bim_blk"], out["s_Cre_blk"], out["s_Cim_blk"] = Bre, Bim, Cre, Cim
    d = inp["ssm_d"].reshape(L, 1024)[:, hf * 512:(hf + 1) * 512]
    out["s_d"] = np.ascontiguousarray(d.reshape(L, 4, 128).transpose(0, 2, 1))
    return out


CONST_NAMES = ["c_ident", "c_ones_f", "c_ones_b", "c_triU", "c_triLE", "c_triI", "c_ntriI", "c_zero_b", "c_mask", "c_iota", "c_iotap"]
ALL_INPUT_NAMES = CONST_NAMES + ["x_in", "w_in", "pre_norm_g",
                                 "s_lre_row", "s_lim_row", "s_ldt_row", "s_lre_f", "s_lim_f", "s_ldt_f", "s_Bre_blk", "s_Bim_blk", "s_Cre_blk", "s_Cim_blk", "s_d",
                                 "mem", "conv_wT", "conv_b_l", "conv_ln_g_l", "conv_ln_b_l", "branch_g_l", "glu_b_l", "halo_scale", "w_out", "xa_wq", "xa_wk", "xa_wv",
                                 "xa_wo", "ssm_glu_w", "xa_mem_g", "post_norm_g", "xa_pre_g", "xa_post_g"]


def host_params(inp, b, hf):
    p = {}
    for n in ("w_in", "pre_norm_g", "w_out", "xa_wq", "xa_wk", "xa_wv", "xa_wo", "ssm_glu_w", "xa_mem_g", "post_norm_g", "xa_pre_g", "xa_post_g"):
        p[n] = inp[n]
    p["mem"] = np.ascontiguousarray(inp["mem"][b])
    p["conv_wT"] = np.ascontiguousarray(inp["conv_w"].reshape(NL, 31, 8, 128).transpose(0, 3, 2, 1))
    for src, dst in (("conv_b", "conv_b_l"), ("conv_ln_g", "conv_ln_g_l"), ("conv_ln_b", "conv_ln_b_l"), ("ssm_glu_b", "glu_b_l")):
        p[dst] = np.ascontiguousarray(inp[src].reshape(NL, 8, 128).transpose(0, 2, 1))
    p["branch_g_l"] = np.ascontiguousarray(inp["branch_norm_g"].reshape(NL, 32, 128).transpose(0, 2, 1))
    p["halo_scale"] = np.full((128, 1), float(hf), np.float32)
    p.update(ssm_host_layout(inp, hf))
    return p


_prog = [None]


def run_fused(inp, trace=False):
    if _prog[0] is None:
        _prog[0] = build_fused()
    nc, P, DR = _prog[0]
    consts = make_consts()
    maps = []
    for c in range(8):
        b, hf = c // 2, c % 2
        d = dict(consts)
        d.update(host_params(inp, b, hf))
        d["x_in"] = np.ascontiguousarray(inp["x"][b, hf * T:(hf + 1) * T])
        maps.append({n: d[n] for n in ALL_INPUT_NAMES})
    res = run_bass_kernel_spmd(nc, maps, core_ids=list(range(8)), trace=trace)
    out = np.zeros((4, S, D), np.float32)
    for c in range(8):
        out[c // 2, (c % 2) * T:(c % 2 + 1) * T] = res.results[c]["x_out"]
    return out, res


def kernel(**inputs):
    inp = {k: np.asarray(v) for k, v in inputs.items()}
    out, _ = run_fused(inp)
    return out.astype(np.float32)
```

```python
import numpy as np
from contextlib import ExitStack
import concourse.bass as bass
import concourse.mybir as mybir
from concourse.bass_utils import run_bass_kernel_spmd

F32 = mybir.dt.float32
BF16 = mybir.dt.bfloat16
AF = mybir.ActivationFunctionType
ALU = mybir.AluOpType
AX = mybir.AxisListType

ENGS = ("pe", "act", "dve", "pool", "sp")
EPOCH = 24000
NDSEM = 6


class Buf:
    __slots__ = ("name", "last_w", "readers")

    def __init__(self, name=""):
        self.name = name
        self.last_w = None
        self.readers = []


class Op:
    __slots__ = ("eng", "fn", "deps", "is_dma", "sig", "tok", "n")

    def __init__(self, eng, fn, is_dma):
        self.eng = eng
        self.fn = fn
        self.is_dma = is_dma
        self.deps = set()
        self.sig = False
        self.tok = None
        self.n = -1


class Prog:
    def __init__(self, nc):
        self.nc = nc
        self.ops = []
        self.stack = ExitStack()
        self.last = {e: None for e in ENGS}
        self.barrier_deps = {e: [] for e in ENGS}
        self.dma_slots = {e: [None] * NDSEM for e in ENGS}
        self.dma_cnt = {e: 0 for e in ENGS}
        self.outstanding_dma = []
        self.dma_slots_nb = {}
        self._n = 0

    def sbuf(self, name, shape, dtype, st=None):
        t = (st or self.stack).enter_context(self.nc.sbuf_tensor(name, list(shape), dtype))
        return t

    def psum(self, name, shape, dtype, st=None):
        t = (st or self.stack).enter_context(self.nc.psum_tensor(name, list(shape), dtype))
        return t

    def _add(self, op, reads, writes):
        op.n = self._n
        self._n += 1
        deps = op.deps
        for r in reads:
            if r.last_w is not None:
                deps.add(r.last_w)
        for w in writes:
            if w.last_w is not None:
                deps.add(w.last_w)
            for rd in w.readers:
                deps.add(rd)
        for r in reads:
            r.readers.append(op)
            if len(r.readers) > 64:
                keep = {}
                dm = []
                for o in r.readers:
                    if o.is_dma:
                        dm.append(o)
                    else:
                        keep[o.eng] = o
                r.readers = dm + list(keep.values())
        for w in writes:
            w.last_w = op
            w.readers = []
        for d in self.barrier_deps[op.eng]:
            deps.add(d)
        self.barrier_deps[op.eng] = []
        deps.discard(op)
        self.ops.append(op)
        self.last[op.eng] = op
        return op

    def op(self, eng, fn, reads=(), writes=()):
        return self._add(Op(eng, fn, False), reads, writes)

    def dma(self, fn, reads=(), writes=(), q="sp", nobar=False):
        op = Op(q, fn, True)
        if nobar:
            qk = q + "_nb"
            if qk not in self.dma_cnt:
                self.dma_cnt[qk] = 0
                self.dma_slots_nb[qk] = [None] * NDSEM
            i = self.dma_cnt[qk]
            self.dma_cnt[qk] = i + 1
            slot = i % NDSEM
            prev = self.dma_slots_nb[qk][slot]
            if prev is not None:
                op.deps.add(prev)
            self.dma_slots_nb[qk][slot] = op
            op.tok = ("d", qk, slot, 16 * (i // NDSEM + 1))
            return self._add(op, reads, writes)
        i = self.dma_cnt[q]
        self.dma_cnt[q] = i + 1
        slot = i % NDSEM
        prev = self.dma_slots[q][slot]
        if prev is not None:
            op.deps.add(prev)
        self.dma_slots[q][slot] = op
        op.tok = ("d", q, slot, 16 * (i // NDSEM + 1))
        self.outstanding_dma.append(op)
        if len(self.outstanding_dma) > 4 * NDSEM * 3:
            self.outstanding_dma = self.outstanding_dma[-(NDSEM * 3):]
        return self._add(op, reads, writes)

    def cc(self, fn, reads=(), writes=()):
        op = Op("pool", fn, True)
        self.n_cc = getattr(self, "n_cc", 0) + 1
        op.tok = ("x", "cc", self.n_cc, 1)
        self.cc_ops = getattr(self, "cc_ops", []) + [op]
        return self._add(op, reads, writes)

    def barrier(self):
        deps = [o for o in self.last.values() if o is not None]
        for q in ENGS:
            for o in self.dma_slots[q]:
                if o is not None:
                    deps.append(o)
        for e in ENGS:
            self.barrier_deps[e] = list(deps)

    def emit(self):
        nc = self.nc
        for op in self.ops:
            for d in op.deps:
                if d.is_dma:
                    continue
                if d.eng == "pe" and op.eng == "pe" and not op.is_dma:
                    continue
                d.sig = True
        cnt = {e: 0 for e in ENGS}
        for op in self.ops:
            if op.is_dma:
                continue
            if op.sig:
                cnt[op.eng] += 1
                c = cnt[op.eng]
                op.tok = ("c", op.eng, (c - 1) // EPOCH, (c - 1) % EPOCH + 1)
        nep = {e: (cnt[e] + EPOCH - 1) // EPOCH for e in ENGS}
        st = self.stack
        csem = {}
        for e in ENGS:
            for k in range(max(nep[e], 0)):
                csem[(e, k)] = st.enter_context(nc.semaphore(f"c_{e}_{k}"))
        dsem = {}
        for q in list(self.dma_cnt.keys()):
            if self.dma_cnt[q]:
                for s in range(NDSEM):
                    dsem[(q, s)] = st.enter_context(nc.semaphore(f"d_{q}_{s}"))
        for k in range(1, getattr(self, "n_cc", 0) + 1):
            dsem[("cc", k)] = st.enter_context(nc.semaphore(f"x_cc_{k}"))
        self.n_sems = len(csem) + len(dsem)

        def semof(tok):
            if tok[0] == "c":
                return csem[(tok[1], tok[2])], tok[3]
            return dsem[(tok[1], tok[2])], tok[3]

        per_eng = {e: [] for e in ENGS}
        for op in self.ops:
            per_eng[op.eng].append(op)

        def run(eng_name, eng):
            waited = {}
            for op in per_eng[eng_name]:
                need = {}
                for d in op.deps:
                    if d.tok is None:
                        continue
                    if (not d.is_dma) and d.eng == "pe" and eng_name == "pe" and not op.is_dma:
                        continue
                    key = d.tok[:3]
                    v = d.tok[3]
                    if d.tok[0] == "c":
                        ek = ("c", d.tok[1])
                        cur = need.get(ek)
                        cand = (d.tok[2], v)
                        if cur is None or cand > cur:
                            need[ek] = cand
                    else:
                        cur = need.get(key)
                        if cur is None or v > cur:
                            need[key] = v
                for k, v in need.items():
                    if k[0] == "c":
                        w = waited.get(k)
                        if w is not None and w >= v:
                            continue
                        waited[k] = v
                        eng.wait_ge(csem[(k[1], v[0])], v[1])
                    else:
                        w = waited.get(k)
                        if w is not None and w >= v:
                            continue
                        waited[k] = v
                        eng.wait_ge(dsem[(k[1], k[2])], v)
                ins = op.fn(eng)
                if op.is_dma and op.tok[0] == "x":
                    s, _ = semof(op.tok)
                    ins.then_inc(s)
                elif op.is_dma:
                    s, _ = semof(op.tok)
                    ins.then_inc(s, 16)
                elif op.sig:
                    s, _ = semof(op.tok)
                    ins.then_inc(s, 1)

        with nc.Block() as block:
            @block.tensor
            def _(e):
                run("pe", e)

            @block.scalar
            def _(e):
                run("act", e)

            @block.vector
            def _(e):
                run("dve", e)

            @block.gpsimd
            def _(e):
                run("pool", e)

            @block.sync
            def _(e):
                run("sp", e)

    def scope(self):
        return Scope(self)

    def finish(self, eng="sp"):
        self.barrier()
        extra = list(getattr(self, "cc_ops", []))
        for sl in self.dma_slots_nb.values():
            extra.extend(o for o in sl if o is not None)
        self.barrier_deps[eng] = self.barrier_deps[eng] + extra
        self.op(eng, lambda e: e.nop() if hasattr(e, "nop") else e.engine_nop())


class Tile:
    __slots__ = ("t", "b")

    def __init__(self, t, name=""):
        self.t = t
        self.b = Buf(name)


class Ring:
    def __init__(self, tiles):
        self.tiles = tiles
        self.i = 0

    def next(self):
        t = self.tiles[self.i % len(self.tiles)]
        self.i += 1
        return t


class Scope:
    _uid = 0

    def __init__(self, P):
        self.P = P
        self.st = ExitStack()

    def __enter__(self):
        self.st.__enter__()
        return self

    def __exit__(self, *a):
        self.P.barrier()
        return self.st.__exit__(*a)

    def _nm(self, name):
        Scope._uid += 1
        return f"{name}_{Scope._uid}"

    def sbuf(self, name, shape, dtype):
        return Tile(self.P.sbuf(self._nm(name), shape, dtype, st=self.st), name)

    def psum(self, name, shape, dtype):
        return Tile(self.P.psum(self._nm(name), shape, dtype, st=self.st), name)

    def ring(self, name, n, shape, dtype, psum=False):
        f = self.psum if psum else self.sbuf
        return Ring([f(f"{name}{i}", shape, dtype) for i in range(n)])


import math
import numpy as np
import ml_dtypes

I32 = mybir.dt.int32
D = 4096
T = 1024
S = 2048
NL = 2
INW = 13312
EPS = 1e-6
NB = 512
XA_NCH = 7
XA_ROWS = XA_NCH * 1024 + 256
XB_NCH = 3
XB_ROWS = XB_NCH * 1024
PAIRS = [[0, 1], [2, 3], [4, 5], [6, 7]]


def xrow(s, ro):
    return (ro // 512) * 1024 + s * 512 + (ro % 512)


_ME = {}


def me_idx(e):
    key = id(e)
    if key not in _ME:
        _ME[key] = e.snap(e.partition_id() % 2)
    return _ME[key]


def gsrc(e, Glist, r, ro, n, cols=slice(None)):
    k, w0 = ro // 512, ro % 512
    return Glist[k][r * 512 + w0:r * 512 + w0 + n, cols]


class Exchange:
    def __init__(self, P, DR, src, srcname, dstname, nch, small_rows=0):
        self.P, self.DR, self.src, self.srcname, self.dstname = P, DR, src, srcname, dstname
        self.n = nch + (1 if small_rows else 0)
        self.rows = [1024 if k < nch else small_rows for k in range(self.n)]
        self.g = [DR.get(f"{dstname}g_{k}", [2 * self.rows[k], T], BF16) for k in range(self.n)]
        self.sel = [DR.get(f"{dstname}_{k}", [self.rows[k], T], BF16) for k in range(self.n)]
        self.issued = set()

    def cc(self, k):
        if k in self.issued:
            return
        self.issued.add(k)
        P, DR = self.P, self.DR
        sl = self.src[k * 1024:k * 1024 + self.rows[k], :]
        g = self.g[k]
        P.cc(lambda e: e.collective_compute("AllGather", ALU.bypass, replica_groups=PAIRS, ins=[sl.opt()], outs=[g.opt()]),
             reads=[DR.chunkbuf(self.srcname, k)], writes=[DR.buf(f"{self.dstname}g_{k}")])

    def finish(self):
        P, DR = self.P, self.DR
        for k in range(self.n):
            self.cc(k)
        for k in range(self.n):
            g, sel = self.g[k], self.sel[k]
            P.dma(lambda e, g=g, sel=sel: e.dma_start(out=sel.rearrange("(r s w) c -> r s w c", r=2, s=1),
                                                      in_=g.rearrange("(r s w) c -> r s w c", r=2, s=2)[:, bass.ds(me_idx(e), 1), :, :]),
                  reads=[DR.buf(f"{self.dstname}g_{k}")], writes=[DR.buf(f"{self.dstname}_{k}")], q="pool", nobar=True)
        return self.sel


SB_SCALE = 1.0 / math.sqrt(128.0)
XA_SCALE = 1.0 / math.sqrt(1024.0)
TWO_PI = 2.0 * math.pi


class Dram:
    def __init__(self, nc, io):
        self.nc = nc
        self.io = io
        self.t = {}
        self.b = {}

    def get(self, name, shape=None, dtype=None):
        if name not in self.t:
            kind = {"in": "ExternalInput", "out": "ExternalOutput"}.get(self.io.get(name), "Internal")
            self.t[name] = self.nc.dram_tensor(name, list(shape), dtype, kind=kind).ap()
            self.b[name] = Buf(name)
        return self.t[name]

    def buf(self, name):
        return self.b[name]

    def chunkbuf(self, name, k):
        key = f"{name}#{k}"
        if key not in self.b:
            self.b[key] = Buf(key)
        return self.b[key]


def mm(ps, lhsT, rhs, start, stop):
    return lambda e: e.matmul(ps, lhsT=lhsT, rhs=rhs, start=start, stop=stop)


def load_consts(P, Sg, DR):
    C = {}
    def ld(name, shape, dtype):
        t = Sg.sbuf(name, shape, dtype)
        src = DR.get(name, shape, dtype)
        P.dma(lambda e: e.dma_start(out=t.t[:], in_=src), writes=[t.b])
        C[name] = t
    ld("c_ident", [128, 128], BF16)
    ld("c_ones_f", [128, 128], F32)
    ld("c_ones_b", [128, 128], BF16)
    ld("c_triU", [128, 128], BF16)
    ld("c_triLE", [128, 128], BF16)
    ld("c_triI", [128, 128], BF16)
    ld("c_ntriI", [128, 128], BF16)
    ld("c_zero_b", [128, 128], BF16)
    ld("c_mask", [128, 4, 512], BF16)
    ld("c_iota", [128, 128], F32)
    ld("c_iotap", [128, 1], F32)
    return C


def rms_rstd(P, ss, rstd, n, reads_extra=()):
    P.op("act", lambda e: e.activation(out=rstd.t[:], in_=ss.t[:], func=AF.Ln, scale=1.0 / n, bias=EPS),
         reads=[ss.b], writes=[rstd.b])
    P.op("act", lambda e: e.activation(out=rstd.t[:], in_=rstd.t[:], func=AF.Exp, scale=-0.5),
         reads=[rstd.b], writes=[rstd.b])


EPS_AP = [None]


def norm_tile_to_T(P, Sc, C, x_t, g_rep, hT, tt, ps_ring, h_ring, ss_ring, nT):
    ss = ss_ring.next()
    rstd = ss_ring.next()
    hb = h_ring.next()
    P.op("act", lambda e: e.activation(out=hb.t[:], in_=x_t.t[:], func=AF.Square, accum_out=ss.t[:]),
         reads=[x_t.b], writes=[hb.b, ss.b])
    rms_rstd(P, ss, rstd, D)
    P.op("dve", lambda e: e.scalar_tensor_tensor(out=hb.t[:], in0=x_t.t[:], scalar=rstd.t[:], in1=g_rep.t[:],
                                                 op0=ALU.mult, op1=ALU.mult),
         reads=[x_t.b, rstd.b, g_rep.b], writes=[hb.b])
    for g4 in range(4):
        ps = ps_ring.next()
        for j in range(8):
            kc = g4 * 8 + j
            P.op("pe", lambda e, ps=ps, j=j, kc=kc: e.transpose(ps.t[:, j * 128:(j + 1) * 128], hb.t[:, kc * 128:(kc + 1) * 128], C["c_ident"].t[:]),
                 reads=[hb.b, C["c_ident"].b], writes=[ps.b])
        eng = "act" if g4 % 2 == 0 else "dve"
        dst = hT.t[:, g4 * 8:(g4 + 1) * 8, tt * 128:(tt + 1) * 128]
        src = ps.t[:].rearrange("p (j t) -> p j t", j=8)
        if eng == "act":
            P.op("act", lambda e, dst=dst, src=src: e.copy(out=dst, in_=src), reads=[ps.b], writes=[hT.b])
        else:
            P.op("dve", lambda e, dst=dst, src=src: e.tensor_copy(out=dst, in_=src), reads=[ps.b], writes=[hT.b])


PENDING_CC = []


def w_stream(P, wring, W2d, n0, n1, nk=32):
    wv = W2d.rearrange("(kc p) n -> p kc n", p=128)
    for nb in range(n0, n1, NB):
        for pc in list(PENDING_CC):
            pc[0] -= 1
            if pc[0] <= 0:
                pc[1]()
                PENDING_CC.remove(pc)
        wt = wring.next()
        step = 8
        for j in range(0, nk, step):
            P.dma(lambda e, wt=wt, j=j, nb=nb: e.dma_start(out=wt.t[:, j:j + step, :], in_=wv[:, j:j + step, nb:nb + NB]),
                  writes=[wt.b], q="pool")
        yield nb, wt


def evac(P, i, dst, src, reads, writes):
    if i % 2 == 0:
        P.op("act", lambda e: e.copy(out=dst, in_=src), reads=reads, writes=writes)
    else:
        P.op("dve", lambda e: e.tensor_copy(out=dst, in_=src), reads=reads, writes=writes)


def proj_feat(P, wt, actT, nT, nk, ps_ring, sink, cnt=[0]):
    for j in range(NB // 128):
        for tb in range(nT // 512):
            ps = ps_ring.next()
            for kc in range(nk):
                P.op("pe", mm(ps.t[:], wt.t[:, kc, j * 128:(j + 1) * 128], actT.t[:, kc, tb * 512:(tb + 1) * 512], kc == 0, kc == nk - 1),
                     reads=[wt.b, actT.b], writes=[ps.b])
            sink(j, tb, ps)


def proj_tok(P, wt, actT, nT, nk, ps_ring, sink, ncols=NB):
    for tt in range(nT // 128):
        ps = ps_ring.next()
        for kc in range(nk):
            P.op("pe", mm(ps.t[:, 0:ncols], actT.t[:, kc, tt * 128:(tt + 1) * 128], wt.t[:, kc, 0:ncols], kc == 0, kc == nk - 1),
                 reads=[wt.b, actT.b], writes=[ps.b])
        sink(tt, ps)


def phase_A(P, DR, C, AT, l, xname, xch):
    nc = P.nc
    xin = DR.get(xname, [T, D], F32)
    XA = DR.get(f"XA{l}", [XA_ROWS, T], BF16)
    aT = DR.get(f"aT{l}", [3072, T], BF16)
    bgT = DR.get(f"bgT{l}", [2048, T], BF16)
    cgT = DR.get(f"cgT{l}", [1024, T], BF16)
    w_in = DR.get("w_in", [NL, D, INW], F32)
    g_d = DR.get("pre_norm_g", [NL, D], F32)
    mem = DR.get("mem", [256, D], F32)
    kTm = DR.get("kTm", [D, 256], BF16); kTmb = DR.buf("kTm")
    vmg = DR.get("vmg", [512, 2048], BF16); vmgb = DR.buf("vmg")
    kTh = DR.get("kTh", [2048, 256], BF16); kThb = DR.buf("kTh")
    vmh = DR.get("vmh", [256, 2048], BF16); vmhb = DR.buf("vmh")
    wk = DR.get("xa_wk_h", [NL, D, 2048], F32); wv = DR.get("xa_wv_h", [NL, D, 2048], F32)
    SA = P.scope()
    SA.__enter__()
    AT = SA.sbuf("AT", [128, 32, T], BF16)
    mT = SA.sbuf("mT", [128, 32, 256], BF16)
    with P.scope() as Sc:
        g_rep = Sc.sbuf("g_rep", [128, D], F32)
        P.dma(lambda e: e.dma_start(out=g_rep.t[:], in_=g_d[l:l + 1, :].partition_broadcast(128)), writes=[g_rep.b])
        gm_rep = grep_load(P, Sc, DR, "xa_mem_g", l)
        xr = Sc.ring("x", 3, [128, D], F32)
        hr = Sc.ring("hb", 3, [128, D], BF16)
        ssr = Sc.ring("ss", 8, [128, 1], F32)
        pst = Sc.ring("pst", 2, [128, 1024], BF16, psum=True)
        for mt in range(2):
            xt = xr.next()
            P.dma(lambda e, xt=xt, mt=mt: e.dma_start(out=xt.t[:], in_=mem[mt * 128:(mt + 1) * 128, :]), writes=[xt.b])
            norm_tile_to_T(P, Sc, C, xt, gm_rep, mT, mt, pst, hr, ssr, 256)
        for tt in range(T // 128):
            xt = xr.next()
            P.dma(lambda e, xt=xt, tt=tt: e.dma_start(out=xt.t[:], in_=xin[tt * 128:(tt + 1) * 128, :]),
                  reads=[DR.buf(xname)], writes=[xt.b])
            norm_tile_to_T(P, Sc, C, xt, g_rep, AT, tt, pst, hr, ssr, T)
    with P.scope() as Sc:
        wring = Sc.ring("w", 3, [128, 32, NB], BF16)
        psr = Sc.ring("ps", 6, [128, 512], F32, psum=True)
        stg = Sc.ring("stg", 4, [128, 512], BF16)
        ev = [0]
        segs = [(0, 3072, aT, None, 0, "feat"),
                (3072, 4096, XA, 0, 0, "feat"), (4096, 5120, XA, 1, 0, "feat"),
                (5120, 6144, XA, 0, 1024, "feat"), (6144, 7168, XA, 1, 1024, "feat"),
                (7168, 8192, XA, 0, 2048, "tok"), (8192, 9216, XA, 1, 2048, "tok"),
                (9216, 11264, bgT, None, 0, "feat"),
                (11264, 11776, XA, 0, 3072, "feat"), (11776, 12288, XA, 1, 3072, "feat"),
                (12288, 13312, cgT, None, 0, "feat")]
        names = {id(aT): f"aT{l}", id(XA): f"XA{l}", id(bgT): f"bgT{l}", id(cgT): f"cgT{l}"}
        def kv_tiles():
            for nb, wt in w_stream(P, wring, wk[l], 0, 2048):
                for j in range(4):
                    ps = psr.next()
                    for kc in range(32):
                        P.op("pe", mm(ps.t[:, 0:256], wt.t[:, kc, j * 128:(j + 1) * 128], mT.t[:, kc, :], kc == 0, kc == 31), reads=[wt.b, mT.b], writes=[ps.b])
                    st = stg.next(); ev[0] += 1
                    evac(P, ev[0], st.t[:, 0:256], ps.t[:, 0:256], [ps.b], [st.b])
                    row = nb + j * 128
                    P.dma(lambda e, st=st, row=row: e.dma_start(out=kTh[row:row + 128, :], in_=st.t[:, 0:256]), reads=[st.b], writes=[kThb])
                yield
            for nb, wt in w_stream(P, wring, wv[l], 0, 2048):
                def sink(tt_, ps, nb=nb):
                    st = stg.next(); ev[0] += 1
                    evac(P, ev[0], st.t[:], ps.t[:], [ps.b], [st.b])
                    P.dma(lambda e: e.dma_start(out=vmh[tt_ * 128:(tt_ + 1) * 128, nb:nb + NB], in_=st.t[:]), reads=[st.b], writes=[vmhb])
                proj_tok(P, wt, mT, 256, 32, psr, sink)
                yield
        kvg = kv_tiles()
        ntile = [0]
        halo_done = [False]

        def write_halo():
            hs = Sc.sbuf("halo", [128, 16, 32], BF16)
            P.dma(lambda e: e.dma_start(out=hs.t[:], in_=aT[0:2048, T - 32:T].rearrange("(c p) t -> p c t", p=128)),
                  reads=[DR.buf(f"aT{l}")], writes=[hs.b])
            for sh_ in range(2):
                base = XA_NCH * 1024 + sh_ * 128
                P.dma(lambda e, base=base: e.dma_start(out=XA[base:base + 128, 0:512].rearrange("p (c t) -> p c t", t=32), in_=hs.t[:]),
                      reads=[hs.b], writes=[DR.chunkbuf(f"XA{l}", XA_NCH)])
            PENDING_CC.append([5, (lambda: xch.cc(XA_NCH))])

        for si_, (c0, c1, dst, sh, r0, mode) in enumerate(segs):
            if si_ == 1:
                write_halo()
            if sh == 0 and si_ > 1 or (sh is None and si_ > 1):
                pass
            for nb, wt in w_stream(P, wring, w_in[l], c0, c1):
                if mode == "feat":
                    def sink(j, tb, ps, nb=nb, dst=dst, r0=r0, c0=c0, sh=sh):
                        st = stg.next()
                        ev[0] += 1
                        evac(P, ev[0], st.t[:], ps.t[:], [ps.b], [st.b])
                        ro = r0 + (nb - c0) + j * 128
                        if sh is None:
                            row, dbuf = ro, DR.buf(names[id(dst)])
                        else:
                            row, dbuf = xrow(sh, ro), DR.chunkbuf(f"XA{l}", ro // 512)
                        P.dma(lambda e: e.dma_start(out=dst[row:row + 128, tb * 512:(tb + 1) * 512], in_=st.t[:]),
                              reads=[st.b], writes=[dbuf])
                    proj_feat(P, wt, AT, T, 32, psr, sink)
                else:
                    def sink(tt, ps, nb=nb, c0=c0, r0=r0, sh=sh, dst=dst):
                        st = stg.next()
                        ev[0] += 1
                        evac(P, ev[0], st.t[:], ps.t[:], [ps.b], [st.b])
                        col = nb - c0
                        ro = r0 + tt * 128
                        row = xrow(sh, ro)
                        P.dma(lambda e: e.dma_start(out=dst[row:row + 128, col:col + 512], in_=st.t[:]),
                              reads=[st.b], writes=[DR.chunkbuf(f"XA{l}", ro // 512)])
                    proj_tok(P, wt, AT, T, 32, psr, sink)
                ntile[0] += 1
                if ntile[0] % 3 == 0:
                    next(kvg, None)
            if sh == 1:
                for k_ in range(r0 // 512, (r0 + (c1 - c0) + 511) // 512):
                    PENDING_CC.append([5, (lambda k_=k_: xch.cc(k_))])
        for _ in kvg:
            pass
        P.cc(lambda e: e.collective_compute("AllGather", ALU.bypass, replica_groups=PAIRS, ins=[kTh.opt()], outs=[kTm.opt()]),
             reads=[kThb], writes=[kTmb])
        P.cc(lambda e: e.collective_compute("AllGather", ALU.bypass, replica_groups=PAIRS, ins=[vmh.opt()], outs=[vmg.opt()]),
             reads=[vmhb], writes=[vmgb])
        for pc in list(PENDING_CC):
            pc[1]()
        PENDING_CC.clear()
    SA.__exit__(None, None, None)


def attn_item(P, C, qT, kT, vt, qb, zl, Ob, Sbufs, wk, h, XB, xbbuf, stg, ev):
    nkb = 4 * qb + 4
    first = True
    mask = C["c_mask"]
    Sx = Sbufs[0]
    P.op("pool", lambda e: e.memset(Sx.t[:], 0.0), writes=[Sx.b])
    P.op("pe", mm(Ob.t[:], C["c_zero_b"].t[:], mask.t[:, 0, :], True, False), reads=[C["c_zero_b"].b, mask.b], writes=[Ob.b])
    for kb in range(nkb - 1, -1, -1):
        diag = kb >= 4 * qb
        mi = kb - 4 * qb
        c0 = mi * 128 if diag else 0
        cs = slice(c0, 512)
        dsl = slice(c0, c0 + 128)
        Z = zl.next()
        P.op("pe", mm(Z.t[:, cs], kT.t[:, kb * 128:(kb + 1) * 128], qT.t[:, qb * 512 + c0:(qb + 1) * 512], True, True),
             reads=[kT.b, qT.b], writes=[Z.b])
        yield
        Ft, SP, Wt = wk["F"].next(), wk["SP"].next(), wk["W"].next()
        P.op("act", lambda e, Ft=Ft, Z=Z, cs=cs: e.activation(out=Ft.t[:, cs], in_=Z.t[:, cs], func=AF.Exp, scale=-SB_SCALE),
             reads=[Z.b], writes=[Ft.b])
        E2 = wk["E2"].next()
        P.op("act", lambda e, E2=E2, Z=Z, cs=cs: e.activation(out=E2.t[:, cs], in_=Z.t[:, cs], func=AF.Exp, scale=SB_SCALE),
             reads=[Z.b], writes=[E2.b])
        yield
        P.op("act", lambda e, Ft=Ft, cs=cs: e.activation(out=Ft.t[:, cs], in_=Ft.t[:, cs], func=AF.Ln, bias=1.0),
             reads=[Ft.b], writes=[Ft.b])
        yield
        P.op("act", lambda e, SP=SP, E2=E2, cs=cs: e.activation(out=SP.t[:, cs], in_=E2.t[:, cs], func=AF.Ln, bias=1.0),
             reads=[E2.b], writes=[SP.b])
        if diag:
            P.op("pool", lambda e, SP=SP, mi=mi, dsl=dsl: e.tensor_tensor(out=SP.t[:, dsl], in0=SP.t[:, dsl], in1=mask.t[:, mi, dsl], op=ALU.mult),
                 reads=[SP.b, mask.b], writes=[SP.b])
        yield
        Lt = zl.next()
        P.op("pe", mm(Lt.t[:, cs], C["c_triU"].t[:], SP.t[:, cs], True, first), reads=[SP.b, C["c_triU"].b], writes=[Lt.b])
        if not first:
            P.op("pe", mm(Lt.t[:, cs], C["c_ones_b"].t[:], Sx.t[:, cs], False, True), reads=[Sx.b, C["c_ones_b"].b], writes=[Lt.b])
        yield
        P.op("dve", lambda e, Ft=Ft, Lt=Lt, cs=cs: e.tensor_tensor(out=Ft.t[:, cs], in0=Lt.t[:, cs], in1=Ft.t[:, cs], op=ALU.add),
             reads=[Lt.b, Ft.b], writes=[Ft.b])
        yield
        P.op("act", lambda e, Wt=Wt, Ft=Ft, cs=cs: e.activation(out=Wt.t[:, cs], in_=Ft.t[:, cs], func=AF.Exp, scale=-1.0),
             reads=[Ft.b], writes=[Wt.b])
        if diag:
            P.op("pool", lambda e, Wt=Wt, mi=mi, dsl=dsl: e.tensor_tensor(out=Wt.t[:, dsl], in0=Wt.t[:, dsl], in1=mask.t[:, mi, dsl], op=ALU.mult),
                 reads=[Wt.b, mask.b], writes=[Wt.b])
        yield
        last = kb == 0
        if diag:
            P.op("pe", mm(Ob.t[:, dsl], vt.t[:, kb, :], Wt.t[:, dsl], False, last), reads=[vt.b, Wt.b], writes=[Ob.b])
            if c0 + 128 < 512:
                osl = slice(c0 + 128, 512)
                P.op("pe", mm(Ob.t[:, osl], vt.t[:, kb, :], Wt.t[:, osl], False, last), reads=[vt.b, Wt.b], writes=[Ob.b])
        else:
            P.op("pe", mm(Ob.t[:], vt.t[:, kb, :], Wt.t[:], False, last), reads=[vt.b, Wt.b], writes=[Ob.b])
        if kb > 0:
            P.op("pool", lambda e, SP=SP, cs=cs: e.tensor_tensor(out=Sx.t[:, cs], in0=Sx.t[:, cs], in1=SP.t[:, cs], op=ALU.add),
                 reads=[SP.b, Sx.b], writes=[Sx.b])
        first = False
        yield
    st = stg.next()
    ev[0] += 1
    evac(P, ev[0], st.t[:], Ob.t[:], [Ob.b], [st.b])
    r = qb // 2
    row = xrow(r, h * 128)
    col = (qb % 2) * 512
    P.dma(lambda e: e.dma_start(out=XB[row:row + 128, col:col + 512], in_=st.t[:]), reads=[st.b], writes=[xbbuf[(h * 128) // 512]])
    yield


NSLOT = 3


def attn_gen(P, DR, C, l, Sc, xch=None):
    GA = DR.GA[l]
    XB = DR.get(f"XB{l}", [XB_ROWS, T], BF16)
    qr = Sc.ring("qT", 2, [128, S], BF16)
    kr = Sc.ring("kT", 2, [128, S], BF16)
    vr = Sc.ring("v", 2, [128, 16, 128], BF16)
    wk = {"F": Sc.ring("F", 6, [128, 512], F32), "SP": Sc.ring("SP", 6, [128, 512], BF16), "W": Sc.ring("W", 6, [128, 512], BF16),
          "E2": Sc.ring("E2", 4, [128, 512], F32)}
    Sb = [[Sc.sbuf(f"S{i}_{j}", [128, 512], BF16) for j in range(2)] for i in range(NSLOT)]
    stg = Sc.ring("stg", 3, [128, 512], BF16)
    zl = Sc.ring("ZL", 3, [128, 512], F32, psum=True)
    Obs = [Sc.psum(f"O{i}", [128, 512], F32) for i in range(NSLOT)]
    ev = [0]
    xbb = [DR.chunkbuf(f"XB{l}", k) for k in range(3)]

    def items():
        for h in range(8):
            qT, kT, vt = qr.next(), kr.next(), vr.next()
            for r in range(2):
                P.dma(lambda e, qT=qT, r=r, h=h: e.dma_start(out=qT.t[:, r * 1024:(r + 1) * 1024], in_=gsrc(e, GA, r, h * 128, 128)),
                      reads=[DR.buf(f"GA{l}_{(h * 128) // 512}")], writes=[qT.b])
                P.dma(lambda e, kT=kT, r=r, h=h: e.dma_start(out=kT.t[:, r * 1024:(r + 1) * 1024], in_=gsrc(e, GA, r, 1024 + h * 128, 128)),
                      reads=[DR.buf(f"GA{l}_{(1024 + h * 128) // 512}")], writes=[kT.b])
                for vb in range(2):
                    P.dma(lambda e, vt=vt, r=r, vb=vb, h=h: e.dma_start(
                        out=vt.t[:, r * 8 + vb * 4:r * 8 + (vb + 1) * 4, :],
                        in_=gsrc(e, GA, r, 2048 + vb * 512, 512, slice(h * 128, (h + 1) * 128)).rearrange("(kt p) d -> p kt d", p=128)),
                        reads=[DR.buf(f"GA{l}_{4 + vb}")], writes=[vt.b])
            for qb in (3, 2, 1, 0):
                yield (lambda sl, qT=qT, kT=kT, vt=vt, qb=qb, h=h:
                       attn_item(P, C, qT, kT, vt, qb, zl, Obs[sl], Sb[sl], wk, h, XB, xbb, stg, ev)), h

    it = items()
    slots = [None] * NSLOT
    slot_head = [None] * NSLOT
    done = False
    left = {h: 4 for h in range(8)}
    while True:
        active = False
        for sl in range(NSLOT):
            if slots[sl] is None and not done:
                nxt = next(it, None)
                if nxt is None:
                    done = True
                else:
                    slots[sl] = nxt[0](sl)
                    slot_head[sl] = nxt[1]
            if slots[sl] is not None:
                active = True
                try:
                    next(slots[sl])
                except StopIteration:
                    slots[sl] = None
                    left[slot_head[sl]] -= 1
                    if xch is not None and all(left[h_] == 0 for h_ in range(4)):
                        xch.cc(0)
        if not active and done:
            break
        yield


def sincos(P, Sc, arg, N, tag):
    outs = []
    for which, shift in (("s", 0.0), ("c", math.pi / 2)):
        ki = Sc.sbuf(f"ki_{tag}{which}", [128, N], I32)
        kf = Sc.sbuf(f"kf_{tag}{which}", [128, N], F32)
        r = Sc.sbuf(f"r_{tag}{which}", [128, N], F32)
        o = Sc.sbuf(f"o_{tag}{which}", [128, N], F32)
        P.op("dve", lambda e, ki=ki, shift=shift: e.tensor_scalar(out=ki.t[:], in0=arg.t[:], scalar1=shift, scalar2=1.0 / TWO_PI,
                                                                  op0=ALU.add, op1=ALU.mult), reads=[arg.b], writes=[ki.b])
        P.op("dve", lambda e, ki=ki, kf=kf: e.tensor_copy(out=kf.t[:], in_=ki.t[:]), reads=[ki.b], writes=[kf.b])
        P.op("dve", lambda e, kf=kf, r=r: e.scalar_tensor_tensor(out=r.t[:], in0=kf.t[:], scalar=-TWO_PI, in1=arg.t[:],
                                                                op0=ALU.mult, op1=ALU.add), reads=[kf.b, arg.b], writes=[r.b])
        P.op("dve", lambda e, r=r, shift=shift: e.tensor_scalar(out=r.t[:], in0=r.t[:], scalar1=3.14159 - shift, scalar2=-3.14159 - shift,
                                                               op0=ALU.min, op1=ALU.max), reads=[r.b], writes=[r.b])
        sh_t = SHIFT_AP[which]
        P.op("act", lambda e, r=r, o=o, sh_t=sh_t: e.activation(out=o.t[:], in_=r.t[:], func=AF.Sin, bias=sh_t.t[:]),
             reads=[r.b, sh_t.b], writes=[o.b])
        outs.append(o)
    return outs


SHIFT_AP = {}
DBG = {}


def tt(P, eng, out, in0, in1, op, reads, writes):
    P.op(eng, lambda e: e.tensor_tensor(out=out, in0=in0, in1=in1, op=op), reads=reads, writes=writes)


def ssm_tables(P, DR, C, l, S0):
    NS = 2048
    d_rows = {n: DR.get(n, [NL, NS], F32) for n in ("s_lre_row", "s_lim_row", "s_ldt_row")}
    d_f = {n: DR.get(n, [NL, 128, 16], F32) for n in ("s_lre_f", "s_lim_f", "s_ldt_f")}
    d_B = {n: DR.get(n, [NL, 128, 4, 512], F32) for n in ("s_Bre_blk", "s_Bim_blk")}
    d_C = {n: DR.get(n, [NL, 128, 16, 128], F32) for n in ("s_Cre_blk", "s_Cim_blk")}
    d_d = DR.get("s_d", [NL, 128, 4], F32)
    if True:
        Tm_re = S0.sbuf("Tm_re", [128, NS], F32); nTm_im = S0.sbuf("nTm_im", [128, NS], F32)
        Tp_re = S0.sbuf("Tp_re", [128, 16, 128], F32); nTp_im = S0.sbuf("nTp_im", [128, 16, 128], F32)
        BDre = S0.sbuf("BDre", [128, 4, 512], BF16); BDim = S0.sbuf("BDim", [128, 4, 512], BF16)
        CBre = S0.sbuf("CBre", [128, 16, 128], BF16); CBim = S0.sbuf("CBim", [128, 16, 128], BF16); nCBim = S0.sbuf("nCBim", [128, 16, 128], BF16)
        L_re = S0.sbuf("L_re", [128, 16], F32); L_im = S0.sbuf("L_im", [128, 16], F32)
        dcol = S0.sbuf("dcol", [128, 4], F32)
        for which, val in (("s", 0.0), ("c", math.pi / 2)):
            t = S0.sbuf(f"shift_{which}", [128, 1], F32)
            P.op("dve", lambda e, t=t, val=val: e.memset(t.t[:], val), writes=[t.b])
            SHIFT_AP[which] = t
        P.dma(lambda e: e.dma_start(out=dcol.t[:], in_=d_d[l]), writes=[dcol.b])
        def row_pre(kq):
          with P.scope() as S1:
              NS_ = 512
              cq = slice(kq * 512, (kq + 1) * 512)
              def ldrow(name):
                  t = S1.sbuf(name, [128, NS_], F32)
                  P.dma(lambda e: e.dma_start(out=t.t[:], in_=d_rows[name][l:l + 1, cq].partition_broadcast(128)), writes=[t.b])
                  return t
              lre, lim, ldt = ldrow("s_lre_row"), ldrow("s_lim_row"), ldrow("s_ldt_row")
              dt = S1.sbuf("dt", [128, NS_], F32)
              P.op("act", lambda e: e.activation(out=dt.t[:], in_=ldt.t[:], func=AF.Exp), reads=[ldt.b], writes=[dt.b])
              a_r = S1.sbuf("a_r", [128, NS_], F32); b_r = S1.sbuf("b_r", [128, NS_], F32)
              tt(P, "dve", a_r.t[:], lre.t[:], dt.t[:], ALU.mult, [lre.b, dt.b], [a_r.b])
              tt(P, "dve", b_r.t[:], lim.t[:], dt.t[:], ALU.mult, [lim.b, dt.b], [b_r.b])
              niota = S1.sbuf("niota", [128, 1], F32)
              P.op("dve", lambda e: e.tensor_scalar(out=niota.t[:], in0=C["c_iotap"].t[:], scalar1=-1.0, scalar2=None, op0=ALU.mult),
                   reads=[C["c_iotap"].b], writes=[niota.b])
              arg = S1.sbuf("arg", [128, NS_], F32)
              P.op("dve", lambda e: e.tensor_scalar(out=arg.t[:], in0=b_r.t[:], scalar1=C["c_iotap"].t[:], scalar2=None, op0=ALU.mult),
                   reads=[b_r.b, C["c_iotap"].b], writes=[arg.b])
              sn, cs = sincos(P, S1, arg, NS_, "m")
              mag = S1.sbuf("mag", [128, NS_], F32)
              P.op("dve", lambda e: e.tensor_scalar(out=mag.t[:], in0=a_r.t[:], scalar1=niota.t[:], scalar2=None, op0=ALU.mult),
                   reads=[a_r.b, niota.b], writes=[mag.b])
              P.op("act", lambda e: e.activation(out=mag.t[:], in_=mag.t[:], func=AF.Exp), reads=[mag.b], writes=[mag.b])
              tt(P, "dve", Tm_re.t[:, cq], mag.t[:], cs.t[:], ALU.mult, [mag.b, cs.b], [Tm_re.b])
              tt(P, "dve", nTm_im.t[:, cq], mag.t[:], sn.t[:], ALU.mult, [mag.b, sn.b], [nTm_im.b])
              sn1, cs1 = sincos(P, S1, b_r, NS_, "one")
              m1 = S1.sbuf("m1", [128, NS_], F32)
              P.op("act", lambda e: e.activation(out=m1.t[:], in_=a_r.t[:], func=AF.Exp), reads=[a_r.b], writes=[m1.b])
              xr_ = S1.sbuf("xr", [128, NS_], F32); yr_ = S1.sbuf("yr", [128, NS_], F32)
              tt(P, "dve", xr_.t[:], m1.t[:], cs1.t[:], ALU.mult, [m1.b, cs1.b], [xr_.b])
              P.op("dve", lambda e: e.tensor_scalar(out=xr_.t[:], in0=xr_.t[:], scalar1=-1.0, scalar2=None, op0=ALU.add), reads=[xr_.b], writes=[xr_.b])
              tt(P, "dve", yr_.t[:], m1.t[:], sn1.t[:], ALU.mult, [m1.b, sn1.b], [yr_.b])
              den = S1.sbuf("den", [128, NS_], F32); t1 = S1.sbuf("t1", [128, NS_], F32); t2 = S1.sbuf("t2", [128, NS_], F32)
              tt(P, "dve", den.t[:], lre.t[:], lre.t[:], ALU.mult, [lre.b], [den.b])
              tt(P, "dve", t1.t[:], lim.t[:], lim.t[:], ALU.mult, [lim.b], [t1.b])
              tt(P, "dve", den.t[:], den.t[:], t1.t[:], ALU.add, [den.b, t1.b], [den.b])
              P.op("dve", lambda e: e.reciprocal(out=den.t[:], in_=den.t[:]), reads=[den.b], writes=[den.b])
              cre = S1.sbuf("cre", [128, NS_], F32); cim = S1.sbuf("cim", [128, NS_], F32)
              tt(P, "dve", t1.t[:], xr_.t[:], lre.t[:], ALU.mult, [xr_.b, lre.b], [t1.b])
              tt(P, "dve", t2.t[:], yr_.t[:], lim.t[:], ALU.mult, [yr_.b, lim.b], [t2.b])
              tt(P, "dve", t1.t[:], t1.t[:], t2.t[:], ALU.add, [t1.b, t2.b], [t1.b])
              tt(P, "dve", cre.t[:], t1.t[:], den.t[:], ALU.mult, [t1.b, den.b], [cre.b])
              tt(P, "dve", t1.t[:], yr_.t[:], lre.t[:], ALU.mult, [yr_.b, lre.b], [t1.b])
              tt(P, "dve", t2.t[:], xr_.t[:], lim.t[:], ALU.mult, [xr_.b, lim.b], [t2.b])
              tt(P, "dve", t1.t[:], t1.t[:], t2.t[:], ALU.subtract, [t1.b, t2.b], [t1.b])
              tt(P, "dve", cim.t[:], t1.t[:], den.t[:], ALU.mult, [t1.b, den.b], [cim.b])
              Bre = S1.sbuf("Bre", [128, NS_], F32); Bim = S1.sbuf("Bim", [128, NS_], F32)
              P.dma(lambda e: e.dma_start(out=Bre.t[:], in_=d_B["s_Bre_blk"][l][:, kq, :]), writes=[Bre.b])
              P.dma(lambda e: e.dma_start(out=Bim.t[:], in_=d_B["s_Bim_blk"][l][:, kq, :]), writes=[Bim.b])
              BDre_f = BDre.t[:, kq, :]; BDim_f = BDim.t[:, kq, :]
              tt(P, "dve", t1.t[:], cre.t[:], Bre.t[:], ALU.mult, [cre.b, Bre.b], [t1.b])
              tt(P, "dve", t2.t[:], cim.t[:], Bim.t[:], ALU.mult, [cim.b, Bim.b], [t2.b])
              tt(P, "dve", BDre_f, t1.t[:], t2.t[:], ALU.subtract, [t1.b, t2.b], [BDre.b])
              tt(P, "dve", t1.t[:], cre.t[:], Bim.t[:], ALU.mult, [cre.b, Bim.b], [t1.b])
              tt(P, "dve", t2.t[:], cim.t[:], Bre.t[:], ALU.mult, [cim.b, Bre.b], [t2.b])
              tt(P, "dve", BDim_f, t1.t[:], t2.t[:], ALU.add, [t1.b, t2.b], [BDim.b])
        for kq_ in range(4):
            row_pre(kq_)
        with P.scope() as S2:
            def ldf(name):
                t = S2.sbuf(name, [128, 16], F32)
                P.dma(lambda e: e.dma_start(out=t.t[:], in_=d_f[name][l]), writes=[t.b])
                return t
            lre, lim, ldt = ldf("s_lre_f"), ldf("s_lim_f"), ldf("s_ldt_f")
            dt = S2.sbuf("dtf", [128, 16], F32)
            P.op("act", lambda e: e.activation(out=dt.t[:], in_=ldt.t[:], func=AF.Exp), reads=[ldt.b], writes=[dt.b])
            a_f = S2.sbuf("a_f", [128, 16], F32); b_f = S2.sbuf("b_f", [128, 16], F32)
            tt(P, "dve", a_f.t[:], lre.t[:], dt.t[:], ALU.mult, [lre.b, dt.b], [a_f.b])
            tt(P, "dve", b_f.t[:], lim.t[:], dt.t[:], ALU.mult, [lim.b, dt.b], [b_f.b])
            argp = S2.sbuf("argp", [128, 16 * 128], F32); ea = S2.sbuf("ea", [128, 16 * 128], F32)
            for i in range(16):
                P.op("dve", lambda e, i=i: e.tensor_scalar(out=argp.t[:, i * 128:(i + 1) * 128], in0=C["c_iota"].t[:], scalar1=b_f.t[:, i:i + 1], scalar2=None, op0=ALU.mult),
                     reads=[b_f.b, C["c_iota"].b], writes=[argp.b])
                P.op("dve", lambda e, i=i: e.tensor_scalar(out=ea.t[:, i * 128:(i + 1) * 128], in0=C["c_iota"].t[:], scalar1=a_f.t[:, i:i + 1], scalar2=None, op0=ALU.mult),
                     reads=[a_f.b, C["c_iota"].b], writes=[ea.b])
            snp, csp = sincos(P, S2, argp, 2048, "p")
            P.op("act", lambda e: e.activation(out=ea.t[:], in_=ea.t[:], func=AF.Exp), reads=[ea.b], writes=[ea.b])
            Tp_re_f = Tp_re.t[:].rearrange("p i t -> p (i t)"); nTp_im_f = nTp_im.t[:].rearrange("p i t -> p (i t)")
            tt(P, "dve", Tp_re_f, ea.t[:], csp.t[:], ALU.mult, [ea.b, csp.b], [Tp_re.b])
            P.op("dve", lambda e: e.scalar_tensor_tensor(out=nTp_im_f, in0=ea.t[:], scalar=-1.0, in1=snp.t[:], op0=ALU.mult, op1=ALU.mult),
                 reads=[ea.b, snp.b], writes=[nTp_im.b])
            a128 = S2.sbuf("a128", [128, 16], F32); b128 = S2.sbuf("b128", [128, 16], F32)
            P.op("dve", lambda e: e.tensor_scalar(out=b128.t[:], in0=b_f.t[:], scalar1=128.0, scalar2=None, op0=ALU.mult), reads=[b_f.b], writes=[b128.b])
            sL, cL = sincos(P, S2, b128, 16, "L")
            P.op("act", lambda e: e.activation(out=a128.t[:], in_=a_f.t[:], func=AF.Exp, scale=128.0), reads=[a_f.b], writes=[a128.b])
            tt(P, "dve", L_re.t[:], a128.t[:], cL.t[:], ALU.mult, [a128.b, cL.b], [L_re.b])
            tt(P, "dve", L_im.t[:], a128.t[:], sL.t[:], ALU.mult, [a128.b, sL.b], [L_im.b])
            for nm, dst in (("s_Cre_blk", CBre), ("s_Cim_blk", CBim)):
                tmp = S2.sbuf("ctmp" + nm, [128, 16, 128], F32)
                P.dma(lambda e, tmp=tmp, nm=nm: e.dma_start(out=tmp.t[:], in_=d_C[nm][l]), writes=[tmp.b])
                P.op("dve", lambda e, tmp=tmp, dst=dst: e.tensor_copy(out=dst.t[:], in_=tmp.t[:]), reads=[tmp.b], writes=[dst.b])
                if nm == "s_Cim_blk":
                    P.op("dve", lambda e, tmp=tmp: e.tensor_scalar(out=nCBim.t[:], in0=tmp.t[:], scalar1=-1.0, scalar2=None, op0=ALU.mult), reads=[tmp.b], writes=[nCBim.b])
    return dict(Tm_re=Tm_re, nTm_im=nTm_im, Tp_re=Tp_re, nTp_im=nTp_im, BDre=BDre, BDim=BDim, CBre=CBre, CBim=CBim, nCBim=nCBim,
                L_re=L_re, L_im=L_im, dcol=dcol)


def ssm_gen(P, DR, C, l, S3, tb, Ba, Bb):
    GA = DR.GA[l]
    XB = DR.get(f"XB{l}", [XB_ROWS, T], BF16)
    xbb = DR.chunkbuf(f"XB{l}", 2)
    Tm_re, nTm_im, Tp_re, nTp_im = tb["Tm_re"], tb["nTm_im"], tb["Tp_re"], tb["nTp_im"]
    BDre, BDim, CBre, CBim, nCBim, L_re, L_im, dcol = tb["BDre"], tb["BDim"], tb["CBre"], tb["CBim"], tb["nCBim"], tb["L_re"], tb["L_im"], tb["dcol"]
    uT = S3.sbuf("uT", [128, 4, S], BF16)
    for r in range(2):
        P.dma(lambda e, r=r: e.dma_start(out=uT.t[:, :, r * 1024:(r + 1) * 1024],
                                         in_=gsrc(e, GA, r, 3072, 512).rearrange("(kc p) t -> p kc t", p=128)),
              reads=[DR.buf(f"GA{l}_6")], writes=[uT.b])
    ysr = [S3.ring(f"ys{k}", 2, [128, 512], F32) for k in range(4)]
    qring = S3.ring("Q", 8, [128, 512], BF16)
    pring = S3.ring("Pp", 8, [128, 4, 128], BF16)
    cr = [[S3.sbuf(f"cr{k}_{i}", [128, 4], F32) for i in range(2)] for k in range(4)]
    ci = [[S3.sbuf(f"ci{k}_{i}", [128, 4], F32) for i in range(2)] for k in range(4)]
    tmp4 = S3.ring("tmp4", 12, [128, 4], F32)
    g1 = S3.ring("g1", 2, [128, 512], F32)
    gob = S3.ring("gob", 2, [128, 512], BF16)
    for k in range(4):
        P.op("dve", lambda e, k=k: e.memset(cr[k][0].t[:], 0.0), writes=[cr[k][0].b])
        P.op("dve", lambda e, k=k: e.memset(ci[k][0].t[:], 0.0), writes=[ci[k][0].b])
    triI, ntriI = C["c_triI"], C["c_ntriI"]
    Ga = Ba.t[:].rearrange("p (i t) -> p i t", i=4)
    Gb = Bb.t[:].rearrange("p (i t) -> p i t", i=4)
    ycur = [None] * 4
    for j in range(16):
        for kc in range(4):
            cur, nxt = j % 2, (j + 1) % 2
            cre_t, cim_t = cr[kc][cur], ci[kc][cur]
            P.op("pe", mm(Ba.t[:], uT.t[:, kc, j * 128:(j + 1) * 128], BDre.t[:, kc, :], True, True), reads=[uT.b, BDre.b], writes=[Ba.b])
            P.op("pe", mm(Bb.t[:], uT.t[:, kc, j * 128:(j + 1) * 128], BDim.t[:, kc, :], True, True), reads=[uT.b, BDim.b], writes=[Bb.b])
            yield
            Q1, Q2, Q3, Q4 = qring.next(), qring.next(), qring.next(), qring.next()
            sl = slice(kc * 512, (kc + 1) * 512)
            tt(P, "dve", Q1.t[:], Ba.t[:], Tm_re.t[:, sl], ALU.mult, [Ba.b, Tm_re.b], [Q1.b])
            tt(P, "dve", Q2.t[:], Bb.t[:], nTm_im.t[:, sl], ALU.mult, [Bb.b, nTm_im.b], [Q2.b])
            tt(P, "dve", Q3.t[:], Bb.t[:], Tm_re.t[:, sl], ALU.mult, [Bb.b, Tm_re.b], [Q3.b])
            tt(P, "dve", Q4.t[:], Ba.t[:], nTm_im.t[:, sl], ALU.mult, [Ba.b, nTm_im.b], [Q4.b])
            yield
            for i in range(4):
                cs_ = slice(i * 128, (i + 1) * 128)
                P.op("pe", mm(Ga[:, i, :], Q1.t[:, cs_], triI.t[:], True, False), reads=[Q1.b, triI.b], writes=[Ba.b])
                P.op("pe", mm(Ga[:, i, :], Q2.t[:, cs_], triI.t[:], False, True), reads=[Q2.b, triI.b], writes=[Ba.b])
                P.op("pe", mm(Gb[:, i, :], Q3.t[:, cs_], triI.t[:], True, False), reads=[Q3.b, triI.b], writes=[Bb.b])
                P.op("pe", mm(Gb[:, i, :], Q4.t[:, cs_], ntriI.t[:], False, True), reads=[Q4.b, ntriI.b], writes=[Bb.b])
            yield
            P1, P2, P3, P4 = pring.next(), pring.next(), pring.next(), pring.next()
            for i in range(4):
                ti = kc * 4 + i
                for (Pt, G, Gbuf, cc, tab) in ((P1, Ga, Ba, cre_t, Tp_re), (P2, Gb, Bb, cim_t, nTp_im), (P3, Ga, Ba, cre_t, nTp_im), (P4, Gb, Bb, cim_t, Tp_re)):
                    P.op("dve", lambda e, Pt=Pt, G=G, cc=cc, tab=tab, i=i, ti=ti: e.scalar_tensor_tensor(
                        out=Pt.t[:, i, :], in0=G[:, i, :], scalar=cc.t[:, i:i + 1], in1=tab.t[:, ti, :], op0=ALU.add, op1=ALU.mult),
                        reads=[Gbuf.b, cc.b, tab.b], writes=[Pt.b])
            if j < 15:
                gcr, gci, u1, u2 = tmp4.next(), tmp4.next(), tmp4.next(), tmp4.next()
                Lr = L_re.t[:, kc * 4:(kc + 1) * 4]; Li = L_im.t[:, kc * 4:(kc + 1) * 4]
                tt(P, "dve", gcr.t[:], Ga[:, :, 127], cre_t.t[:], ALU.add, [Ba.b, cre_t.b], [gcr.b])
                tt(P, "dve", gci.t[:], Gb[:, :, 127], cim_t.t[:], ALU.add, [Bb.b, cim_t.b], [gci.b])
                ncr, nci = cr[kc][nxt], ci[kc][nxt]
                tt(P, "dve", u1.t[:], gcr.t[:], Lr, ALU.mult, [gcr.b, L_re.b], [u1.b])
                tt(P, "dve", u2.t[:], gci.t[:], Li, ALU.mult, [gci.b, L_im.b], [u2.b])
                tt(P, "dve", ncr.t[:], u1.t[:], u2.t[:], ALU.subtract, [u1.b, u2.b], [ncr.b])
                u3, u4 = tmp4.next(), tmp4.next()
                tt(P, "dve", u3.t[:], gci.t[:], Lr, ALU.mult, [gci.b, L_re.b], [u3.b])
                tt(P, "dve", u4.t[:], gcr.t[:], Li, ALU.mult, [gcr.b, L_im.b], [u4.b])
                tt(P, "dve", nci.t[:], u3.t[:], u4.t[:], ALU.add, [u3.b, u4.b], [nci.b])
            yield
            Y = Ba.t[:, 0:128]
            n = 0
            for i in range(4):
                ti = kc * 4 + i
                for (Cm, Pt) in ((CBre, P1), (CBre, P2), (CBim, P3), (nCBim, P4)):
                    P.op("pe", mm(Y, Cm.t[:, ti, :], Pt.t[:, i, :], n == 0, n == 15), reads=[Cm.b, Pt.b], writes=[Ba.b])
                    n += 1
            yield
            if j % 4 == 0:
                ycur[kc] = ysr[kc].next()
            yt = ycur[kc]
            jj = j % 4
            P.op("dve", lambda e, yt=yt, kc=kc, j=j, jj=jj: e.scalar_tensor_tensor(
                out=yt.t[:, jj * 128:(jj + 1) * 128], in0=uT.t[:, kc, j * 128:(j + 1) * 128], scalar=dcol.t[:, kc:kc + 1],
                in1=Ba.t[:, 0:128], op0=ALU.mult, op1=ALU.add), reads=[uT.b, dcol.b, Ba.b], writes=[yt.b])
            if jj == 3:
                a_ = g1.next(); o = gob.next()
                y = yt.t[:]
                tt(P, "dve", a_.t[:], y, y, ALU.mult, [yt.b], [a_.b])
                P.op("dve", lambda e, a_=a_: e.tensor_scalar(out=a_.t[:], in0=a_.t[:], scalar1=0.044715, scalar2=1.0, op0=ALU.mult, op1=ALU.add), reads=[a_.b], writes=[a_.b])
                tt(P, "dve", a_.t[:], a_.t[:], y, ALU.mult, [a_.b, yt.b], [a_.b])
                P.op("act", lambda e, a_=a_: e.activation(out=a_.t[:], in_=a_.t[:], func=AF.Sigmoid, scale=1.5957691216057308), reads=[a_.b], writes=[a_.b])
                tt(P, "dve", o.t[:], a_.t[:], y, ALU.mult, [a_.b, yt.b], [o.b])
                t0 = (j - 3) * 128
                r = t0 // 1024
                row = xrow(r, 1024 + kc * 128)
                col = t0 % 1024
                P.dma(lambda e, o=o, row=row, col=col: e.dma_start(out=XB[row:row + 128, col:col + 512], in_=o.t[:]), reads=[o.b], writes=[xbb])
            yield


def phase_B(P, DR, C, l, xch):
    with P.scope() as S0:
        tb = ssm_tables(P, DR, C, l, S0)
        with P.scope() as Sc:
            Ba = Sc.psum("ssmA", [128, 512], F32)
            Bb = Sc.psum("ssmB", [128, 512], F32)
            ga = attn_gen(P, DR, C, l, Sc, xch)
            gs = ssm_gen(P, DR, C, l, Sc, tb, Ba, Bb)
            a_live = s_live = True
            rnd = 0
            while a_live or s_live:
                rnd += 1
                if a_live:
                    try:
                        next(ga)
                    except StopIteration:
                        a_live = False
                if s_live and (rnd % 2 == 0 or not a_live):
                    try:
                        next(gs)
                    except StopIteration:
                        s_live = False
                        xch.cc(2)


def grep_load(P, Sc, DR, name, l):
    g_d = DR.get(name, [NL, D], F32)
    t = Sc.sbuf("g_" + name, [128, D], F32)
    P.dma(lambda e: e.dma_start(out=t.t[:], in_=g_d[l:l + 1, :].partition_broadcast(128)), writes=[t.b])
    return t


def row_rstd(P, ps_pair, n, out_t):
    for hb in range(2):
        P.op("act", lambda e, hb=hb: e.activation(out=out_t.t[:, hb * 512:(hb + 1) * 512], in_=ps_pair[hb].t[:], func=AF.Ln, scale=1.0 / n, bias=EPS),
             reads=[ps_pair[hb].b], writes=[out_t.b])
    P.op("act", lambda e: e.activation(out=out_t.t[:], in_=out_t.t[:], func=AF.Exp, scale=-0.5), reads=[out_t.b], writes=[out_t.b])


def sumsq_acc(P, C, ps_pair, src_bf, sqr, first, last, extra_reads):
    sq = sqr.next()
    P.op("act", lambda e: e.activation(out=sq.t[:], in_=src_bf, func=AF.Square), reads=extra_reads, writes=[sq.b])
    for hb in range(2):
        P.op("pe", mm(ps_pair[hb].t[:], C["c_ones_b"].t[:], sq.t[:, hb * 512:(hb + 1) * 512], first, last),
             reads=[sq.b, C["c_ones_b"].b], writes=[ps_pair[hb].b])


def phase_C(P, DR, C, AT_unused, l, xname, xoname):
    GA = DR.GA[l]
    GB = DR.GB[l]
    aT = DR.get(f"aT{l}", [3072, T], BF16); aTb = DR.buf(f"aT{l}")
    bgT = DR.get(f"bgT{l}", [2048, T], BF16); bgb = DR.buf(f"bgT{l}")
    cgT = DR.get(f"cgT{l}", [1024, T], BF16); cgb = DR.buf(f"cgT{l}")
    xin = DR.get(xname, [T, D], F32); xinb = DR.buf(xname)
    xo = DR.get(xoname, [T, D], F32); xob = DR.buf(xoname)
    mem = DR.get("mem", [256, D], F32)
    r1 = DR.get("r1", [T, D], F32); r1b = DR.buf("r1")
    x1 = DR.get("x1", [T, D], F32); x1b = DR.buf("x1")
    qTd = DR.get("qTd", [D, T], BF16); qTb = DR.buf("qTd")
    kTm = DR.get("kTm", [D, 256], BF16); kTmb = DR.buf("kTm")
    vmg = DR.get("vmg", [512, 2048], BF16); vmgb = DR.buf("vmg")
    d_convw = DR.get("conv_wT", [NL, 128, 8, 31], F32)
    d_cb = DR.get("conv_b_l", [NL, 128, 8], F32); d_lg = DR.get("conv_ln_g_l", [NL, 128, 8], F32); d_lb = DR.get("conv_ln_b_l", [NL, 128, 8], F32)
    d_bg = DR.get("branch_g_l", [NL, 128, 32], F32); d_glub = DR.get("glu_b_l", [NL, 128, 8], F32)
    d_hs = DR.get("halo_scale", [128, 1], F32)
    w_out = DR.get("w_out", [NL, D, D], F32); wq = DR.get("xa_wq", [NL, D, D], F32)
    wo = DR.get("xa_wo", [NL, D, D], F32); gluw = DR.get("ssm_glu_w", [NL, 1024, 1024], F32)

    SA = P.scope()
    SA.__enter__()
    AT = SA.sbuf("AT", [128, 32, T], BF16)
    with P.scope() as Sc:
        cw = Sc.sbuf("cw", [128, 8, 31], F32); cb = Sc.sbuf("cb", [128, 8], F32); lg = Sc.sbuf("lg", [128, 8], F32); lb = Sc.sbuf("lb", [128, 8], F32)
        bgn = Sc.sbuf("bgn", [128, 32], F32); glub = Sc.sbuf("glub", [128, 8], F32); hsc = Sc.sbuf("hsc", [128, 1], F32)
        for t_, d_ in ((cw, d_convw), (cb, d_cb), (lg, d_lg), (lb, d_lb), (bgn, d_bg), (glub, d_glub)):
            P.dma(lambda e, t_=t_, d_=d_: e.dma_start(out=t_.t[:], in_=d_[l]), writes=[t_.b])
        P.dma(lambda e: e.dma_start(out=hsc.t[:], in_=d_hs), writes=[hsc.b])
        sqr = Sc.ring("sq", 3, [128, T], BF16)
        S1p = [Sc.psum(f"S1p{i}", [128, 512], F32) for i in range(2)]
        S2p = [Sc.psum(f"S2p{i}", [128, 512], F32) for i in range(2)]
        SSp = [Sc.psum(f"SSp{i}", [128, 512], F32) for i in range(2)]
        rstd_br = [Sc.sbuf(f"rstd_br{i}", [128, T], F32) for i in range(3)]
        def halo_v(e, slot):
            return GA[XA_NCH][0:128, slot * 32:(slot + 1) * 32]
        gab = DR.buf(f"GA{l}_{XA_NCH}")

        def conv_part():
          with P.scope() as S1:
            yconv = S1.sbuf("yconv", [128, 8, T], F32)
            vr = S1.ring("val", 3, [128, T + 32], BF16); gr_ = S1.ring("glu", 3, [128, T + 32], BF16)
            sgr = S1.ring("sig", 2, [128, T + 32], F32); ur = S1.ring("u", 3, [128, T + 32], BF16)
            ybr = S1.ring("yb", 2, [128, T], BF16)
            dgr = S1.ring("dg", 2, [128, 31, 128], BF16)
            cps = S1.ring("cps", 2, [128, 512], F32, psum=True)
            def prep(cc):
                val, glu, sig, u = vr.next(), gr_.next(), sgr.next(), ur.next()
                P.dma(lambda e: e.dma_start(out=val.t[:, 32:], in_=aT[cc * 128:(cc + 1) * 128, :]), reads=[aTb], writes=[val.b])
                P.dma(lambda e: e.dma_start(out=glu.t[:, 32:], in_=aT[1024 + cc * 128:1024 + (cc + 1) * 128, :]), reads=[aTb], writes=[glu.b])
                P.dma(lambda e: e.dma_start(out=val.t[:, 0:32], in_=halo_v(e, cc)), reads=[gab], writes=[val.b])
                P.dma(lambda e: e.dma_start(out=glu.t[:, 0:32], in_=halo_v(e, 8 + cc)), reads=[gab], writes=[glu.b])
                dg = dgr.next()
                for k in range(31):
                    if k % 2 == 0:
                        P.op("act", lambda e, k=k: e.activation(out=dg.t[:, k, :], in_=C["c_ident"].t[:], func=AF.Copy, scale=cw.t[:, cc, k:k + 1]),
                             reads=[C["c_ident"].b, cw.b], writes=[dg.b])
                    else:
                        P.op("dve", lambda e, k=k: e.tensor_scalar(out=dg.t[:, k, :], in0=C["c_ident"].t[:], scalar1=cw.t[:, cc, k:k + 1], scalar2=None, op0=ALU.mult),
                             reads=[C["c_ident"].b, cw.b], writes=[dg.b])
                P.op("act", lambda e: e.activation(out=sig.t[:], in_=glu.t[:], func=AF.Sigmoid), reads=[glu.b], writes=[sig.b])
                tt(P, "dve", u.t[:], val.t[:], sig.t[:], ALU.mult, [val.b, sig.b], [u.b])
                P.op("dve", lambda e: e.tensor_scalar(out=u.t[:, 0:32], in0=u.t[:, 0:32], scalar1=hsc.t[:], scalar2=None, op0=ALU.mult),
                     reads=[u.b, hsc.b], writes=[u.b])
                return dg, u

            def main(cc, dg, u):
                acc = yconv.t[:, cc, :]
                for tb in range(2):
                    ps = cps.next()
                    for k in range(31):
                        P.op("pe", mm(ps.t[:], dg.t[:, k, :], u.t[:, 2 + k + tb * 512:2 + k + tb * 512 + 512], k == 0, k == 30), reads=[dg.b, u.b], writes=[ps.b])
                    P.op("act", lambda e, ps=ps, tb=tb: e.activation(out=yconv.t[:, cc, tb * 512:(tb + 1) * 512], in_=ps.t[:], func=AF.Identity, bias=cb.t[:, cc:cc + 1]),
                         reads=[ps.b, cb.b], writes=[yconv.b])
                yb = ybr.next()
                P.op("act", lambda e: e.copy(out=yb.t[:], in_=acc), reads=[yconv.b], writes=[yb.b])
                for hb in range(2):
                    P.op("pe", mm(S1p[hb].t[:], C["c_ones_b"].t[:], yb.t[:, hb * 512:(hb + 1) * 512], cc == 0, cc == 7), reads=[yb.b, C["c_ones_b"].b], writes=[S1p[hb].b])
                sumsq_acc(P, C, S2p, yb.t[:], sqr, cc == 0, cc == 7, [yb.b])

            nxt_ = prep(0)
            for cc in range(8):
                cur_ = nxt_
                if cc + 1 < 8:
                    nxt_ = prep(cc + 1)
                main(cc, *cur_)
            mu = S1.sbuf("mu", [128, T], F32); rs = S1.sbuf("rs", [128, T], F32); m2 = S1.sbuf("m2", [128, T], F32)
            for hb in range(2):
                hs_ = slice(hb * 512, (hb + 1) * 512)
                P.op("act", lambda e, hb=hb, hs_=hs_: e.mul(out=mu.t[:, hs_], in_=S1p[hb].t[:], mul=1.0 / 1024), reads=[S1p[hb].b], writes=[mu.b])
                P.op("act", lambda e, hb=hb, hs_=hs_: e.mul(out=rs.t[:, hs_], in_=S2p[hb].t[:], mul=1.0 / 1024), reads=[S2p[hb].b], writes=[rs.b])
            tt(P, "dve", m2.t[:], mu.t[:], mu.t[:], ALU.mult, [mu.b], [m2.b])
            tt(P, "dve", rs.t[:], rs.t[:], m2.t[:], ALU.subtract, [rs.b, m2.b], [rs.b])
            P.op("act", lambda e: e.activation(out=rs.t[:], in_=rs.t[:], func=AF.Ln, bias=EPS), reads=[rs.b], writes=[rs.b])
            P.op("act", lambda e: e.activation(out=rs.t[:], in_=rs.t[:], func=AF.Exp, scale=-0.5), reads=[rs.b], writes=[rs.b])
            gtr = S1.ring("gt", 3, [128, T], BF16); gsr = S1.ring("gs", 2, [128, T], F32); zr = S1.ring("z", 2, [128, T], F32)
            for cc in range(8):
                acc = yconv.t[:, cc, :]
                z = zr.next(); gt = gtr.next(); gs = gsr.next()
                tt(P, "dve", z.t[:], acc, mu.t[:], ALU.subtract, [yconv.b, mu.b], [z.b])
                tt(P, "dve", z.t[:], z.t[:], rs.t[:], ALU.mult, [z.b, rs.b], [z.b])
                P.op("dve", lambda e, z=z, cc=cc: e.tensor_scalar(out=z.t[:], in0=z.t[:], scalar1=lg.t[:, cc:cc + 1], scalar2=lb.t[:, cc:cc + 1], op0=ALU.mult, op1=ALU.add),
                     reads=[z.b, lg.b, lb.b], writes=[z.b])
                P.op("act", lambda e, z=z: e.activation(out=z.t[:], in_=z.t[:], func=AF.Silu), reads=[z.b], writes=[z.b])
                P.dma(lambda e, gt=gt, cc=cc: e.dma_start(out=gt.t[:], in_=aT[2048 + cc * 128:2048 + (cc + 1) * 128, :]), reads=[aTb], writes=[gt.b])
                P.op("act", lambda e, gs=gs, gt=gt: e.activation(out=gs.t[:], in_=gt.t[:], func=AF.Silu), reads=[gt.b], writes=[gs.b])
                tt(P, "dve", AT.t[:, cc, :], z.t[:], gs.t[:], ALU.mult, [z.b, gs.b], [AT.b])
                sumsq_acc(P, C, SSp, AT.t[:, cc, :], sqr, cc == 0, cc == 7, [AT.b])
            row_rstd(P, SSp, 1024, rstd_br[0])
        conv_part()

        def attn_part():
          with P.scope() as S1:
            orr = S1.ring("o", 4, [128, T], BF16); gtr = S1.ring("gt", 4, [128, T], BF16); gsr = S1.ring("gs", 4, [128, T], F32)
            for hc in range(16):
                o = orr.next(); gt = gtr.next(); gs = gsr.next()
                r, h = hc // 8, hc % 8
                P.dma(lambda e, o=o, r=r, h=h: e.dma_start(out=o.t[:], in_=gsrc(e, GB, r, h * 128, 128)), reads=[DR.buf(f"GB{l}_{(h * 128) // 512}")], writes=[o.b])
                P.dma(lambda e, gt=gt, hc=hc: e.dma_start(out=gt.t[:], in_=bgT[hc * 128:(hc + 1) * 128, :]), reads=[bgb], writes=[gt.b])
                P.op("act", lambda e, gs=gs, gt=gt: e.activation(out=gs.t[:], in_=gt.t[:], func=AF.Silu), reads=[gt.b], writes=[gs.b])
                tt(P, "dve", AT.t[:, 8 + hc, :], o.t[:], gs.t[:], ALU.mult, [o.b, gs.b], [AT.b])
                sumsq_acc(P, C, SSp, AT.t[:, 8 + hc, :], sqr, hc == 0, hc == 15, [AT.b])
            row_rstd(P, SSp, 2048, rstd_br[1])
        attn_part()

        def ssm_part():
          with P.scope() as S1:
            zps = S1.ring("zps", 2, [128, 512], F32, psum=True)
            ygT = S1.sbuf("ygT", [128, 8, T], BF16)
            gw = S1.sbuf("gw", [128, 8, 1024], BF16)
            gwv = gluw[l].rearrange("(kc p) n -> p kc n", p=128)
            for j in range(0, 8, 4):
                P.dma(lambda e, j=j: e.dma_start(out=gw.t[:, j:j + 4, :], in_=gwv[:, j:j + 4, :]), writes=[gw.b], q="pool")
            for c in range(8):
                r, k4 = c // 4, c % 4
                P.dma(lambda e, c=c, r=r, k4=k4: e.dma_start(out=ygT.t[:, c, :], in_=gsrc(e, GB, r, 1024 + k4 * 128, 128)), reads=[DR.buf(f"GB{l}_2")], writes=[ygT.b])
            sgr = S1.ring("sg", 3, [128, T], F32); gtr = S1.ring("gt", 3, [128, T], BF16); gsr = S1.ring("gs", 3, [128, T], F32)
            for n in range(8):
                sg = sgr.next(); gt = gtr.next(); gs = gsr.next()
                for tb in range(2):
                    ps = zps.next()
                    for kc in range(8):
                        P.op("pe", mm(ps.t[:], gw.t[:, kc, n * 128:(n + 1) * 128], ygT.t[:, kc, tb * 512:(tb + 1) * 512], kc == 0, kc == 7), reads=[gw.b, ygT.b], writes=[ps.b])
                    P.op("act", lambda e, sg=sg, ps=ps, tb=tb, n=n: e.activation(out=sg.t[:, tb * 512:(tb + 1) * 512], in_=ps.t[:], func=AF.Sigmoid, bias=glub.t[:, n:n + 1]),
                         reads=[ps.b, glub.b], writes=[sg.b])
                P.dma(lambda e, gt=gt, n=n: e.dma_start(out=gt.t[:], in_=cgT[n * 128:(n + 1) * 128, :]), reads=[cgb], writes=[gt.b])
                P.op("act", lambda e, gs=gs, gt=gt: e.activation(out=gs.t[:], in_=gt.t[:], func=AF.Silu), reads=[gt.b], writes=[gs.b])
                tt(P, "dve", sg.t[:], sg.t[:], ygT.t[:, n, :], ALU.mult, [sg.b, ygT.b], [sg.b])
                tt(P, "dve", AT.t[:, 24 + n, :], sg.t[:], gs.t[:], ALU.mult, [sg.b, gs.b], [AT.b])
                sumsq_acc(P, C, SSp, AT.t[:, 24 + n, :], sqr, n == 0, n == 7, [AT.b])
            row_rstd(P, SSp, 1024, rstd_br[2])
        ssm_part()
        for c in range(32):
            br = 0 if c < 8 else (1 if c < 24 else 2)
            P.op("dve", lambda e, c=c, br=br: e.scalar_tensor_tensor(out=AT.t[:, c, :], in0=AT.t[:, c, :], scalar=bgn.t[:, c:c + 1], in1=rstd_br[br].t[:],
                                                                  op0=ALU.mult, op1=ALU.mult), reads=[AT.b, bgn.b, rstd_br[br].b], writes=[AT.b])

    def proj_to_r1(W2d):
        with P.scope() as Sc:
            wring = Sc.ring("w", 3, [128, 32, NB], BF16)
            psr = Sc.ring("ps", 6, [128, 512], F32, psum=True)
            stg = Sc.ring("stg", 4, [128, 512], F32)
            ev = [0]
            for nb, wt in w_stream(P, wring, W2d, 0, D):
                def sink(tt_, ps, nb=nb):
                    st = stg.next(); ev[0] += 1
                    evac(P, ev[0], st.t[:], ps.t[:], [ps.b], [st.b])
                    P.dma(lambda e: e.dma_start(out=r1[tt_ * 128:(tt_ + 1) * 128, nb:nb + NB], in_=st.t[:]), reads=[st.b], writes=[r1b])
                proj_tok(P, wt, AT, T, 32, psr, sink)

    def resid_norm(gname, xsrc, xsrcb, xdst, xdstb, to_AT_gname=None):
        with P.scope() as Sc:
            g_rep = grep_load(P, Sc, DR, gname, l)
            g2 = grep_load(P, Sc, DR, to_AT_gname, l) if to_AT_gname else None
            rr = Sc.ring("r", 2 if to_AT_gname else 3, [128, D], F32); xr = Sc.ring("x", 3, [128, D], F32)
            ssr = Sc.ring("ss", 12, [128, 1], F32)
            hr = Sc.ring("hb", 2, [128, D], BF16)
            if to_AT_gname:
                pst = Sc.ring("pst", 2, [128, 1024], BF16, psum=True)
            for tt_ in range(T // 128):
                r_t = rr.next(); x_t = xr.next(); ss = ssr.next(); rstd = ssr.next()
                P.dma(lambda e, r_t=r_t, tt_=tt_: e.dma_start(out=r_t.t[:], in_=r1[tt_ * 128:(tt_ + 1) * 128, :]), reads=[r1b], writes=[r_t.b])
                P.dma(lambda e, x_t=x_t, tt_=tt_: e.dma_start(out=x_t.t[:], in_=xsrc[tt_ * 128:(tt_ + 1) * 128, :]), reads=[xsrcb], writes=[x_t.b])
                junk = hr.next()
                P.op("act", lambda e, junk=junk, r_t=r_t, ss=ss: e.activation(out=junk.t[:], in_=r_t.t[:], func=AF.Square, accum_out=ss.t[:]), reads=[r_t.b], writes=[junk.b, ss.b])
                rms_rstd(P, ss, rstd, D)
                P.op("dve", lambda e, r_t=r_t, rstd=rstd: e.scalar_tensor_tensor(out=r_t.t[:], in0=r_t.t[:], scalar=rstd.t[:], in1=g_rep.t[:], op0=ALU.mult, op1=ALU.mult),
                     reads=[r_t.b, rstd.b, g_rep.b], writes=[r_t.b])
                tt(P, "dve", x_t.t[:], x_t.t[:], r_t.t[:], ALU.add, [x_t.b, r_t.b], [x_t.b])
                P.dma(lambda e, x_t=x_t, tt_=tt_: e.dma_start(out=xdst[tt_ * 128:(tt_ + 1) * 128, :], in_=x_t.t[:]), reads=[x_t.b], writes=[xdstb], q="pool")
                if to_AT_gname:
                    norm_tile_to_T(P, Sc, C, x_t, g2, AT, tt_, pst, hr, ssr, T)

    proj_to_r1(w_out[l])
    resid_norm("post_norm_g", xin, xinb, x1, x1b, to_AT_gname="xa_pre_g")

    with P.scope() as Sc:
        wring = Sc.ring("w", 3, [128, 32, NB], BF16)
        psr = Sc.ring("ps", 6, [128, 512], F32, psum=True)
        stg = Sc.ring("stg", 4, [128, 512], BF16)
        ev = [0]
        for nb, wt in w_stream(P, wring, wq[l], 0, D):
            def sink(j, tb, ps, nb=nb):
                st = stg.next(); ev[0] += 1
                evac(P, ev[0], st.t[:], ps.t[:], [ps.b], [st.b])
                row = nb + j * 128
                P.dma(lambda e: e.dma_start(out=qTd[row:row + 128, tb * 512:(tb + 1) * 512], in_=st.t[:]), reads=[st.b], writes=[qTb])
            proj_feat(P, wt, AT, T, 32, psr, sink)

    with P.scope() as Sc:
        qr = Sc.ring("q", 2, [128, 8, T], BF16); kr = Sc.ring("k", 2, [128, 8, 256], BF16); vr = Sc.ring("v", 2, [128, 2, 1024], BF16)
        er = Sc.ring("e", 8, [128, 512], BF16); rsr = Sc.ring("rs", 4, [128, 512], F32)
        sps = Sc.ring("sps", 4, [128, 512], F32, psum=True); sump = Sc.ring("sump", 2, [128, 512], F32, psum=True); ops_ = Sc.ring("ops", 2, [128, 512], F32, psum=True)
        for hd in range(4):
            q_t, k_t, v_t = qr.next(), kr.next(), vr.next()
            P.dma(lambda e, q_t=q_t, hd=hd: e.dma_start(out=q_t.t[:], in_=qTd[hd * 1024:(hd + 1) * 1024, :].rearrange("(dc p) t -> p dc t", p=128)), reads=[qTb], writes=[q_t.b])
            P.dma(lambda e, k_t=k_t, hd=hd: e.dma_start(out=k_t.t[:], in_=kTm[hd * 1024:(hd + 1) * 1024, :].rearrange("(dc p) m -> p dc m", p=128)), reads=[kTmb], writes=[k_t.b])
            P.dma(lambda e, v_t=v_t, hd=hd: e.dma_start(out=v_t.t[:], in_=vmg[(hd // 2) * 256:(hd // 2 + 1) * 256, (hd % 2) * 1024:(hd % 2 + 1) * 1024].rearrange("(mt p) d -> p mt d", p=128)),
                  reads=[vmgb], writes=[v_t.b])
            es = {}
            sms = {}
            rss = {}
            for tb in range(2):
                for mt in range(2):
                    sp_ = sps.next()
                    for dc in range(8):
                        P.op("pe", mm(sp_.t[:], k_t.t[:, dc, mt * 128:(mt + 1) * 128], q_t.t[:, dc, tb * 512:(tb + 1) * 512], dc == 0, dc == 7), reads=[k_t.b, q_t.b], writes=[sp_.b])
                    es[(tb, mt)] = (er.next(), sp_)
            for tb in range(2):
                for mt in range(2):
                    e_t, sp_ = es[(tb, mt)]
                    P.op("act", lambda e, e_t=e_t, sp_=sp_: e.activation(out=e_t.t[:], in_=sp_.t[:], func=AF.Exp, scale=XA_SCALE), reads=[sp_.b], writes=[e_t.b])
            for tb in range(2):
                sm = sump.next()
                for mt in range(2):
                    P.op("pe", mm(sm.t[:], C["c_ones_b"].t[:], es[(tb, mt)][0].t[:], mt == 0, mt == 1), reads=[es[(tb, mt)][0].b, C["c_ones_b"].b], writes=[sm.b])
                sms[tb] = sm
            for tb in range(2):
                rs = rsr.next()
                P.op("dve", lambda e, rs=rs, sm=sms[tb]: e.reciprocal(out=rs.t[:], in_=sm.t[:]), reads=[sms[tb].b], writes=[rs.b])
                rss[tb] = rs
            for tb in range(2):
                for dc in range(8):
                    op_ = ops_.next()
                    for mt in range(2):
                        P.op("pe", mm(op_.t[:], v_t.t[:, mt, dc * 128:(dc + 1) * 128], es[(tb, mt)][0].t[:], mt == 0, mt == 1), reads=[v_t.b, es[(tb, mt)][0].b], writes=[op_.b])
                    tt(P, "dve", AT.t[:, hd * 8 + dc, tb * 512:(tb + 1) * 512], op_.t[:], rss[tb].t[:], ALU.mult, [op_.b, rss[tb].b], [AT.b])

    proj_to_r1(wo[l])
    resid_norm("xa_post_g", x1, x1b, xo, xob, to_AT_gname=None)
    SA.__exit__(None, None, None)


def make_consts():
    bf = ml_dtypes.bfloat16
    i = np.arange(128)
    c = {}
    c["c_ident"] = np.eye(128, dtype=np.float32).astype(bf)
    c["c_ones_f"] = np.ones((128, 128), np.float32)
    c["c_ones_b"] = np.ones((128, 128), np.float32).astype(bf)
    c["c_triU"] = (i[:, None] > i[None, :]).astype(np.float32).astype(bf)
    c["c_triLE"] = (i[:, None] <= i[None, :]).astype(np.float32).astype(bf)
    c["c_triI"] = (i[:, None] <= i[None, :]).astype(np.float32).astype(bf)
    c["c_ntriI"] = (-(i[:, None] <= i[None, :]).astype(np.float32)).astype(bf)
    c["c_zero_b"] = np.zeros((128, 128), np.float32).astype(bf)
    m = np.zeros((128, 4, 512), np.float32)
    for blk in range(4):
        s_pos = blk * 128 + i
        m[:, blk, :] = (s_pos[:, None] < np.arange(512)[None, :])
    c["c_mask"] = m.astype(bf)
    c["c_iota"] = np.tile(np.arange(128, dtype=np.float32)[None, :], (128, 1))
    c["c_iotap"] = np.arange(128, dtype=np.float32)[:, None].copy()
    return c


def build_fused(nlayers=NL):
    _ME.clear()
    nc = bass.Bass("TRN2", target_bir_lowering=False)
    P = Prog(nc)
    io = {n: "in" for n in ALL_INPUT_NAMES}
    io["x_out"] = "out"
    DR = Dram(nc, io)
    DR.GA, DR.GB = {}, {}
    with P.scope() as Sg:
        C = load_consts(P, Sg, DR)
        xname = "x_in"
        for l in range(nlayers):
            XA = DR.get(f"XA{l}", [XA_ROWS, T], BF16)
            xa = Exchange(P, DR, XA, f"XA{l}", f"GA{l}", XA_NCH, small_rows=256)
            phase_A(P, DR, C, None, l, xname, xa)
            DR.GA[l] = xa.finish()
            XB = DR.get(f"XB{l}", [XB_ROWS, T], BF16)
            xb = Exchange(P, DR, XB, f"XB{l}", f"GB{l}", XB_NCH)
            phase_B(P, DR, C, l, xb)
            DR.GB[l] = xb.finish()
            xo = "x_out" if l == nlayers - 1 else f"x2_{l}"
            phase_C(P, DR, C, None, l, xname, xo)
            xname = xo
        P.finish()
    P.emit()
    return nc, P, DR


def ssm_host_layout(inp, hf):
    L = NL
    out = {}
    gl = np.arange(32)
    G = hf * 32 + gl
    lre = inp["ssm_lambda_re"][:, G, :]
    lim = inp["ssm_lambda_im"][:, G, :]
    ldt = np.repeat(inp["ssm_log_dt"][:, G][:, :, None], 64, axis=2)
    out["s_lre_row"] = np.ascontiguousarray(lre.reshape(L, 2048))
    out["s_lim_row"] = np.ascontiguousarray(lim.reshape(L, 2048))
    out["s_ldt_row"] = np.ascontiguousarray(ldt.reshape(L, 2048))
    for nm, a in (("s_lre_f", lre), ("s_lim_f", lim), ("s_ldt_f", ldt)):
        out[nm] = np.ascontiguousarray(a.reshape(L, 16, 128).transpose(0, 2, 1))
    Bre = np.zeros((L, 128, 4, 512), np.float32); Bim = np.zeros((L, 128, 4, 512), np.float32)
    Cre = np.zeros((L, 128, 16, 128), np.float32); Cim = np.zeros((L, 128, 16, 128), np.float32)
    for g in range(32):
        kc, g8 = g // 8, g % 8
        Bre[:, g8 * 16:(g8 + 1) * 16, kc, g8 * 64:(g8 + 1) * 64] = inp["ssm_b_re"][:, G[g]].transpose(0, 2, 1)
        Bim[:, g8 * 16:(g8 + 1) * 16, kc, g8 * 64:(g8 + 1) * 64] = inp["ssm_b_im"][:, G[g]].transpose(0, 2, 1)
        i, a = g // 2, g % 2
        Cre[:, a * 64:(a + 1) * 64, i, g8 * 16:(g8 + 1) * 16] = inp["ssm_c_re"][:, G[g]].transpose(0, 2, 1)
        Cim[:, a * 64:(a + 1) * 64, i, g8 * 16:(g8 + 1) * 16] = inp["ssm_c_im"][:, G[g]].transpose(0, 2, 1)
    out["s_Bre_blk"], out["s_Bim_blk"], out["s_Cre_blk"], out["s_Cim_blk"] = Bre, Bim, Cre, Cim
    d = inp["ssm_d"].reshape(L, 1024)[:, hf * 512:(hf + 1) * 512]
    out["s_d"] = np.ascontiguousarray(d.reshape(L, 4, 128).transpose(0, 2, 1))
    return out


CONST_NAMES = ["c_ident", "c_ones_f", "c_ones_b", "c_triU", "c_triLE", "c_triI", "c_ntriI", "c_zero_b", "c_mask", "c_iota", "c_iotap"]
ALL_INPUT_NAMES = CONST_NAMES + ["x_in", "w_in", "pre_norm_g",
                                 "s_lre_row", "s_lim_row", "s_ldt_row", "s_lre_f", "s_lim_f", "s_ldt_f", "s_Bre_blk", "s_Bim_blk", "s_Cre_blk", "s_Cim_blk", "s_d",
                                 "mem", "conv_wT", "conv_b_l", "conv_ln_g_l", "conv_ln_b_l", "branch_g_l", "glu_b_l", "halo_scale", "w_out", "xa_wq", "xa_wk_h", "xa_wv_h",
                                 "xa_wo", "ssm_glu_w", "xa_mem_g", "post_norm_g", "xa_pre_g", "xa_post_g"]


_WKV = {}


def host_params(inp, b, hf):
    p = {}
    for n in ("w_in", "pre_norm_g", "w_out", "xa_wq", "xa_wo", "ssm_glu_w", "xa_mem_g", "post_norm_g", "xa_pre_g", "xa_post_g"):
        p[n] = inp[n]
    if hf not in _WKV:
        _WKV[hf] = (np.ascontiguousarray(inp["xa_wk"][:, :, hf * 2048:(hf + 1) * 2048]),
                    np.ascontiguousarray(inp["xa_wv"][:, :, hf * 2048:(hf + 1) * 2048]))
    p["xa_wk_h"], p["xa_wv_h"] = _WKV[hf]
    p["mem"] = np.ascontiguousarray(inp["mem"][b])
    p["conv_wT"] = np.ascontiguousarray(inp["conv_w"].reshape(NL, 31, 8, 128).transpose(0, 3, 2, 1))
    for src, dst in (("conv_b", "conv_b_l"), ("conv_ln_g", "conv_ln_g_l"), ("conv_ln_b", "conv_ln_b_l"), ("ssm_glu_b", "glu_b_l")):
        p[dst] = np.ascontiguousarray(inp[src].reshape(NL, 8, 128).transpose(0, 2, 1))
    p["branch_g_l"] = np.ascontiguousarray(inp["branch_norm_g"].reshape(NL, 32, 128).transpose(0, 2, 1))
    p["halo_scale"] = np.full((128, 1), float(hf), np.float32)
    p.update(ssm_host_layout(inp, hf))
    return p


_prog = [None]


def run_fused(inp, trace=False):
    _WKV.clear()
    if _prog[0] is None:
        _prog[0] = build_fused()
    nc, P, DR = _prog[0]
    consts = make_consts()
    maps = []
    for c in range(8):
        b, hf = c // 2, c % 2
        d = dict(consts)
        d.update(host_params(inp, b, hf))
        d["x_in"] = np.ascontiguousarray(inp["x"][b, hf * T:(hf + 1) * T])
        maps.append({n: d[n] for n in ALL_INPUT_NAMES})
    res = run_bass_kernel_spmd(nc, maps, core_ids=list(range(8)), trace=trace)
    out = np.zeros((4, S, D), np.float32)
    for c in range(8):
        out[c // 2, (c % 2) * T:(c % 2 + 1) * T] = res.results[c]["x_out"]
    return out, res


def kernel(**inputs):
    inp = {k: np.asarray(v) for k, v in inputs.items()}
    out, _ = run_fused(inp)
    return out.astype(np.float32)
```
